# Optimizing a Trainium2 kernel written in Bass

```python
import math
import jax, jax.numpy as jnp
from jax import lax
import numpy as np

D_MODEL = 1024
BATCH = 8
SEQ = 2048
DEPTH = 4
DEC_BATCH = 8
DEC_SEQ = 64
PAST_LEN = 1024

CHUNK = 64
N_MIXERS = 2
N_A = (DEPTH + 1) // 2
N_B = DEPTH // 2
A_HEADS = 8
A_DK = D_MODEL // A_HEADS // 2
A_DV = 2 * A_DK
A_QK = A_HEADS * 2 * A_DK
A_V = A_HEADS * A_DV
B_Q_HEADS = 16
B_KV_HEADS = 2
B_GROUP = B_Q_HEADS // B_KV_HEADS
B_HD = 64
WINDOW = 128
W_CHUNKS = WINDOW // CHUNK
NUM_BUCKETS = 32
MAX_DIST = 128
N_MAPS = 16
D_FF = 4 * D_MODEL
QBLOCK = 128
EPS = 1e-6
NEG = -1e30

kernel_name = "hybrid_diffattn_swa_sink_stream_step"


def rms_norm(x, g):
    xf = x.astype(jnp.float32)
    y = xf * lax.rsqrt(jnp.mean(xf * xf, axis=-1, keepdims=True) + EPS)
    return (y * g.astype(jnp.float32)).astype(x.dtype)


def t5_bucket(rel):
    nb = NUM_BUCKETS // 2
    n = -rel
    ret = jnp.where(n < 0, nb, 0)
    n = jnp.abs(n)
    max_exact = nb // 2
    nf = jnp.maximum(n, 1).astype(jnp.float32)
    large = max_exact + (jnp.log(nf / max_exact) / math.log(MAX_DIST / max_exact)
                         * (nb - max_exact)).astype(jnp.int32)
    large = jnp.minimum(large, nb - 1)
    return ret + jnp.where(n < max_exact, n, large)


def rel_bias(q_pos, k_pos, table):
    return table[t5_bucket(k_pos[None, :] - q_pos[:, None])].astype(jnp.float32)


def lambda_init(layer):
    return 0.8 - 0.6 * math.exp(-0.3 * layer)


def diff_project(h, w_qkv):
    B, S = h.shape[:2]
    qkv = h @ w_qkv
    q = qkv[..., :A_QK].reshape(B, S, A_HEADS, 2, A_DK)
    k = qkv[..., A_QK:2 * A_QK].reshape(B, S, A_HEADS, 2, A_DK)
    v = qkv[..., 2 * A_QK:].reshape(B, S, A_HEADS, A_DV)
    return q, k, v


def diff_lambda(lam_p, lam_init):
    lp = lam_p.astype(jnp.float32)
    return jnp.exp(jnp.sum(lp[0] * lp[1])) - jnp.exp(jnp.sum(lp[2] * lp[3])) + lam_init


def diff_attn_core(q, k, v, bias, mask, lam):
    s = jnp.einsum('bqhmd,bkhmd->bhmqk', q, k, preferred_element_type=jnp.float32) * (A_DK ** -0.5)
    s = s + jnp.transpose(bias, (2, 3, 0, 1))[None]
    if mask is not None:
        s = jnp.where(mask, s, NEG)
    p = jax.nn.softmax(s, axis=-1)
    a = p[:, :, 0] - lam * p[:, :, 1]
    return jnp.einsum('bhqk,bkhe->bqhe', a.astype(v.dtype), v)


def diff_out(o, subln_g, w_o, lam_init):
    B, S = o.shape[:2]
    o = rms_norm(o, subln_g) * (1.0 - lam_init)
    return o.reshape(B, S, A_V) @ w_o


def diff_prompt(h, w_qkv, lam_p, subln_g, w_o, table, lam_init):
    B, S = h.shape[:2]
    q, k, v = diff_project(h, w_qkv)
    lam = diff_lambda(lam_p, lam_init)
    kpos = jnp.arange(S)
    kchunk = kpos // CHUNK
    nblk = S // QBLOCK
    qb = jnp.moveaxis(q.reshape(B, nblk, QBLOCK, A_HEADS, 2, A_DK), 1, 0)

    def one_block(args):
        qi, bi = args
        qpos = bi * QBLOCK + jnp.arange(QBLOCK)
        mask = kchunk[None, :] <= (qpos // CHUNK)[:, None]
        bias = rel_bias(qpos, kpos, table).reshape(QBLOCK, S, A_HEADS, 2)
        return diff_attn_core(qi, k, v, bias, mask, lam)

    o = lax.map(one_block, (qb, jnp.arange(nblk)))
    o = jnp.moveaxis(o, 0, 1).reshape(B, S, A_HEADS, A_DV)
    return diff_out(o, subln_g, w_o, lam_init), k, v


def diff_sample(h, cache_k, cache_v, w_qkv, lam_p, subln_g, w_o, table, lam_init):
    T = h.shape[1]
    P = cache_k.shape[1]
    q, k, v = diff_project(h, w_qkv)
    lam = diff_lambda(lam_p, lam_init)
    kk = jnp.concatenate([cache_k, k], axis=1)
    vv = jnp.concatenate([cache_v, v], axis=1)
    bias = rel_bias(P + jnp.arange(T), jnp.arange(P + T), table).reshape(T, P + T, A_HEADS, 2)
    o = diff_attn_core(q, kk, vv, bias, None, lam)
    return diff_out(o, subln_g, w_o, lam_init), k, v


def swa_project(h, w_qkv):
    B, S = h.shape[:2]
    qkv = h @ w_qkv
    nq, nk = B_Q_HEADS * B_HD, B_KV_HEADS * B_HD
    q = qkv[..., :nq].reshape(B, S, B_KV_HEADS, B_GROUP, B_HD)
    k = qkv[..., nq:nq + nk].reshape(B, S, B_KV_HEADS, B_HD)
    v = qkv[..., nq + nk:].reshape(B, S, B_KV_HEADS, B_HD)
    return q, k, v


def sink_attn_core(q, k, v, bias, mask, sink):
    s = jnp.einsum('bqngd,bknd->bngqk', q, k, preferred_element_type=jnp.float32) * (B_HD ** -0.5)
    s = s + jnp.transpose(bias, (2, 3, 0, 1))[None]
    if mask is not None:
        s = jnp.where(mask, s, NEG)
    sk = jnp.broadcast_to(sink.astype(jnp.float32)[None, :, :, None, None], s.shape[:-1] + (1,))
    p = jax.nn.softmax(jnp.concatenate([s, sk], axis=-1), axis=-1)[..., :-1]
    return jnp.einsum('bngqk,bknd->bqngd', p.astype(v.dtype), v)


def swa_prompt(h, w_qkv, sinks, w_o, table):
    B, S = h.shape[:2]
    q, k, v = swa_project(h, w_qkv)
    nc = S // CHUNK
    band_len = (W_CHUNKS + 1) * CHUNK

    def band(t):
        tc = t.reshape(B, nc, CHUNK, B_KV_HEADS, B_HD)
        tp = jnp.pad(tc, ((0, 0), (W_CHUNKS, 0), (0, 0), (0, 0), (0, 0)))
        return jnp.concatenate([tp[:, j:j + nc] for j in range(W_CHUNKS + 1)], axis=2)

    kb, vb = band(k), band(v)
    key_chunk = jnp.arange(nc)[:, None] - W_CHUNKS + jnp.arange(band_len)[None, :] // CHUNK
    mask = key_chunk >= 0
    bias = rel_bias(jnp.arange(CHUNK), jnp.arange(band_len) - W_CHUNKS * CHUNK, table)
    bias = bias.reshape(CHUNK, band_len, B_KV_HEADS, B_GROUP)
    qc = q.reshape(B, nc, CHUNK, B_KV_HEADS, B_GROUP, B_HD)
    sink = sinks.reshape(B_KV_HEADS, B_GROUP)
    o = jax.vmap(sink_attn_core, in_axes=(1, 1, 1, None, 0, None), out_axes=1)(qc, kb, vb, bias, mask, sink)
    o = o.reshape(B, S, B_Q_HEADS * B_HD)
    return o @ w_o, k[:, S - WINDOW:], v[:, S - WINDOW:]


def swa_sample(h, cache_k, cache_v, w_qkv, sinks, w_o, table):
    B, T = h.shape[:2]
    W = cache_k.shape[1]
    q, k, v = swa_project(h, w_qkv)
    kk = jnp.concatenate([cache_k, k], axis=1)
    vv = jnp.concatenate([cache_v, v], axis=1)
    bias = rel_bias(jnp.arange(T), jnp.arange(W + T) - W, table).reshape(T, W + T, B_KV_HEADS, B_GROUP)
    o = sink_attn_core(q, kk, vv, bias, None, sinks.reshape(B_KV_HEADS, B_GROUP))
    o = o.reshape(B, T, B_Q_HEADS * B_HD)
    return o @ w_o, kk[:, T:], vv[:, T:]


def sq_relu_mlp(h, w_up, w_down):
    return jnp.square(jax.nn.relu(h @ w_up)) @ w_down


def run_trunk(x, sample, cache_a_k, cache_a_v, cache_b_k, cache_b_v, rel_table,
              norm_mix_g, norm_mlp_g, final_norm_g, a_w_qkv, a_lambda, a_subln_g, a_w_o,
              b_w_qkv, b_sinks, b_w_o, mlp_w_up, mlp_w_down):
    ak, av, bk, bv = [], [], [], []
    for i in range(DEPTH):
        j = i // N_MIXERS
        h = rms_norm(x, norm_mix_g[i])
        if i % N_MIXERS == 0:
            li = lambda_init(i)
            if sample:
                o, k, v = diff_sample(h, cache_a_k[j], cache_a_v[j], a_w_qkv[j], a_lambda[j],
                                      a_subln_g[j], a_w_o[j], rel_table, li)
            else:
                o, k, v = diff_prompt(h, a_w_qkv[j], a_lambda[j], a_subln_g[j], a_w_o[j], rel_table, li)
            ak.append(k)
            av.append(v)
        else:
            if sample:
                o, k, v = swa_sample(h, cache_b_k[j], cache_b_v[j], b_w_qkv[j], b_sinks[j], b_w_o[j], rel_table)
            else:
                o, k, v = swa_prompt(h, b_w_qkv[j], b_sinks[j], b_w_o[j], rel_table)
            bk.append(k)
            bv.append(v)
        x = x + o
        x = x + sq_relu_mlp(rms_norm(x, norm_mlp_g[i]), mlp_w_up[i], mlp_w_down[i])
    return rms_norm(x, final_norm_g), jnp.stack(ak), jnp.stack(av), jnp.stack(bk), jnp.stack(bv)


def setup_inputs(seed: int = 0) -> dict:
    key = jax.random.key(seed)
    ks = jax.random.split(key, 20)
    f32 = jnp.float32
    nrm = lambda k, shape, s: (jax.random.normal(k, shape, f32) * s)
    b_cols = (B_Q_HEADS + 2 * B_KV_HEADS) * B_HD
    return {
        "x_prompt": nrm(ks[0], (BATCH, SEQ, D_MODEL), 1.0),
        "x_sample": nrm(ks[1], (DEC_BATCH, DEC_SEQ, D_MODEL), 1.0),
        "cache_a_k": nrm(ks[2], (N_A, DEC_BATCH, PAST_LEN, A_HEADS, 2, A_DK), 1.0),
        "cache_a_v": nrm(ks[3], (N_A, DEC_BATCH, PAST_LEN, A_HEADS, A_DV), 1.0),
        "cache_b_k": nrm(ks[4], (N_B, DEC_BATCH, WINDOW, B_KV_HEADS, B_HD), 1.0),
        "cache_b_v": nrm(ks[5], (N_B, DEC_BATCH, WINDOW, B_KV_HEADS, B_HD), 1.0),
        "rel_table": nrm(ks[6], (NUM_BUCKETS, N_MAPS), 0.5),
        "norm_mix_g": 1.0 + nrm(ks[7], (DEPTH, D_MODEL), 0.01),
        "norm_mlp_g": 1.0 + nrm(ks[8], (DEPTH, D_MODEL), 0.01),
        "final_norm_g": 1.0 + nrm(ks[9], (D_MODEL,), 0.01),
        "a_w_qkv": nrm(ks[10], (N_A, D_MODEL, 2 * A_QK + A_V), D_MODEL ** -0.5),
        "a_lambda": nrm(ks[11], (N_A, 4, A_DK), 0.1),
        "a_subln_g": 1.0 + nrm(ks[12], (N_A, A_DV), 0.01),
        "a_w_o": nrm(ks[13], (N_A, A_V, D_MODEL), A_V ** -0.5),
        "b_w_qkv": nrm(ks[14], (N_B, D_MODEL, b_cols), D_MODEL ** -0.5),
        "b_sinks": nrm(ks[15], (N_B, B_Q_HEADS), 0.5),
        "b_w_o": nrm(ks[16], (N_B, B_Q_HEADS * B_HD, D_MODEL), (B_Q_HEADS * B_HD) ** -0.5),
        "mlp_w_up": nrm(ks[17], (DEPTH, D_MODEL, D_FF), D_MODEL ** -0.5),
        "mlp_w_down": nrm(ks[18], (DEPTH, D_FF, D_MODEL), D_FF ** -0.5),
    }


def reference(x_prompt, x_sample, cache_a_k, cache_a_v, cache_b_k, cache_b_v, rel_table,
              norm_mix_g, norm_mlp_g, final_norm_g, a_w_qkv, a_lambda, a_subln_g, a_w_o,
              b_w_qkv, b_sinks, b_w_o, mlp_w_up, mlp_w_down):
    y_prompt, a_k_prompt, a_v_prompt, b_k_prompt, b_v_prompt = run_trunk(
        x_prompt, False, None, None, None, None, rel_table, norm_mix_g, norm_mlp_g, final_norm_g,
        a_w_qkv, a_lambda, a_subln_g, a_w_o, b_w_qkv, b_sinks, b_w_o, mlp_w_up, mlp_w_down)
    y_sample, a_k_sample, a_v_sample, b_k_sample, b_v_sample = run_trunk(
        x_sample, True, cache_a_k, cache_a_v, cache_b_k, cache_b_v, rel_table, norm_mix_g, norm_mlp_g,
        final_norm_g, a_w_qkv, a_lambda, a_subln_g, a_w_o, b_w_qkv, b_sinks, b_w_o, mlp_w_up, mlp_w_down)
    return (y_prompt, y_sample, a_k_prompt, a_v_prompt, b_k_prompt, b_v_prompt,
            a_k_sample, a_v_sample, b_k_sample, b_v_sample)
```

```python
import math
from contextlib import ExitStack
import numpy as np
import concourse.bass as bass
import concourse.mybir as mybir
from concourse.bass_utils import run_bass_kernel_spmd

F32 = mybir.dt.float32
BF16 = mybir.dt.bfloat16
AF = mybir.ActivationFunctionType
ALU = mybir.AluOpType

D = 1024
SEQ = 2048
TS = 64
NTOK = SEQ + TS
NT = 17
PAST = 1024
WIN = 128
DFF = 4096
EPS = 1e-6
NRING = 6
RING_EL = 2048


def tile_rows(t):
    return 128 if t < 16 else 64


class Tok:
    __slots__ = ("sem", "val", "eng")

    def __init__(self, sem, val, eng):
        self.sem, self.val, self.eng = sem, val, eng


class Buf:
    __slots__ = ("name", "w", "r", "excl")

    def __init__(self, name, excl=False):
        self.name, self.w, self.r, self.excl = name, None, {}, excl


class Eng:
    def __init__(self, name, eng, sem, is_pe=False):
        self.name, self.eng, self.sem, self.count = name, eng, sem, 0
        self.seen = {}
        self.is_pe = is_pe

    def wait(self, tok):
        key = id(tok.sem)
        if self.seen.get(key, 0) >= tok.val:
            return
        self.eng.wait_ge(tok.sem, tok.val)
        self.seen[key] = tok.val


class K:
    def __init__(self, nc, es):
        self.nc, self.es = nc, es
        self.pe = Eng("pe", nc.tensor, self.sem("s_pe"), True)
        self.act = Eng("act", nc.scalar, self.sem("s_act"))
        self.dve = Eng("dve", nc.vector, self.sem("s_dve"))
        self.pool = Eng("pool", nc.gpsimd, self.sem("s_pool"))
        self.sp = Eng("sp", nc.sync, self.sem("s_sp"))
        self.nsem = 5
        self.store_sems = []
        self.marks = []
        self.pe_n = 0

    def sem(self, name):
        return self.es.enter_context(self.nc.semaphore(name))

    def sb(self, name, shape, dt):
        return self.es.enter_context(self.nc.sbuf_tensor(name, shape, dt))

    def _deps(self, E, reads, writes, own_sem=None):
        toks = []
        for b in reads:
            if b.w is not None:
                t = b.w
                if not (t.eng is E and E.is_pe):
                    toks.append(t)
            if b.excl:
                for t in b.r.values():
                    if t.eng is not E:
                        toks.append(t)
        for b in writes:
            if b.w is not None and b.w.eng is not E and b.w.sem is not own_sem:
                toks.append(b.w)
            for t in b.r.values():
                if t.eng is not E:
                    toks.append(t)
        for t in toks:
            E.wait(t)

    def _commit(self, tok, reads, writes):
        for b in reads:
            key = id(tok.sem)
            b.r[key] = tok
        for b in writes:
            b.w = tok
            b.r = {}

    def mark(self, label):
        self.marks.append((label, self.pe_n))

    def op(self, E, fn, reads=(), writes=(), signal=True):
        self._deps(E, reads, writes)
        inst = fn()
        if E.is_pe:
            self.pe_n += 1
        if signal:
            E.count += 1
            inst.then_inc(E.sem, 1)
            tok = Tok(E.sem, E.count, E)
        else:
            tok = Tok(E.sem, E.count + 1, E)
        self._commit(tok, reads, writes)
        return tok

    def dma(self, Q, out, in_, reads, writes, semslot, n=1, **kw):
        self._deps(Q, reads, writes, own_sem=semslot[0])
        Q.eng.dma_start(out=out, in_=in_, **kw).then_inc(semslot[0], 16)
        semslot[1] += 16
        tok = Tok(semslot[0], semslot[1], None)
        self._commit(tok, reads, writes)
        return tok

    def semslot(self, name):
        return [self.sem(name), 0]


def build_program():
    nc = bass.Bass("TRN2", target_bir_lowering=False)
    es = ExitStack()
    with es:
        _build(nc, es)
    return nc


_MARKS = []


def _build(nc, es):
    k = K(nc, es)
    _MARKS.clear()
    k.marks = _MARKS
    PE, ACT, DVE, POOL, SP = k.pe, k.act, k.dve, k.pool, k.sp

    def din(name, shape, dt=F32):
        return nc.dram_tensor(name, list(shape), dt, kind="ExternalInput")

    def dout(name, shape, dt=F32):
        return nc.dram_tensor(name, list(shape), dt, kind="ExternalOutput")

    x_prompt = din("x_prompt", [SEQ, D]).ap()
    x_sample = din("x_sample", [TS, D]).ap()
    cache_a_k = din("cache_a_k", [2, PAST, 8, 128]).ap()
    cache_a_v = din("cache_a_v", [2, PAST, 8, 128]).ap()
    cache_b_k = din("cache_b_k", [2, WIN, 2, 64]).ap()
    cache_b_v = din("cache_b_v", [2, WIN, 2, 64]).ap()
    rel_table = din("rel_table", [32, 16])
    norm_mix_g = din("norm_mix_g", [4, D])
    norm_mlp_g = din("norm_mlp_g", [4, D])
    final_norm_g = din("final_norm_g", [D])
    a_w_qkv = din("a_w_qkv", [2, D, 3072]).ap()
    a_lambda = din("a_lambda", [2, 256])
    a_subln_g = din("a_subln_g", [2, 128])
    a_w_o = din("a_w_o", [2, D, D]).ap()
    b_w_qkv = din("b_w_qkv", [2, D, 1280]).ap()
    b_sinks = din("b_sinks", [2, 16])
    b_w_o = din("b_w_o", [2, D, D]).ap()
    mlp_w_up = din("mlp_w_up", [4, D, DFF]).ap()
    mlp_w_down = din("mlp_w_down", [4, DFF, D]).ap()
    ohr_d = din("c_ohr", [32, 383]).ap()
    ident_d = din("c_ident", [128, 128]).ap()

    y_prompt = dout("y_prompt", [SEQ, D]).ap()
    y_sample = dout("y_sample", [TS, D]).ap()
    a_k_prompt = dout("a_k_prompt", [2, SEQ, 8, 128]).ap()
    a_v_prompt = dout("a_v_prompt", [2, SEQ, 8, 128]).ap()
    b_k_prompt = dout("b_k_prompt", [2, WIN, 2, 64]).ap()
    b_v_prompt = dout("b_v_prompt", [2, WIN, 2, 64]).ap()
    a_k_sample = dout("a_k_sample", [2, TS, 8, 128]).ap()
    a_v_sample = dout("a_v_sample", [2, TS, 8, 128]).ap()
    b_k_sample = dout("b_k_sample", [2, WIN, 2, 64]).ap()
    b_v_sample = dout("b_v_sample", [2, WIN, 2, 64]).ap()
    efd = nc.dram_tensor("efd_scratch", [16, 384], F32, kind="Internal")

    X = k.sb("X", [128, NT, D], F32)
    XB = [Buf(f"X{t}") for t in range(NT)]
    E = k.sb("E", [128, 16, 256], BF16)
    EB = Buf("E")
    QS = k.sb("QS", [64, 16, 64], BF16)
    RING = k.sb("RING", [128, NRING, RING_EL], BF16)
    ARENA = k.sb("ARENA", [128, 41856], BF16)
    FA = k.sb("FA", [128, 4224], F32)
    ident = k.sb("ident", [128, 128], BF16)
    gT = k.sb("gT", [128, 9, 8], F32)
    cfar = k.sb("cfar", [128, 16], F32)
    esink = k.sb("esink", [128, 2, 16], F32)
    lamb = k.sb("lamb", [128, 2, 260], F32)
    gsub = k.sb("gsub", [128, 2, 128], F32)
    stats = k.sb("stats", [128, 96], F32)
    otok = k.sb("otok", [128, 4, 128], BF16)
    junk2 = k.sb("junk2", [128, 128], BF16)
    junk1 = k.sb("junk1", [128, D], BF16)
    tbs = FA[0:32, 800:816]
    ohs = FA[0:32, 0:384]
    efs = FA[0:16, 384:768]

    PS = [es.enter_context(nc.psum_tensor(f"ps{i}", [128, 512], F32)) for i in range(8)]
    PB = [Buf(f"ps{i}", excl=True) for i in range(8)]

    A_hT = 0
    A_OT = A_hT + 8 * NTOK
    A_U = A_OT + 2 * NTOK
    KTW = NTOK + PAST
    VW = 132
    NVT = 25
    U_QT = 0
    U_KT = NTOK
    U_V = U_KT + KTW
    USZ = U_V + NVT * VW
    A_PT = A_U + 2 * USZ
    A_KTOK = A_PT + 4 * 512
    A_CK = A_KTOK + 512
    A_END = A_CK + 1024
    assert A_END <= 41856
    A_HN = A_U
    A_JUNK = A_HN + 4096
    M_hT = 0
    M_AT = 8 * 1088
    M_END = M_AT + 8 * 1088
    assert M_END <= A_U

    def hT_ap(c, c0, c1):
        return ARENA[:, A_hT + c * NTOK + c0:A_hT + c * NTOK + c1]

    def OT_ap(kk, c0, c1):
        return ARENA[:, A_OT + kk * NTOK + c0:A_OT + kk * NTOK + c1]

    def hn_ap(i, R):
        return ARENA[0:R, A_HN + i * 1024:A_HN + (i + 1) * 1024]

    def junk_ap(R, w=1024):
        return ARENA[0:R, A_JUNK:A_JUNK + w]

    F_TMP = 0
    F_OA = 1024
    F_STG = 2560
    F_O = 3584
    F_T1 = 4096
    F_END = 4224

    hTB = [Buf(f"hT{g}") for g in range(5)]
    OTB = [Buf(f"OT{g}") for g in range(5)]
    ringB = [Buf(f"ring{i}") for i in range(NRING)]
    ring_sem = [k.semslot(f"ringsem{i}") for i in range(NRING)]

    def _maxtok(a, b):
        if a is None:
            return b
        return a if a.val >= b.val else b

    def arena_barrier(bufs_old, bufs_new):
        for nb in bufs_new:
            for ob in bufs_old:
                if ob.w is not None:
                    nb.r[("w", id(ob.w.sem))] = _maxtok(nb.r.get(("w", id(ob.w.sem))), ob.w)
                for kk, t in ob.r.items():
                    nb.r[("r", id(t.sem))] = _maxtok(nb.r.get(("r", id(t.sem))), t)

    reqs = []
    ring_state = dict(issued=0, consumed=0, released=set())

    def gen_requests():
        def attn_layer(kind, j, w_qkv, w_o):
            def unit_reqs(u):
                if kind == "A":
                    reqs.append([(lambda ra: ra[:, 0:1024].rearrange("p (c f) -> p c f", c=8),
                                  w_qkv[j, :, u * 128:(u + 1) * 128].rearrange("(c p) f -> p c f", p=128))])
                    reqs.append([(lambda ra: ra[:, 0:2048].rearrange("p (c f) -> p c f", c=8)[:, :, 0:128],
                                  w_qkv[j, :, 1024 + u * 128:1024 + (u + 1) * 128].rearrange("(c p) f -> p c f", p=128)),
                                 (lambda ra: ra[:, 0:2048].rearrange("p (c f) -> p c f", c=8)[:, :, 128:256],
                                  w_qkv[j, :, 2048 + u * 128:2048 + (u + 1) * 128].rearrange("(c p) f -> p c f", p=128))])
                else:
                    n = u // 4
                    reqs.append([(lambda ra: ra[:, 0:1024].rearrange("p (c f) -> p c f", c=8),
                                  w_qkv[j, :, u * 128:(u + 1) * 128].rearrange("(c p) f -> p c f", p=128))])
                    if u % 4 == 0:
                        reqs.append([(lambda ra: ra[:, 0:1024].rearrange("p (c f) -> p c f", c=8)[:, :, 0:64],
                                      w_qkv[j, :, 1024 + n * 64:1024 + (n + 1) * 64].rearrange("(c p) f -> p c f", p=128)),
                                     (lambda ra: ra[:, 0:1024].rearrange("p (c f) -> p c f", c=8)[:, :, 64:128],
                                      w_qkv[j, :, 1152 + n * 64:1152 + (n + 1) * 64].rearrange("(c p) f -> p c f", p=128))])
            unit_reqs(0)
            for u in range(8):
                if u + 1 < 8:
                    unit_reqs(u + 1)
                if u % 2 == 1:
                    r0 = (u - 1) * 128
                    reqs.append([(lambda ra: ra[:, 0:2048].rearrange("p (k d) -> p k d", k=2),
                                  w_o[j, r0:r0 + 256, :].rearrange("(k p) d -> p k d", p=128))])

        def mlp_layer(li):
            for tg in range(2):
                for fq in range(4):
                    for cc in range(4):
                        f0 = fq * 1024 + cc * 256
                        reqs.append([(lambda ra: ra[:, 0:2048].rearrange("p (c f) -> p c f", c=8),
                                      mlp_w_up[li, :, f0:f0 + 256].rearrange("(c p) f -> p c f", p=128))])
                    for cc in range(4):
                        r0 = fq * 1024 + cc * 256
                        reqs.append([(lambda ra: ra[:, 0:2048].rearrange("p (k d) -> p k d", k=2),
                                      mlp_w_down[li, r0:r0 + 256, :].rearrange("(k p) d -> p k d", p=128))])
        attn_layer("A", 0, a_w_qkv, a_w_o)
        mlp_layer(0)
        attn_layer("B", 0, b_w_qkv, b_w_o)
        mlp_layer(1)
        attn_layer("A", 1, a_w_qkv, a_w_o)
        mlp_layer(2)
        attn_layer("B", 1, b_w_qkv, b_w_o)
        mlp_layer(3)

    def ring_try_issue():
        while ring_state["issued"] < len(reqs):
            n = ring_state["issued"]
            if n >= NRING and (n - NRING) not in ring_state["released"]:
                break
            s = n % NRING
            for (dfn, src) in reqs[n]:
                k.dma(POOL, dfn(RING[:, s, :]), src, [], [ringB[s]], ring_sem[s])
            ring_state["issued"] += 1

    def ring_next():
        n = ring_state["consumed"]
        ring_state["consumed"] += 1
        ring_try_issue()
        assert n < ring_state["issued"], f"ring chunk {n} not issued (deadlock in request order)"
        return n, n % NRING

    def ring_done(n):
        ring_state["released"].add(n)
        ring_try_issue()

    ld = k.semslot("ld")
    tabB = Buf("tab")
    misc = Buf("misc")
    ssB = Buf("ss")
    tab_sem = k.semslot("tabld")
    k.dma(SP, tbs, rel_table.ap(), [], [tabB], tab_sem)
    k.dma(SP, ohs[:, 0:383], ohr_d, [], [tabB], tab_sem)
    for g4 in range(4):
        k.dma(SP, X[:, 4 * g4:4 * g4 + 4, :],
              x_prompt[512 * g4:512 * g4 + 512, :].rearrange("(t p) d -> p t d", p=128),
              [], [XB[4 * g4 + i] for i in range(4)], k.semslot(f"xl{g4}"))
    k.op(DVE, lambda: nc.vector.memset(X[:, 16, :], 0.0), [], [XB[16]])
    k.dma(SP, X[0:64, 16, :], x_sample, [], [XB[16]], k.semslot("xls"))
    identB = Buf("ident")
    k.dma(POOL, ident[:], ident_d, [], [identB], k.semslot("identld"))
    gen_requests()
    ring_try_issue()
    with nc.allow_non_contiguous_dma(reason="small strided parameter loads"):
        k.dma(SP, gT[:, 0:4, :], norm_mix_g.ap().rearrange("l (c p) -> p l c", p=128), [], [misc], ld)
        k.dma(SP, cfar[:], bass.AP(rel_table, 15 * 16, [[0, 128], [1, 16]]), [], [misc], ld)
        k.dma(SP, esink[:].rearrange("p l h -> p (l h)"), bass.AP(b_sinks, 0, [[0, 128], [1, 32]]), [], [misc], ld)
        k.dma(SP, lamb[:, :, 0:256], bass.AP(a_lambda, 0, [[0, 128], [256, 2], [1, 256]]), [], [misc], ld)
        k.dma(SP, gsub[:], bass.AP(a_subln_g, 0, [[0, 128], [128, 2], [1, 128]]), [], [misc], ld)
        k.dma(SP, gT[:, 4:8, :], norm_mlp_g.ap().rearrange("l (c p) -> p l c", p=128), [], [misc], ld)
        k.dma(SP, gT[:, 8, :], final_norm_g.ap().rearrange("(c p) -> p c", p=128), [], [misc], ld)
    st_misc = k.semslot("st_misc")
    k.store_sems.append(st_misc)
    for j in range(2):
        k.dma(SP, b_k_sample[j, 0:64], cache_b_k[j, 64:128], [], [], st_misc)
        k.dma(SP, b_v_sample[j, 0:64], cache_b_v[j, 64:128], [], [], st_misc)

    k.op(PE, lambda: nc.tensor.matmul(PS[0][0:16, 0:383], tbs, ohs[:, 0:383], start=True, stop=True),
         [tabB], [PB[0]])
    ncf = stats[0:16, 90:91]
    stB = Buf("stB")
    k.op(DVE, lambda: nc.vector.tensor_scalar(ncf, PS[0][0:16, 382:383], -1.0, None, ALU.mult), [PB[0]], [stB])
    efB = Buf("efs")
    k.op(ACT, lambda: nc.scalar.activation(out=efs[:, 0:383], in_=PS[0][0:16, 0:383], func=AF.Exp, bias=ncf, scale=1.0),
         [PB[0], stB], [efB])
    efdB = Buf("efd")
    k.dma(SP, efd.ap()[:, 0:383], efs[:, 0:383], [efB], [efdB], k.semslot("efd_st"))
    esem = k.semslot("eld")
    e32B = Buf("e32")
    E32 = ARENA[:, A_U + USZ:A_U + USZ + 8192].bitcast(F32).rearrange("p (m c) -> p m c", m=16)
    for p in range(128):
        src = bass.AP(efd, 127 - p, [[384, 16], [1, 256]])
        k.dma(SP, E32[p:p + 1, :, :], src, [efdB], [e32B], esem)
    QSB = Buf("QS")
    e_done = [False]

    def finish_E():
        if e_done[0]:
            return
        e_done[0] = True
        k.op(DVE, lambda: nc.vector.tensor_copy(E[:, 0:8, :], E32[:, 0:8, :]), [e32B], [EB])
        k.op(DVE, lambda: nc.vector.tensor_copy(E[:, 8:16, :], E32[:, 8:16, :]), [e32B], [EB])
        arena_barrier([e32B], unit_bufs)
        k.op(DVE, lambda: nc.vector.memset(E[64:128, :, 0:64], 0.0), [EB], [EB])
        k.op(DVE, lambda: nc.vector.tensor_copy(QS[:], E[0:64, :, 192:256]), [EB], [QSB])

    k.op(DVE, lambda: nc.vector.memset(stats[:, 0:34], 0.0), [], [ssB])
    k.op(ACT, lambda: nc.scalar.activation(out=esink[:].rearrange("p l h -> p (l h)"),
                                           in_=esink[:].rearrange("p l h -> p (l h)"), func=AF.Exp), [misc], [misc])
    fa_users = []

    ps_rr = [0]

    def next_bank(cands):
        b = cands[ps_rr[0] % len(cands)]
        ps_rr[0] += 1
        return b

    evac_rr = [0]
    evac_mode = ["alt"]

    def evac_engine():
        if evac_mode[0] == "dve":
            return DVE
        evac_rr[0] += 1
        return ACT if evac_rr[0] % 8 in (0, 2, 3, 5, 6) else DVE

    def rstd_from_ss(ss_ap, out_ap, n, bufs):
        k.op(ACT, lambda: nc.scalar.activation(out=out_ap, in_=ss_ap, func=AF.Ln, scale=1.0 / n, bias=EPS), bufs, bufs)
        k.op(ACT, lambda: nc.scalar.activation(out=out_ap, in_=out_ap, func=AF.Exp, scale=-0.5), bufs, bufs)

    hnB = [Buf(f"hn{i}") for i in range(4)]
    junkB = Buf("junk")
    junk2B = Buf("junk2")

    pre_sq = set()
    junk1B = Buf("junk1")

    def emit_sq(t):
        R = tile_rows(t)
        k.op(ACT, lambda: nc.scalar.activation(out=junk1[0:R, :], in_=X[0:R, t, :], func=AF.Square, accum_out=stats[0:R, t:t + 1]),
             [XB[t], ssB], [junk1B, ssB])
        pre_sq.add(t)

    def norm_to_hT(gl, tiles, dst_fn, dstB_fn, banks):
        nt = len(tiles)
        t_0 = tiles[0]
        for t in tiles:
            if t not in pre_sq:
                emit_sq(t)
        rstd_from_ss(stats[:, t_0:t_0 + nt], stats[:, 17 + t_0:17 + t_0 + nt], D, [ssB])
        for g0 in range(0, nt, 4):
            grp = tiles[g0:g0 + 4]
            col = 0
            cols = []
            for i, t in enumerate(grp):
                R = tile_rows(t)
                k.op(DVE, lambda t=t, R=R, i=i: nc.vector.tensor_scalar(hn_ap(i, R), X[0:R, t, :], stats[0:R, 17 + t:18 + t],
                                                                        None, ALU.mult),
                     [XB[t], ssB], [hnB[i]])
                cols.append((t, R, i, col))
                col += R
            W = col
            for c in range(8):
                bank = next_bank(banks)
                pbf = PS[bank][:].bitcast(BF16)
                for jj, (t, R, hb, cc) in enumerate(cols):
                    k.op(PE, lambda: nc.tensor.transpose(pbf[:, cc:cc + R], hn_ap(hb, R)[:, c * 128:(c + 1) * 128], ident[0:R, 0:R]),
                         [hnB[hb], identB], [PB[bank]], signal=(jj == len(cols) - 1))
                t0 = grp[0]
                c0 = t0 * 128
                Eg = evac_engine()
                if Eg is ACT:
                    k.op(ACT, lambda: nc.scalar.activation(out=dst_fn(c, c0, c0 + W), in_=pbf[:, 0:W], func=AF.Copy,
                                                           scale=gT[:, gl, c:c + 1]), [PB[bank], misc], [dstB_fn(t0)])
                else:
                    k.op(DVE, lambda: nc.vector.tensor_scalar(dst_fn(c, c0, c0 + W), pbf[:, 0:W], gT[:, gl, c:c + 1], None, ALU.mult),
                         [PB[bank], misc], [dstB_fn(t0)])
        k.op(DVE, lambda: nc.vector.memset(stats[:, t_0:t_0 + nt], 0.0), [], [ssB])
        for t in tiles:
            pre_sq.discard(t)

    QTB = [Buf("QT0"), Buf("QT1")]
    KTB = [[Buf(f"KT{p}_{g}") for g in range(6)] for p in range(2)]
    VB = [[Buf(f"V{p}_{t}") for t in range(NVT)] for p in range(2)]
    PTB = [[Buf(f"PT{s}{i}") for i in range(2)] for s in range(2)]
    TMPB = [[Buf(f"TMP{s}{i}") for i in range(2)] for s in range(2)]
    OAB = Buf("OA")
    KTOKB = Buf("KTOK")
    CKB = Buf("CK")
    NSTG = 4
    STGB = [Buf(f"stg{i}") for i in range(NSTG)]
    stg_sem = [k.semslot(f"stg{i}") for i in range(NSTG)]
    k.store_sems += stg_sem
    stg_n = [0]
    OB = [Buf(f"O{i}") for i in range(4)]
    T1B = Buf("T1")
    otokB = [Buf(f"otok{i}") for i in range(4)]
    finB = Buf("fin")
    OPTB = [Buf("opt0"), Buf("opt1")]
    arena_barrier(fa_users, [OAB, T1B] + STGB + OB + OPTB)
    ck_sem = k.semslot("ck")
    cv_sem = [k.semslot("cv0"), k.semslot("cv1")]
    unit_bufs = QTB + KTB[0] + KTB[1] + VB[0] + VB[1]

    def QT_ap(p, r0, r1, c0, c1):
        o = A_U + p * USZ + U_QT
        return ARENA[r0:r1, o + c0:o + c1]

    def KT_ap(p, r0, r1, c0, c1):
        o = A_U + p * USZ + U_KT
        return ARENA[r0:r1, o + c0:o + c1]

    def V_ap(p, r0, r1, t, c0, c1):
        o = A_U + p * USZ + U_V + t * VW
        return ARENA[r0:r1, o + c0:o + c1]

    def Vall_ap(p, t0, t1):
        o = A_U + p * USZ + U_V
        return ARENA[:, o + t0 * VW:o + t1 * VW].rearrange("p (t w) -> p t w", w=VW)

    def PT_ap(s, i, r0, r1, c0, c1):
        o = A_PT + (s * 2 + i) * 512
        return ARENA[r0:r1, o + c0:o + c1]

    def TMP_ap(s, i, r0, r1, c0, c1):
        o = F_TMP + (s * 2 + i) * 256
        return FA[r0:r1, o + c0:o + c1]

    def ktb(p, col):
        return KTB[p][min(col // 512, 4)] if col < NTOK else KTB[p][5]

    def advance(bg):
        if bg is None:
            return False
        try:
            next(bg)
            return True
        except StopIteration:
            return False

    def attention(u, qp, kp, dv, groups, finalize, bg):
        dvp = dv + 1
        step_list = []
        for gi, g in enumerate(groups):
            q = g["q"]
            merged, cur, w = [], [], 0
            for st in g["steps"]:
                (kcol, nk, vt, lo, hi, classes) = st
                wd = q[hi][0] + q[hi][1] - q[lo][0]
                if cur and w + wd > 512:
                    merged.append(cur)
                    cur, w = [], 0
                cur.append(st + (w,))
                w += wd
            if cur:
                merged.append(cur)
            for si, subs in enumerate(merged):
                step_list.append((gi, si == len(merged) - 1, subs))
        last_for = {}
        for n, (gi, lastg, subs) in enumerate(step_list):
            for sub in subs:
                for qi in range(sub[3], sub[4] + 1):
                    last_for[(gi, qi)] = (n, sub[0])
        bank_started = {}
        pending = []
        cur_n = [0]

        def acc_ap(gi, s, qi, R):
            G = len(groups[gi]["q"])
            a = s * G + qi
            bank = 4 + a // 3
            off = (a % 3) * dvp
            return bank, PS[bank][0:R, off:off + dvp], a

        def emit_qk(n):
            gi, lastg, subs = step_list[n]
            q = groups[gi]["q"]
            for s in range(2):
                bank = 2 * (n % 2) + s
                for (kcol, nk, vt, lo, hi, classes, coff) in subs:
                    c0 = q[lo][0]
                    c1 = q[hi][0] + q[hi][1]
                    k.op(PE, lambda: nc.tensor.matmul(PS[bank][0:nk, coff:coff + c1 - c0], KT_ap(kp, 64 * s, 64 * s + 64, kcol, kcol + nk),
                                                      QT_ap(qp, 64 * s, 64 * s + 64, c0, c1), start=True, stop=True, skip_group_check=True),
                         [ktb(kp, kcol), QTB[qp]], [PB[bank]])

        def emit_exp(n):
            gi, lastg, subs = step_list[n]
            q = groups[gi]["q"]
            segs = []
            for (kcol, nk, vt, lo, hi, classes, coff) in subs:
                c0 = q[lo][0]
                nsp = 0
                while nsp < len(classes) and classes[nsp] != "F":
                    nsp += 1
                wsp = sum(q[lo + i][1] for i in range(nsp))
                wtot = q[hi][0] + q[hi][1] - c0
                if nsp > 0:
                    segs.append(["S", nk, coff, wsp, 0 if classes[0] == "D" else 128])
                if wtot > wsp:
                    if segs and segs[-1][0] == "F" and segs[-1][1] == nk and segs[-1][2] + segs[-1][3] == coff + wsp:
                        segs[-1][3] += wtot - wsp
                    else:
                        segs.append(["F", nk, coff + wsp, wtot - wsp])
            i2 = n % 2
            for s in range(2):
                mp = 2 * u + s
                bank = 2 * (n % 2) + s
                for sg in segs:
                    nk, c, w = sg[1], sg[2], sg[3]
                    k.op(ACT, lambda: nc.scalar.activation(out=PT_ap(s, i2, 0, nk, c, c + w), in_=PS[bank][0:nk, c:c + w], func=AF.Exp,
                                                           bias=cfar[0:nk, mp:mp + 1], scale=1.0), [PB[bank], misc], [PTB[s][i2]])
                for sg in segs:
                    if sg[0] == "S":
                        nk, c, w, ecol = sg[1], sg[2], sg[3], sg[4]
                        k.op(DVE, lambda: nc.vector.tensor_tensor(PT_ap(s, i2, 0, nk, c, c + w), PT_ap(s, i2, 0, nk, c, c + w),
                                                                  E[0:nk, mp, ecol:ecol + w], ALU.mult),
                             [PTB[s][i2], EB], [PTB[s][i2]])

        def emit_pv(n):
            gi, lastg, subs = step_list[n]
            q = groups[gi]["q"]
            i2 = n % 2
            items = [(s, sub, qi) for s in range(2) for sub in subs for qi in range(sub[3], sub[4] + 1)]
            for idx, (s, sub, qi) in enumerate(items):
                (kcol, nk, vt, lo, hi, classes, coff) = sub
                c0 = q[lo][0]
                qc, nq = q[qi]
                bank, oap, a = acc_ap(gi, s, qi, nq)
                first = not bank_started.get((gi, bank), False)
                bank_started[(gi, bank)] = True
                k.op(PE, lambda: nc.tensor.matmul(oap, PT_ap(s, i2, 0, nk, coff + qc - c0, coff + qc - c0 + nq), V_ap(kp, 0, nk, vt, 0, dvp),
                                                  start=first, stop=(last_for[(gi, qi)] == (n, kcol)), skip_group_check=True),
                     [PTB[s][i2], VB[kp][vt]], [PB[bank]], signal=(idx == len(items) - 1))
            if lastg:
                for x in sorted(pending, key=lambda x: x[0]):
                    x[1]()
                pending.clear()
                for (dl, fn) in finalize(gi, groups[gi], acc_ap):
                    pending.append((cur_n[0] + dl, fn))

        N = len(step_list)
        emit_qk(0)
        for n in range(N + 1):
            cur_n[0] = n
            due = [x for x in pending if x[0] <= n]
            for x in due:
                pending.remove(x)
                x[1]()
            if n < N:
                emit_exp(n)
            if n >= 1:
                emit_pv(n - 1)
            if n + 1 < N:
                emit_qk(n + 1)
            advance(bg)
        for x in sorted(pending, key=lambda x: x[0]):
            x[1]()
        pending.clear()
        while advance(bg):
            pass

    oa_stride = [129]

    def evac_acc(G, R, dvp):
        nacc = 2 * G
        nb = (nacc + 2) // 3
        oa_stride[0] = dvp
        for b in range(nb):
            na = min(3, nacc - 3 * b)
            w = na * dvp
            k.op(DVE, lambda: nc.vector.tensor_copy(FA[0:R, F_OA + 3 * b * dvp:F_OA + 3 * b * dvp + w], PS[4 + b][0:R, 0:w]),
                 [PB[4 + b]], [OAB])

    def oa_ap(a, R, c0, c1):
        o = F_OA + a * oa_stride[0]
        return FA[0:R, o + c0:o + c1]

    def oa_sums(R, a0, na):
        dvp = oa_stride[0]
        return FA[0:R, F_OA + a0 * dvp:F_OA + (a0 + na) * dvp].rearrange("p (a w) -> p a w", w=dvp)[:, :, dvp - 1:dvp]

    def transposes_to_OT(kk, q, R):
        bank = 7
        pbf = PS[bank][:].bitcast(BF16)
        col = 0
        for qi, (qc, nq) in enumerate(q):
            k.op(PE, lambda: nc.tensor.transpose(pbf[:, col:col + nq], otok[0:nq, qi, :], ident[0:nq, 0:nq]),
                 [otokB[qi], identB], [PB[bank]], signal=(qi == len(q) - 1))
            col += nq
        c0 = q[0][0]
        gq = min(c0 // 512, 4)
        k.op(DVE, lambda: nc.vector.tensor_copy(OT_ap(kk, c0, c0 + col), pbf[:, 0:col]), [PB[bank]], [OTB[gq]])

    opt_n = [0]

    def out_proj_pair(j, after_tile=None):
        rid, s = ring_next()
        for t in range(NT):
            R = tile_rows(t)
            g = min(t // 4, 4)
            for dh in range(2):
                bank = next_bank([0, 1, 2, 3])
                for kk in range(2):
                    k.op(PE, lambda: nc.tensor.matmul(PS[bank][0:R, :], OT_ap(kk, t * 128, t * 128 + R),
                                                      RING[:, s, kk * 1024 + dh * 512:kk * 1024 + dh * 512 + 512],
                                                      start=(kk == 0), stop=(kk == 1)),
                         [OTB[g], ringB[s]], [PB[bank]], signal=(kk == 1))
                if True:
                    k.op(DVE, lambda: nc.vector.tensor_tensor(X[0:R, t, dh * 512:dh * 512 + 512], PS[bank][0:R, :],
                                                              X[0:R, t, dh * 512:dh * 512 + 512], ALU.add),
                         [PB[bank], XB[t]], [XB[t]])
                else:
                    ob = opt_n[0] % 2
                    opt_n[0] += 1
                    tmp = FA[0:R, F_TMP + ob * 512:F_TMP + ob * 512 + 512]
                    k.op(ACT, lambda: nc.scalar.copy(tmp, PS[bank][0:R, :]), [PB[bank]], [OPTB[ob]])
                    k.op(POOL, lambda: nc.gpsimd.tensor_tensor(X[0:R, t, dh * 512:dh * 512 + 512], tmp,
                                                               X[0:R, t, dh * 512:dh * 512 + 512], ALU.add),
                         [OPTB[ob], XB[t]], [XB[t]])
            if after_tile is not None:
                after_tile(t)
        ring_done(rid)

    def q_chunks(rq_s, qp, split=True):
        rq, sq = rq_s
        for g in range(5):
            g0 = g * 512
            g1 = min(g0 + 512, NTOK)
            halves = [(g0, g0 + 256), (g0 + 256, g1)] if (split and g1 - g0 == 512) else [(g0, g1)]
            for (c0, c1) in halves:
                bank = 7
                for c in range(8):
                    k.op(PE, lambda: nc.tensor.matmul(PS[bank][:, 0:c1 - c0], RING[:, sq, c * 128:(c + 1) * 128], hT_ap(c, c0, c1),
                                                      start=(c == 0), stop=(c == 7)),
                         [hTB[g], ringB[sq]], [PB[bank]], signal=(c == 7))
                k.op(DVE, lambda: nc.vector.tensor_scalar(QT_ap(qp, 0, 128, c0, c1), PS[bank][:, 0:c1 - c0], 0.125, None, ALU.mult),
                     [PB[bank]], [QTB[qp]])
                yield
        ring_done(rq)

    def ktrans_chunk(kp, t0, ntl):
        bank2 = 7
        pbf = PS[bank2][:].bitcast(BF16)
        col = 0
        for i in range(ntl):
            Ri = tile_rows(t0 + i)
            k.op(PE, lambda: nc.tensor.transpose(pbf[:, col:col + Ri], ARENA[0:Ri, A_KTOK + i * 128:A_KTOK + i * 128 + 128], ident[0:Ri, 0:Ri]),
                 [KTOKB, identB], [PB[bank2]], signal=(i == ntl - 1))
            col += Ri
        k.op(DVE, lambda: nc.vector.tensor_copy(KT_ap(kp, 0, 128, t0 * 128, t0 * 128 + col), pbf[:, 0:col]),
             [PB[bank2]], [KTB[kp][min(t0 // 4, 4)]])

    def proj_A(li, j, h):
        par = h % 2
        rq_s = ring_next()
        rkv, skv = ring_next()
        k.dma(POOL, ARENA[:, A_CK:A_CK + 1024].rearrange("p (t f) -> p t f", t=8),
              cache_a_k[j, :, h, :].rearrange("(t p) f -> p t f", p=128), [], [CKB], ck_sem)
        k.dma(POOL, Vall_ap(par, 17, 25)[:, :, 0:128], cache_a_v[j, :, h, :].rearrange("(t p) f -> p t f", p=128),
              [], VB[par][17:25], cv_sem[par])
        yield from q_chunks(rq_s, par)
        for t in range(NT):
            R = tile_rows(t)
            g = min(t // 4, 4)
            bank = 7
            for c in range(8):
                k.op(PE, lambda: nc.tensor.matmul(PS[bank][0:R, 0:256], hT_ap(c, t * 128, t * 128 + R), RING[:, skv, c * 256:(c + 1) * 256],
                                                  start=(c == 0), stop=(c == 7)), [hTB[g], ringB[skv]], [PB[bank]], signal=(c == 7))
            sg = stg_n[0] % NSTG
            stg_n[0] += 1
            stg = FA[0:R, F_STG + sg * 256:F_STG + sg * 256 + 256]
            k.op(DVE, lambda: nc.vector.tensor_copy(stg, PS[bank][0:R, 0:256]), [PB[bank]], [STGB[sg]])
            i4 = t % 4
            k.op(POOL, lambda: nc.gpsimd.tensor_copy(ARENA[0:R, A_KTOK + i4 * 128:A_KTOK + i4 * 128 + 128], stg[:, 0:128]),
                 [STGB[sg]], [KTOKB])
            k.op(POOL, lambda: nc.gpsimd.tensor_copy(V_ap(par, 0, R, t, 0, 128), stg[:, 128:256]), [STGB[sg]], [VB[par][t]])
            if t < 16:
                ko = a_k_prompt[j, t * 128:(t + 1) * 128, h, :]
                vo = a_v_prompt[j, t * 128:(t + 1) * 128, h, :]
            else:
                ko = a_k_sample[j, :, h, :]
                vo = a_v_sample[j, :, h, :]
            k.dma(SP, ko, stg[:, 0:128], [STGB[sg]], [], stg_sem[sg])
            k.dma(SP, vo, stg[:, 128:256], [STGB[sg]], [], stg_sem[sg])
            yield
            if i4 == 3 or t == NT - 1:
                ktrans_chunk(par, t - i4, i4 + 1)
                yield
        ring_done(rkv)
        for half in range(2):
            bank2 = 7
            pbf = PS[bank2][:].bitcast(BF16)
            for i in range(4):
                tt = half * 4 + i
                k.op(PE, lambda: nc.tensor.transpose(pbf[:, i * 128:(i + 1) * 128], ARENA[:, A_CK + tt * 128:A_CK + tt * 128 + 128], ident[:, :]),
                     [CKB, identB], [PB[bank2]], signal=(i == 3))
            k.op(DVE, lambda: nc.vector.tensor_copy(KT_ap(par, 0, 128, NTOK + half * 512, NTOK + half * 512 + 512), pbf[:, 0:512]),
                 [PB[bank2]], [KTB[par][5]])
            yield

    def proj_B(li, j, u):
        qp = u % 2
        n = u // 4
        kp = n % 2
        rq_s = ring_next()
        yield from q_chunks(rq_s, qp, split=False)
        if u % 4 != 0:
            return
        rkv, skv = ring_next()
        for dd in range(2):
            k.dma(POOL, ARENA[:, A_CK + dd * 64:A_CK + dd * 64 + 64], cache_b_k[j, :, n, :], [], [CKB], ck_sem)
        k.dma(POOL, V_ap(kp, 0, 128, 17, 0, 64), cache_b_v[j, :, n, :], [], [VB[kp][17]], cv_sem[kp])
        for t in range(NT):
            R = tile_rows(t)
            g = min(t // 4, 4)
            bank = 7
            for c in range(8):
                k.op(PE, lambda: nc.tensor.matmul(PS[bank][0:R, 0:128], hT_ap(c, t * 128, t * 128 + R), RING[:, skv, c * 128:(c + 1) * 128],
                                                  start=(c == 0), stop=(c == 7)), [hTB[g], ringB[skv]], [PB[bank]], signal=(c == 7))
            sg = stg_n[0] % NSTG
            stg_n[0] += 1
            stg = FA[0:R, F_STG + sg * 256:F_STG + sg * 256 + 128]
            k.op(DVE, lambda: nc.vector.tensor_copy(stg, PS[bank][0:R, 0:128]), [PB[bank]], [STGB[sg]])
            i4 = t % 4
            for dd in range(2):
                k.op(POOL, lambda: nc.gpsimd.tensor_copy(ARENA[0:R, A_KTOK + i4 * 128 + dd * 64:A_KTOK + i4 * 128 + dd * 64 + 64], stg[:, 0:64]),
                     [STGB[sg]], [KTOKB])
            k.op(POOL, lambda: nc.gpsimd.tensor_copy(V_ap(kp, 0, R, t, 0, 64), stg[:, 64:128]), [STGB[sg]], [VB[kp][t]])
            if t >= 15:
                if t == 15:
                    ko, vo = b_k_prompt[j, :, n, :], b_v_prompt[j, :, n, :]
                else:
                    ko, vo = b_k_sample[j, 64:128, n, :], b_v_sample[j, 64:128, n, :]
                k.dma(SP, ko, stg[:, 0:64], [STGB[sg]], [], stg_sem[sg])
                k.dma(SP, vo, stg[:, 64:128], [STGB[sg]], [], stg_sem[sg])
            yield
            if i4 == 3 or t == NT - 1:
                ktrans_chunk(kp, t - i4, i4 + 1)
                yield
        ring_done(rkv)
        bank2 = 7
        pbf = PS[bank2][:].bitcast(BF16)
        k.op(PE, lambda: nc.tensor.transpose(pbf[:, 0:128], ARENA[:, A_CK:A_CK + 128], ident[:, :]), [CKB, identB], [PB[bank2]])
        k.op(DVE, lambda: nc.vector.tensor_copy(KT_ap(kp, 0, 128, NTOK, NTOK + 128), pbf[:, 0:128]), [PB[bank2]], [KTB[kp][5]])
        yield

    def drain(gen):
        for _ in gen:
            pass

    def layer_A(li, j):
        lam_init = 0.8 - 0.6 * math.exp(-0.3 * li)
        lamB = Buf("lam")
        k.op(DVE, lambda: nc.vector.tensor_tensor(lamb[:, j, 0:64], lamb[:, j, 0:64], lamb[:, j, 64:128], ALU.mult), [misc], [lamB])
        k.op(DVE, lambda: nc.vector.tensor_tensor(lamb[:, j, 128:192], lamb[:, j, 128:192], lamb[:, j, 192:256], ALU.mult), [lamB], [lamB])
        k.op(DVE, lambda: nc.vector.reduce_sum(lamb[:, j, 256:257], lamb[:, j, 0:64], mybir.AxisListType.X), [lamB], [lamB])
        k.op(DVE, lambda: nc.vector.reduce_sum(lamb[:, j, 257:258], lamb[:, j, 128:192], mybir.AxisListType.X), [lamB], [lamB])
        k.op(ACT, lambda: nc.scalar.activation(out=lamb[:, j, 256:258], in_=lamb[:, j, 256:258], func=AF.Exp), [lamB], [lamB])
        k.op(DVE, lambda: nc.vector.tensor_tensor(lamb[:, j, 258:259], lamb[:, j, 256:257], lamb[:, j, 257:258], ALU.subtract), [lamB], [lamB])
        k.op(DVE, lambda: nc.vector.tensor_scalar(lamb[:, j, 258:259], lamb[:, j, 258:259], lam_init, None, ALU.add), [lamB], [lamB])
        k.op(DVE, lambda: nc.vector.tensor_scalar(gsub[:, j, :], gsub[:, j, :], 1.0 - lam_init, None, ALU.mult), [misc], [lamB])
        lam_ap = lamb[:, j, 258:259]
        if li > 0:
            k.op(DVE, lambda: nc.vector.tensor_copy(E[0:64, :, 192:256], QS[:]), [QSB], [EB])

        k.mark(f"A{li} norm")
        arena_barrier(unit_bufs, hnB + [junkB])
        evac_mode[0] = "alt"
        norm_to_hT(li, list(range(NT)), hT_ap, lambda t0: hTB[min(t0 // 4, 4)], [0, 1, 2, 3])
        arena_barrier(hnB + [junkB], unit_bufs)
        evac_mode[0] = "dve"
        k.op(DVE, lambda: nc.vector.memset(Vall_ap(0, 0, NVT)[:, :, 128:129], 1.0), [], VB[0])

        groups = []
        for g in range(4):
            q = [((4 * g + i) * 128, 128) for i in range(4)]
            steps = []
            for kt in range(4 * g + 4):
                lo = max(kt, 4 * g) - 4 * g
                cls = []
                for qt in range(4 * g + lo, 4 * g + 4):
                    cls.append("D" if qt == kt else ("P" if qt == kt + 1 else "F"))
                steps.append((kt * 128, 128, kt, lo, 3, cls))
            groups.append(dict(q=q, steps=steps))
        steps = []
        for i in range(8):
            steps.append((NTOK + 128 * i, 128, 17 + i, 0, 0, ["P" if i == 7 else "F"]))
        steps.append((2048, 64, 16, 0, 0, ["D"]))
        groups.append(dict(q=[(2048, 64)], steps=steps))

        k.mark(f"A{li} proj0")
        drain(proj_A(li, j, 0))
        finish_E()
        k.op(DVE, lambda: nc.vector.memset(Vall_ap(1, 0, NVT)[:, :, 128:129], 1.0), [], VB[1])
        for h in range(8):
            def finalize(gi, g, acc_ap, h=h):
                q = g["q"]
                G = len(q)
                R = q[0][1]
                evac_acc(G, R, 129)
                nacc = 2 * G

                def s2():
                    oa_stride[0] = 129
                    k.op(DVE, lambda: nc.vector.reciprocal(stats[0:R, 48:48 + nacc].rearrange("p (a o) -> p a o", o=1), oa_sums(R, 0, nacc)),
                         [OAB], [finB])
                    k.op(DVE, lambda: nc.vector.tensor_scalar(stats[0:R, 48 + G:48 + 2 * G], stats[0:R, 48 + G:48 + 2 * G], lam_ap[0:R, :], None, ALU.mult),
                         [finB, lamB], [finB])
                    k.op(DVE, lambda: nc.vector.memset(stats[0:R, 64:64 + G], 0.0), [], [finB])

                def s2q(qi):
                    def f():
                        oa_stride[0] = 129
                        k.op(DVE, lambda: nc.vector.tensor_scalar(oa_ap(G + qi, R, 0, 128), oa_ap(G + qi, R, 0, 128),
                                                                  stats[0:R, 48 + G + qi:49 + G + qi], None, ALU.mult),
                             [OAB, finB], [OAB])
                        k.op(DVE, lambda: nc.vector.scalar_tensor_tensor(
                            FA[0:R, F_O + qi * 128:F_O + qi * 128 + 128], oa_ap(qi, R, 0, 128), stats[0:R, 48 + qi:49 + qi],
                            oa_ap(G + qi, R, 0, 128), ALU.mult, ALU.subtract), [OAB, finB], [OB[qi]])
                    return f

                def s3():
                    for qi in range(G):
                        k.op(ACT, lambda: nc.scalar.activation(out=junk2[0:R, :], in_=FA[0:R, F_O + qi * 128:F_O + qi * 128 + 128],
                                                               func=AF.Square, accum_out=stats[0:R, 64 + qi:65 + qi]),
                             [OB[qi], finB], [junk2B, finB])
                    rstd_from_ss(stats[0:R, 64:64 + G], stats[0:R, 72:72 + G], 128, [finB])

                def s4q(qi):
                    def f():
                        k.op(DVE, lambda: nc.vector.scalar_tensor_tensor(
                            otok[0:R, qi, :], FA[0:R, F_O + qi * 128:F_O + qi * 128 + 128], stats[0:R, 72 + qi:73 + qi],
                            gsub[0:R, j, :], ALU.mult, ALU.mult), [OB[qi], finB, lamB], [otokB[qi]])
                    return f

                def s5():
                    transposes_to_OT(h % 2, q, R)
                if G == 1:
                    return [(1, s2), (1, s2q(0)), (2, s3), (3, s4q(0)), (4, s5)]
                return [(1, s2), (1, s2q(0)), (2, s2q(1)), (2, s2q(2)), (3, s2q(3)), (4, s3),
                        (5, s4q(0)), (5, s4q(1)), (6, s4q(2)), (6, s4q(3)), (7, s5)]

            k.mark(f"A{li} u{h} attn")
            bg = proj_A(li, j, h + 1) if h + 1 < 8 else None
            attention(h, h % 2, h % 2, 128, groups, finalize, bg)
            if h % 2 == 1:
                k.mark(f"A{li} u{h} oproj")
                out_proj_pair(j, emit_sq if h == 7 else None)

    def layer_B(li, j):
        k.op(DVE, lambda: nc.vector.memset(E[0:64, :, 192:256], 0.0), [], [EB])
        k.mark(f"B{li} norm")
        arena_barrier(unit_bufs, hnB + [junkB])
        evac_mode[0] = "alt"
        norm_to_hT(li, list(range(NT)), hT_ap, lambda t0: hTB[min(t0 // 4, 4)], [0, 1, 2, 3])
        arena_barrier(hnB + [junkB], unit_bufs)
        evac_mode[0] = "dve"
        for p in range(2):
            k.op(DVE, lambda: nc.vector.memset(Vall_ap(p, 0, NVT)[:, :, 64:65], 1.0), [], VB[p])
        groups = []
        for g in range(4):
            q = [((4 * g + i) * 128, 128) for i in range(4)]
            steps = []
            for kt in range(max(4 * g - 1, 0), 4 * g + 4):
                lo = max(kt, 4 * g)
                hi = min(kt + 1, 4 * g + 3)
                cls = ["D" if qt == kt else "P" for qt in range(lo, hi + 1)]
                steps.append((kt * 128, 128, kt, lo - 4 * g, hi - 4 * g, cls))
            groups.append(dict(q=q, steps=steps))
        groups.append(dict(q=[(2048, 64)], steps=[(NTOK, 128, 17, 0, 0, ["P"]), (2048, 64, 16, 0, 0, ["D"])]))

        k.mark(f"B{li} proj0")
        drain(proj_B(li, j, 0))
        for u in range(8):
            def finalize(gi, g, acc_ap, u=u):
                q = g["q"]
                G = len(q)
                R = q[0][1]
                evac_acc(G, R, 65)

                def s2():
                    oa_stride[0] = 65
                    for s in range(2):
                        hd = 2 * u + s
                        k.op(DVE, lambda: nc.vector.tensor_scalar(stats[0:R, 48 + s * G:48 + (s + 1) * G].rearrange("p (a o) -> p a o", o=1),
                                                                  oa_sums(R, s * G, G), esink[0:R, j, hd:hd + 1], None, ALU.add),
                             [OAB, misc], [finB])
                    k.op(DVE, lambda: nc.vector.reciprocal(stats[0:R, 48:48 + 2 * G], stats[0:R, 48:48 + 2 * G]), [finB], [finB])
                    for qi in range(G):
                        for s in range(2):
                            a = s * G + qi
                            k.op(DVE, lambda: nc.vector.tensor_scalar(otok[0:R, qi, s * 64:s * 64 + 64], oa_ap(a, R, 0, 64),
                                                                      stats[0:R, 48 + a:49 + a], None, ALU.mult),
                                 [OAB, finB], [otokB[qi]])

                def s5():
                    transposes_to_OT(u % 2, q, R)
                return [(1, s2), (3, s5)]

            k.mark(f"B{li} u{u} attn")
            bg = proj_B(li, j, u + 1) if u + 1 < 8 else None
            attention(u, u % 2, (u // 4) % 2, 64, groups, finalize, bg)
            if u % 2 == 1:
                k.mark(f"B{li} u{u} oproj")
                out_proj_pair(j, emit_sq if u == 7 else None)

    mhTB = Buf("mhT")
    ATB = Buf("AT")
    RELB = [Buf("rel0"), Buf("rel1")]

    def mlp(li):
        evac_mode[0] = "alt"
        arena_barrier(hTB + OTB, [mhTB, ATB])
        arena_barrier(unit_bufs, hnB + [junkB])
        arena_barrier(TMPB[0] + TMPB[1] + OPTB + fa_users, RELB)
        for tg in range(2):
            tiles = list(range(8)) if tg == 0 else list(range(8, NT))
            tok0 = tiles[0] * 128
            ntok = sum(tile_rows(t) for t in tiles)
            mov = [(0, 512), (512, 1024)] + ([(1024, 1088)] if tg == 1 else [])

            def dst_fn(c, c0, c1):
                return ARENA[:, M_hT + c * 1088 + (c0 - tok0):M_hT + c * 1088 + (c1 - tok0)]
            k.mark(f"M{li} tg{tg} norm")
            norm_to_hT(4 + li, tiles, dst_fn, lambda t0: mhTB, [0, 1, 2, 3])
            for fq in range(4):
                k.mark(f"M{li} tg{tg} fq{fq} up")
                ups = [ring_next() for cc in range(4)]
                for fc in range(8):
                    s = ups[fc // 2][1]
                    for (m0, m1) in mov:
                        bank = next_bank([0, 1, 2, 3])
                        for c in range(8):
                            k.op(PE, lambda c=c, bank=bank, s=s, fc=fc, m0=m0, m1=m1: nc.tensor.matmul(
                                PS[bank][:, 0:m1 - m0], RING[:, s, c * 256 + (fc % 2) * 128:c * 256 + (fc % 2) * 128 + 128],
                                ARENA[:, M_hT + c * 1088 + m0:M_hT + c * 1088 + m1], start=(c == 0), stop=(c == 7)),
                                [mhTB, ringB[s]], [PB[bank]], signal=(c == 7))
                        rb = evac_rr[0] % 2
                        evac_rr[0] += 1
                        rel = FA[:, rb * 512:rb * 512 + (m1 - m0)]
                        k.op(ACT, lambda bank=bank, rel=rel, m0=m0, m1=m1: nc.scalar.activation(out=rel, in_=PS[bank][:, 0:m1 - m0], func=AF.Relu),
                             [PB[bank]], [RELB[rb]])
                        k.op(DVE, lambda rel=rel, fc=fc, m0=m0, m1=m1: nc.vector.tensor_tensor(
                            ARENA[:, M_AT + fc * 1088 + m0:M_AT + fc * 1088 + m1], rel, rel, ALU.mult), [RELB[rb]], [ATB])
                    if fc % 2 == 1:
                        ring_done(ups[fc // 2][0])
                k.mark(f"M{li} tg{tg} fq{fq} down")
                downs = [ring_next() for cc in range(4)]
                for t in tiles:
                    R = tile_rows(t)
                    lc = t * 128 - tok0
                    for dh in range(2):
                        bank = next_bank([4, 5, 6, 7])
                        for fc in range(8):
                            s = downs[fc // 2][1]
                            k.op(PE, lambda fc=fc, s=s, bank=bank, R=R, lc=lc, dh=dh: nc.tensor.matmul(
                                PS[bank][0:R, :], ARENA[:, M_AT + fc * 1088 + lc:M_AT + fc * 1088 + lc + R],
                                RING[:, s, (fc % 2) * 1024 + dh * 512:(fc % 2) * 1024 + dh * 512 + 512],
                                start=(fc == 0), stop=(fc == 7)), [ATB, ringB[s]], [PB[bank]], signal=(fc == 7))
                        k.op(DVE, lambda bank=bank, t=t, R=R, dh=dh: nc.vector.tensor_tensor(
                            X[0:R, t, dh * 512:dh * 512 + 512], PS[bank][0:R, :], X[0:R, t, dh * 512:dh * 512 + 512], ALU.add),
                            [PB[bank], XB[t]], [XB[t]])
                    if fq == 3:
                        emit_sq(t)
                for cc in range(4):
                    ring_done(downs[cc][0])
        arena_barrier([mhTB, ATB], hTB + OTB)
        arena_barrier(hnB + [junkB], unit_bufs)
        arena_barrier(RELB, TMPB[0] + TMPB[1] + OPTB)

    def final_norm():
        gB = Buf("gfin")
        allfa = [OAB, T1B] + OB + STGB + TMPB[0] + TMPB[1] + RELB + OPTB
        arena_barrier(allfa, [gB])
        arena_barrier(unit_bufs, [junkB])
        k.dma(SP, FA[:, 0:1024], final_norm_g.ap().partition_broadcast(128), [], [gB], k.semslot("gfin"))
        yB = [Buf("y0"), Buf("y1")]
        arena_barrier(allfa, yB)
        ysem = [k.semslot("y0"), k.semslot("y1")]
        k.store_sems += ysem
        for t in range(NT):
            if t not in pre_sq:
                emit_sq(t)
        rstd_from_ss(stats[:, 0:NT], stats[:, 17:17 + NT], D, [ssB])
        for t in range(NT):
            R = tile_rows(t)
            yb = t % 2
            ya = FA[0:R, 1024 + yb * 1024:2048 + yb * 1024]
            k.op(DVE, lambda: nc.vector.scalar_tensor_tensor(ya, X[0:R, t, :], stats[0:R, 17 + t:18 + t], FA[0:R, 0:1024],
                                                             ALU.mult, ALU.mult), [XB[t], ssB, gB], [yB[yb]])
            dst = y_prompt[t * 128:(t + 1) * 128, :] if t < 16 else y_sample
            k.dma(SP, dst, ya, [yB[yb]], [], ysem[yb])

    layer_A(0, 0)
    mlp(0)
    layer_B(1, 0)
    mlp(1)
    layer_A(2, 1)
    mlp(2)
    layer_B(3, 1)
    mlp(3)
    k.mark("final")
    final_norm()
    k.mark("end")
    for ss in k.store_sems:
        if ss[1] > 0:
            nc.sync.wait_ge(ss[0], ss[1])


_CACHE = {}


def _consts():
    import jax
    import jax.numpy as jnp
    cpu = jax.devices("cpu")[0]
    with jax.default_device(cpu):
        rel = jnp.asarray(127 - np.arange(383), dtype=jnp.int32)
        nb = 16
        n = -rel
        ret = jnp.where(n < 0, nb, 0)
        n = jnp.abs(n)
        max_exact = nb // 2
        nf = jnp.maximum(n, 1).astype(jnp.float32)
        large = max_exact + (jnp.log(nf / max_exact) / math.log(128 / max_exact) * (nb - max_exact)).astype(jnp.int32)
        large = jnp.minimum(large, nb - 1)
        bucket = np.asarray(ret + jnp.where(n < max_exact, n, large))
    oh = np.zeros((32, 383), np.float32)
    oh[bucket, np.arange(383)] = 1.0
    ident = np.eye(128, dtype=np.float32)
    return oh, ident


def kernel(**inputs):
    inp = {k_: np.asarray(v) for k_, v in inputs.items()}
    if "nc" not in _CACHE:
        _CACHE["nc"] = build_program()
        _CACHE["consts"] = _consts()
    nc = _CACHE["nc"]
    oh, ident = _CACHE["consts"]
    f = lambda a: np.ascontiguousarray(a, dtype=np.float32)
    shared = {
        "rel_table": f(inp["rel_table"]), "norm_mix_g": f(inp["norm_mix_g"]), "norm_mlp_g": f(inp["norm_mlp_g"]),
        "final_norm_g": f(inp["final_norm_g"]), "a_w_qkv": f(inp["a_w_qkv"]),
        "a_lambda": f(inp["a_lambda"]).reshape(2, 256), "a_subln_g": f(inp["a_subln_g"]), "a_w_o": f(inp["a_w_o"]),
        "b_w_qkv": f(inp["b_w_qkv"]), "b_sinks": f(inp["b_sinks"]), "b_w_o": f(inp["b_w_o"]),
        "mlp_w_up": f(inp["mlp_w_up"]), "mlp_w_down": f(inp["mlp_w_down"]), "c_ohr": oh, "c_ident": ident,
    }
    in_maps = []
    for b in range(8):
        m = dict(shared)
        m["x_prompt"] = f(inp["x_prompt"][b])
        m["x_sample"] = f(inp["x_sample"][b])
        m["cache_a_k"] = f(inp["cache_a_k"][:, b]).reshape(2, PAST, 8, 128)
        m["cache_a_v"] = f(inp["cache_a_v"][:, b])
        m["cache_b_k"] = f(inp["cache_b_k"][:, b])
        m["cache_b_v"] = f(inp["cache_b_v"][:, b])
        in_maps.append(m)
    res = run_bass_kernel_spmd(nc, in_maps, core_ids=list(range(8)))
    R = res.results
    st = lambda name, ax, shp=None: np.stack([np.asarray(R[b][name], dtype=np.float32) for b in range(8)], axis=ax)
    y_prompt = st("y_prompt", 0)
    y_sample = st("y_sample", 0)
    a_k_prompt = st("a_k_prompt", 1).reshape(2, 8, SEQ, 8, 2, 64)
    a_v_prompt = st("a_v_prompt", 1).reshape(2, 8, SEQ, 8, 128)
    b_k_prompt = st("b_k_prompt", 1)
    b_v_prompt = st("b_v_prompt", 1)
    a_k_sample = st("a_k_sample", 1).reshape(2, 8, TS, 8, 2, 64)
    a_v_sample = st("a_v_sample", 1).reshape(2, 8, TS, 8, 128)
    b_k_sample = st("b_k_sample", 1)
    b_v_sample = st("b_v_sample", 1)
    return (y_prompt, y_sample, a_k_prompt, a_v_prompt, b_k_prompt, b_v_prompt,
            a_k_sample, a_v_sample, b_k_sample, b_v_sample)
```

```python
import math
from contextlib import ExitStack
import numpy as np
import concourse.bass as bass
import concourse.mybir as mybir
from concourse.bass_utils import run_bass_kernel_spmd

F32 = mybir.dt.float32
BF16 = mybir.dt.bfloat16
AF = mybir.ActivationFunctionType
ALU = mybir.AluOpType

D = 1024
SEQ = 2048
TS = 64
NTOK = SEQ + TS
NT = 17
PAST = 1024
WIN = 128
DFF = 4096
EPS = 1e-6
NRING = 6
RING_EL = 2048


def tile_rows(t):
    return 128 if t < 16 else 64


class Tok:
    __slots__ = ("sem", "val", "eng")

    def __init__(self, sem, val, eng):
        self.sem, self.val, self.eng = sem, val, eng


class Buf:
    __slots__ = ("name", "w", "r", "excl")

    def __init__(self, name, excl=False):
        self.name, self.w, self.r, self.excl = name, None, {}, excl


class Eng:
    def __init__(self, name, eng, sem, is_pe=False):
        self.name, self.eng, self.sem, self.count = name, eng, sem, 0
        self.seen = {}
        self.is_pe = is_pe

    def wait(self, tok):
        key = id(tok.sem)
        if self.seen.get(key, 0) >= tok.val:
            return
        self.eng.wait_ge(tok.sem, tok.val)
        self.seen[key] = tok.val


class K:
    def __init__(self, nc, es):
        self.nc, self.es = nc, es
        self.pe = Eng("pe", nc.tensor, self.sem("s_pe"), True)
        self.act = Eng("act", nc.scalar, self.sem("s_act"))
        self.dve = Eng("dve", nc.vector, self.sem("s_dve"))
        self.pool = Eng("pool", nc.gpsimd, self.sem("s_pool"))
        self.sp = Eng("sp", nc.sync, self.sem("s_sp"))
        self.nsem = 5
        self.store_sems = []
        self.marks = []
        self.pe_n = 0

    def sem(self, name):
        return self.es.enter_context(self.nc.semaphore(name))

    def sb(self, name, shape, dt):
        return self.es.enter_context(self.nc.sbuf_tensor(name, shape, dt))

    def _deps(self, E, reads, writes, own_sem=None):
        toks = []
        for b in reads:
            if b.w is not None:
                t = b.w
                if not (t.eng is E and E.is_pe):
                    toks.append(t)
            if b.excl:
                for t in b.r.values():
                    if t.eng is not E:
                        toks.append(t)
        for b in writes:
            if b.w is not None and b.w.eng is not E and b.w.sem is not own_sem:
                toks.append(b.w)
            for t in b.r.values():
                if t.eng is not E:
                    toks.append(t)
        for t in toks:
            E.wait(t)

    def _commit(self, tok, reads, writes):
        for b in reads:
            key = id(tok.sem)
            b.r[key] = tok
        for b in writes:
            b.w = tok
            b.r = {}

    def mark(self, label):
        self.marks.append((label, self.pe_n))

    def op(self, E, fn, reads=(), writes=(), signal=True):
        self._deps(E, reads, writes)
        inst = fn()
        if E.is_pe:
            self.pe_n += 1
        if signal:
            E.count += 1
            inst.then_inc(E.sem, 1)
            tok = Tok(E.sem, E.count, E)
        else:
            tok = Tok(E.sem, E.count + 1, E)
        self._commit(tok, reads, writes)
        return tok

    def dma(self, Q, out, in_, reads, writes, semslot, n=1, **kw):
        self._deps(Q, reads, writes, own_sem=semslot[0])
        Q.eng.dma_start(out=out, in_=in_, **kw).then_inc(semslot[0], 16)
        semslot[1] += 16
        tok = Tok(semslot[0], semslot[1], None)
        self._commit(tok, reads, writes)
        return tok

    def semslot(self, name):
        return [self.sem(name), 0]


def build_program():
    nc = bass.Bass("TRN2", target_bir_lowering=False)
    es = ExitStack()
    with es:
        _build(nc, es)
    return nc


_MARKS = []


def _build(nc, es):
    k = K(nc, es)
    _MARKS.clear()
    k.marks = _MARKS
    PE, ACT, DVE, POOL, SP = k.pe, k.act, k.dve, k.pool, k.sp

    def din(name, shape, dt=F32):
        return nc.dram_tensor(name, list(shape), dt, kind="ExternalInput")

    def dout(name, shape, dt=F32):
        return nc.dram_tensor(name, list(shape), dt, kind="ExternalOutput")

    x_prompt = din("x_prompt", [SEQ, D]).ap()
    x_sample = din("x_sample", [TS, D]).ap()
    cache_a_k = din("cache_a_k", [2, PAST, 8, 128]).ap()
    cache_a_v = din("cache_a_v", [2, PAST, 8, 128]).ap()
    cache_b_k = din("cache_b_k", [2, WIN, 2, 64]).ap()
    cache_b_v = din("cache_b_v", [2, WIN, 2, 64]).ap()
    rel_table = din("rel_table", [32, 16])
    norm_mix_g = din("norm_mix_g", [4, D])
    norm_mlp_g = din("norm_mlp_g", [4, D])
    final_norm_g = din("final_norm_g", [D])
    a_w_qkv = din("a_w_qkv", [2, D, 3072]).ap()
    a_lambda = din("a_lambda", [2, 256])
    a_subln_g = din("a_subln_g", [2, 128])
    a_w_o = din("a_w_o", [2, D, D]).ap()
    b_w_qkv = din("b_w_qkv", [2, D, 1280]).ap()
    b_sinks = din("b_sinks", [2, 16])
    b_w_o = din("b_w_o", [2, D, D]).ap()
    mlp_w_up = din("mlp_w_up", [4, D, DFF]).ap()
    mlp_w_down = din("mlp_w_down", [4, DFF, D]).ap()
    ohr_d = din("c_ohr", [32, 383]).ap()
    ident_d = din("c_ident", [128, 128]).ap()

    y_prompt = dout("y_prompt", [SEQ, D]).ap()
    y_sample = dout("y_sample", [TS, D]).ap()
    a_k_prompt = dout("a_k_prompt", [2, SEQ, 8, 128]).ap()
    a_v_prompt = dout("a_v_prompt", [2, SEQ, 8, 128]).ap()
    b_k_prompt = dout("b_k_prompt", [2, WIN, 2, 64]).ap()
    b_v_prompt = dout("b_v_prompt", [2, WIN, 2, 64]).ap()
    a_k_sample = dout("a_k_sample", [2, TS, 8, 128]).ap()
    a_v_sample = dout("a_v_sample", [2, TS, 8, 128]).ap()
    b_k_sample = dout("b_k_sample", [2, WIN, 2, 64]).ap()
    b_v_sample = dout("b_v_sample", [2, WIN, 2, 64]).ap()
    efd = nc.dram_tensor("efd_scratch", [16, 384], F32, kind="Internal")

    X = k.sb("X", [128, NT, D], F32)
    XB = [Buf(f"X{t}") for t in range(NT)]
    E = k.sb("E", [128, 16, 256], BF16)
    EB = Buf("E")
    QS = k.sb("QS", [64, 16, 64], BF16)
    RING = k.sb("RING", [128, NRING, RING_EL], BF16)
    ARENA = k.sb("ARENA", [128, 41856], BF16)
    FA = k.sb("FA", [128, 4224], F32)
    ident = k.sb("ident", [128, 128], BF16)
    gT = k.sb("gT", [128, 9, 8], F32)
    cfar = k.sb("cfar", [128, 16], F32)
    esink = k.sb("esink", [128, 2, 16], F32)
    lamb = k.sb("lamb", [128, 2, 260], F32)
    gsub = k.sb("gsub", [128, 2, 128], F32)
    stats = k.sb("stats", [128, 96], F32)
    otok = k.sb("otok", [128, 4, 128], BF16)
    junk2 = k.sb("junk2", [128, 128], BF16)
    junk1 = k.sb("junk1", [128, D], BF16)
    tbs = FA[0:32, 800:816]
    ohs = FA[0:32, 0:384]
    efs = FA[0:16, 384:768]

    PS = [es.enter_context(nc.psum_tensor(f"ps{i}", [128, 512], F32)) for i in range(8)]
    PB = [Buf(f"ps{i}", excl=True) for i in range(8)]

    A_hT = 0
    A_OT = A_hT + 8 * NTOK
    A_U = A_OT + 2 * NTOK
    KTW = NTOK + PAST
    VW = 132
    NVT = 25
    U_QT = 0
    U_KT = NTOK
    U_V = U_KT + KTW
    USZ = U_V + NVT * VW
    A_PT = A_U + 2 * USZ
    A_KTOK = A_PT + 4 * 512
    A_CK = A_KTOK + 512
    A_END = A_CK + 1024
    assert A_END <= 41856
    A_HN = A_U
    A_JUNK = A_HN + 4096
    M_hT = 0
    M_AT = 8 * 1088
    M_END = M_AT + 8 * 1088
    assert M_END <= A_U

    def hT_ap(c, c0, c1):
        return ARENA[:, A_hT + c * NTOK + c0:A_hT + c * NTOK + c1]

    def OT_ap(kk, c0, c1):
        return ARENA[:, A_OT + kk * NTOK + c0:A_OT + kk * NTOK + c1]

    def hn_ap(i, R):
        return ARENA[0:R, A_HN + i * 1024:A_HN + (i + 1) * 1024]

    def junk_ap(R, w=1024):
        return ARENA[0:R, A_JUNK:A_JUNK + w]

    F_TMP = 0
    F_OA = 1024
    F_STG = 2560
    F_O = 3584
    F_T1 = 4096
    F_END = 4224

    hTB = [Buf(f"hT{g}") for g in range(5)]
    OTB = [Buf(f"OT{g}") for g in range(5)]
    ringB = [Buf(f"ring{i}") for i in range(NRING)]
    ring_sem = [k.semslot(f"ringsem{i}") for i in range(NRING)]

    def _maxtok(a, b):
        if a is None:
            return b
        return a if a.val >= b.val else b

    def arena_barrier(bufs_old, bufs_new):
        for nb in bufs_new:
            for ob in bufs_old:
                if ob.w is not None:
                    nb.r[("w", id(ob.w.sem))] = _maxtok(nb.r.get(("w", id(ob.w.sem))), ob.w)
                for kk, t in ob.r.items():
                    nb.r[("r", id(t.sem))] = _maxtok(nb.r.get(("r", id(t.sem))), t)

    reqs = []
    ring_state = dict(issued=0, consumed=0, released=set())

    def gen_requests():
        def attn_layer(kind, j, w_qkv, w_o):
            def unit_reqs(u):
                if kind == "A":
                    reqs.append([(lambda ra: ra[:, 0:1024].rearrange("p (c f) -> p c f", c=8),
                                  w_qkv[j, :, u * 128:(u + 1) * 128].rearrange("(c p) f -> p c f", p=128))])
                    reqs.append([(lambda ra: ra[:, 0:2048].rearrange("p (c f) -> p c f", c=8)[:, :, 0:128],
                                  w_qkv[j, :, 1024 + u * 128:1024 + (u + 1) * 128].rearrange("(c p) f -> p c f", p=128)),
                                 (lambda ra: ra[:, 0:2048].rearrange("p (c f) -> p c f", c=8)[:, :, 128:256],
                                  w_qkv[j, :, 2048 + u * 128:2048 + (u + 1) * 128].rearrange("(c p) f -> p c f", p=128))])
                else:
                    n = u // 4
                    reqs.append([(lambda ra: ra[:, 0:1024].rearrange("p (c f) -> p c f", c=8),
                                  w_qkv[j, :, u * 128:(u + 1) * 128].rearrange("(c p) f -> p c f", p=128))])
                    if u % 4 == 0:
                        reqs.append([(lambda ra: ra[:, 0:1024].rearrange("p (c f) -> p c f", c=8)[:, :, 0:64],
                                      w_qkv[j, :, 1024 + n * 64:1024 + (n + 1) * 64].rearrange("(c p) f -> p c f", p=128)),
                                     (lambda ra: ra[:, 0:1024].rearrange("p (c f) -> p c f", c=8)[:, :, 64:128],
                                      w_qkv[j, :, 1152 + n * 64:1152 + (n + 1) * 64].rearrange("(c p) f -> p c f", p=128))])
            unit_reqs(0)
            for u in range(8):
                if u + 1 < 8:
                    unit_reqs(u + 1)
                if u % 2 == 1:
                    r0 = (u - 1) * 128
                    reqs.append([(lambda ra: ra[:, 0:2048].rearrange("p (k d) -> p k d", k=2),
                                  w_o[j, r0:r0 + 256, :].rearrange("(k p) d -> p k d", p=128))])

        def mlp_layer(li):
            for tg in range(2):
                for fq in range(4):
                    for cc in range(4):
                        f0 = fq * 1024 + cc * 256
                        reqs.append([(lambda ra: ra[:, 0:2048].rearrange("p (c f) -> p c f", c=8),
                                      mlp_w_up[li, :, f0:f0 + 256].rearrange("(c p) f -> p c f", p=128))])
                    for cc in range(4):
                        r0 = fq * 1024 + cc * 256
                        reqs.append([(lambda ra: ra[:, 0:2048].rearrange("p (k d) -> p k d", k=2),
                                      mlp_w_down[li, r0:r0 + 256, :].rearrange("(k p) d -> p k d", p=128))])
        attn_layer("A", 0, a_w_qkv, a_w_o)
        mlp_layer(0)
        attn_layer("B", 0, b_w_qkv, b_w_o)
        mlp_layer(1)
        attn_layer("A", 1, a_w_qkv, a_w_o)
        mlp_layer(2)
        attn_layer("B", 1, b_w_qkv, b_w_o)
        mlp_layer(3)

    def ring_try_issue():
        while ring_state["issued"] < len(reqs):
            n = ring_state["issued"]
            if n >= NRING and (n - NRING) not in ring_state["released"]:
                break
            s = n % NRING
            for (dfn, src) in reqs[n]:
                k.dma(POOL, dfn(RING[:, s, :]), src, [], [ringB[s]], ring_sem[s])
            ring_state["issued"] += 1

    def ring_next():
        n = ring_state["consumed"]
        ring_state["consumed"] += 1
        ring_try_issue()
        assert n < ring_state["issued"], f"ring chunk {n} not issued (deadlock in request order)"
        return n, n % NRING

    def ring_done(n):
        ring_state["released"].add(n)
        ring_try_issue()

    ld = k.semslot("ld")
    tabB = Buf("tab")
    misc = Buf("misc")
    ssB = Buf("ss")
    tab_sem = k.semslot("tabld")
    k.dma(SP, tbs, rel_table.ap(), [], [tabB], tab_sem)
    k.dma(SP, ohs[:, 0:383], ohr_d, [], [tabB], tab_sem)
    for g4 in range(4):
        k.dma(SP, X[:, 4 * g4:4 * g4 + 4, :],
              x_prompt[512 * g4:512 * g4 + 512, :].rearrange("(t p) d -> p t d", p=128),
              [], [XB[4 * g4 + i] for i in range(4)], k.semslot(f"xl{g4}"))
    k.op(DVE, lambda: nc.vector.memset(X[:, 16, :], 0.0), [], [XB[16]])
    k.dma(SP, X[0:64, 16, :], x_sample, [], [XB[16]], k.semslot("xls"))
    identB = Buf("ident")
    k.dma(POOL, ident[:], ident_d, [], [identB], k.semslot("identld"))
    gen_requests()
    ring_try_issue()
    with nc.allow_non_contiguous_dma(reason="small strided parameter loads"):
        k.dma(SP, gT[:, 0:4, :], norm_mix_g.ap().rearrange("l (c p) -> p l c", p=128), [], [misc], ld)
        k.dma(SP, cfar[:], bass.AP(rel_table, 15 * 16, [[0, 128], [1, 16]]), [], [misc], ld)
        k.dma(SP, esink[:].rearrange("p l h -> p (l h)"), bass.AP(b_sinks, 0, [[0, 128], [1, 32]]), [], [misc], ld)
        k.dma(SP, lamb[:, :, 0:256], bass.AP(a_lambda, 0, [[0, 128], [256, 2], [1, 256]]), [], [misc], ld)
        k.dma(SP, gsub[:], bass.AP(a_subln_g, 0, [[0, 128], [128, 2], [1, 128]]), [], [misc], ld)
        k.dma(SP, gT[:, 4:8, :], norm_mlp_g.ap().rearrange("l (c p) -> p l c", p=128), [], [misc], ld)
        k.dma(SP, gT[:, 8, :], final_norm_g.ap().rearrange("(c p) -> p c", p=128), [], [misc], ld)
    st_misc = k.semslot("st_misc")
    k.store_sems.append(st_misc)
    for j in range(2):
        k.dma(SP, b_k_sample[j, 0:64], cache_b_k[j, 64:128], [], [], st_misc)
        k.dma(SP, b_v_sample[j, 0:64], cache_b_v[j, 64:128], [], [], st_misc)

    k.op(PE, lambda: nc.tensor.matmul(PS[0][0:16, 0:383], tbs, ohs[:, 0:383], start=True, stop=True),
         [tabB], [PB[0]])
    ncf = stats[0:16, 90:91]
    stB = Buf("stB")
    k.op(DVE, lambda: nc.vector.tensor_scalar(ncf, PS[0][0:16, 382:383], -1.0, None, ALU.mult), [PB[0]], [stB])
    efB = Buf("efs")
    k.op(ACT, lambda: nc.scalar.activation(out=efs[:, 0:383], in_=PS[0][0:16, 0:383], func=AF.Exp, bias=ncf, scale=1.0),
         [PB[0], stB], [efB])
    efdB = Buf("efd")
    k.dma(SP, efd.ap()[:, 0:383], efs[:, 0:383], [efB], [efdB], k.semslot("efd_st"))
    esem = k.semslot("eld")
    e32B = Buf("e32")
    E32 = ARENA[:, A_U + USZ:A_U + USZ + 8192].bitcast(F32).rearrange("p (m c) -> p m c", m=16)
    for p in range(128):
        src = bass.AP(efd, 127 - p, [[384, 16], [1, 256]])
        k.dma(SP, E32[p:p + 1, :, :], src, [efdB], [e32B], esem)
    QSB = Buf("QS")
    e_done = [False]

    def finish_E():
        if e_done[0]:
            return
        e_done[0] = True
        k.op(DVE, lambda: nc.vector.tensor_copy(E[:, 0:8, :], E32[:, 0:8, :]), [e32B], [EB])
        k.op(DVE, lambda: nc.vector.tensor_copy(E[:, 8:16, :], E32[:, 8:16, :]), [e32B], [EB])
        arena_barrier([e32B], unit_bufs)
        k.op(DVE, lambda: nc.vector.memset(E[64:128, :, 0:64], 0.0), [EB], [EB])
        k.op(DVE, lambda: nc.vector.tensor_copy(QS[:], E[0:64, :, 192:256]), [EB], [QSB])

    k.op(DVE, lambda: nc.vector.memset(stats[:, 0:34], 0.0), [], [ssB])
    k.op(ACT, lambda: nc.scalar.activation(out=esink[:].rearrange("p l h -> p (l h)"),
                                           in_=esink[:].rearrange("p l h -> p (l h)"), func=AF.Exp), [misc], [misc])
    fa_users = []

    ps_rr = [0]

    def next_bank(cands):
        b = cands[ps_rr[0] % len(cands)]
        ps_rr[0] += 1
        return b

    evac_rr = [0]
    evac_mode = ["alt"]

    def evac_engine():
        if evac_mode[0] == "dve":
            return DVE
        evac_rr[0] += 1
        return ACT if evac_rr[0] % 8 in (0, 2, 3, 5, 6) else DVE

    def rstd_from_ss(ss_ap, out_ap, n, bufs):
        k.op(ACT, lambda: nc.scalar.activation(out=out_ap, in_=ss_ap, func=AF.Ln, scale=1.0 / n, bias=EPS), bufs, bufs)
        k.op(ACT, lambda: nc.scalar.activation(out=out_ap, in_=out_ap, func=AF.Exp, scale=-0.5), bufs, bufs)

    hnB = [Buf(f"hn{i}") for i in range(8)]
    junkB = Buf("junk")
    junk2B = Buf("junk2")

    pre_sq = set()
    junk1B = Buf("junk1")

    def emit_sq(t):
        R = tile_rows(t)
        k.op(ACT, lambda: nc.scalar.activation(out=junk1[0:R, :], in_=X[0:R, t, :], func=AF.Square, accum_out=stats[0:R, t:t + 1]),
             [XB[t], ssB], [junk1B, ssB])
        pre_sq.add(t)

    def norm_to_hT(gl, tiles, dst_fn, dstB_fn, banks):
        nt = len(tiles)
        t_0 = tiles[0]
        for t in tiles:
            if t not in pre_sq:
                emit_sq(t)
        rstd_from_ss(stats[:, t_0:t_0 + nt], stats[:, 17 + t_0:17 + t_0 + nt], D, [ssB])
        ngrp = (nt + 3) // 4
        grp_cols = {}

        def emit_hn(gidx):
            grp = tiles[4 * gidx:4 * gidx + 4]
            col = 0
            cols = []
            for i, t in enumerate(grp):
                R = tile_rows(t)
                hb = (gidx % 2) * 4 + i
                k.op(DVE, lambda: nc.vector.tensor_scalar(hn_ap(hb, R), X[0:R, t, :], stats[0:R, 17 + t:18 + t], None, ALU.mult),
                     [XB[t], ssB], [hnB[hb]])
                cols.append((t, R, hb, col))
                col += R
            grp_cols[gidx] = (cols, col)

        def emit_TE(gidx, after_T):
            cols, W = grp_cols[gidx]
            t0 = cols[0][0]
            c0 = t0 * 128
            for c in range(8):
                bank = next_bank(banks)
                pbf = PS[bank][:].bitcast(BF16)
                for jj, (t, R, hb, cc) in enumerate(cols):
                    k.op(PE, lambda: nc.tensor.transpose(pbf[:, cc:cc + R], hn_ap(hb, R)[:, c * 128:(c + 1) * 128], ident[0:R, 0:R]),
                         [hnB[hb], identB], [PB[bank]], signal=(jj == len(cols) - 1))
                if c == 0 and after_T is not None:
                    after_T()
                Eg = evac_engine()
                if Eg is ACT:
                    k.op(ACT, lambda: nc.scalar.activation(out=dst_fn(c, c0, c0 + W), in_=pbf[:, 0:W], func=AF.Copy,
                                                           scale=gT[:, gl, c:c + 1]), [PB[bank], misc], [dstB_fn(t0)])
                else:
                    k.op(DVE, lambda: nc.vector.tensor_scalar(dst_fn(c, c0, c0 + W), pbf[:, 0:W], gT[:, gl, c:c + 1], None, ALU.mult),
                         [PB[bank], misc], [dstB_fn(t0)])

        emit_hn(0)
        for gidx in range(ngrp):
            emit_TE(gidx, (lambda g=gidx: emit_hn(g + 1)) if gidx + 1 < ngrp else None)
        k.op(DVE, lambda: nc.vector.memset(stats[:, t_0:t_0 + nt], 0.0), [], [ssB])
        for t in tiles:
            pre_sq.discard(t)

    QTB = [Buf("QT0"), Buf("QT1")]
    KTB = [[Buf(f"KT{p}_{g}") for g in range(6)] for p in range(2)]
    VB = [[Buf(f"V{p}_{t}") for t in range(NVT)] for p in range(2)]
    PTB = [[Buf(f"PT{s}{i}") for i in range(2)] for s in range(2)]
    TMPB = [[Buf(f"TMP{s}{i}") for i in range(2)] for s in range(2)]
    OAB = Buf("OA")
    KTOKB = Buf("KTOK")
    CKB = Buf("CK")
    NSTG = 4
    STGB = [Buf(f"stg{i}") for i in range(NSTG)]
    stg_sem = [k.semslot(f"stg{i}") for i in range(NSTG)]
    k.store_sems += stg_sem
    stg_n = [0]
    OB = [Buf(f"O{i}") for i in range(4)]
    T1B = Buf("T1")
    otokB = [Buf(f"otok{i}") for i in range(4)]
    finB = Buf("fin")
    OPTB = [Buf("opt0"), Buf("opt1")]
    arena_barrier(fa_users, [OAB, T1B] + STGB + OB + OPTB)
    ck_sem = k.semslot("ck")
    cv_sem = [k.semslot("cv0"), k.semslot("cv1")]
    unit_bufs = QTB + KTB[0] + KTB[1] + VB[0] + VB[1]

    def QT_ap(p, r0, r1, c0, c1):
        o = A_U + p * USZ + U_QT
        return ARENA[r0:r1, o + c0:o + c1]

    def KT_ap(p, r0, r1, c0, c1):
        o = A_U + p * USZ + U_KT
        return ARENA[r0:r1, o + c0:o + c1]

    def V_ap(p, r0, r1, t, c0, c1):
        o = A_U + p * USZ + U_V + t * VW
        return ARENA[r0:r1, o + c0:o + c1]

    def Vall_ap(p, t0, t1):
        o = A_U + p * USZ + U_V
        return ARENA[:, o + t0 * VW:o + t1 * VW].rearrange("p (t w) -> p t w", w=VW)

    def PT_ap(s, i, r0, r1, c0, c1):
        o = A_PT + (s * 2 + i) * 512
        return ARENA[r0:r1, o + c0:o + c1]

    def TMP_ap(s, i, r0, r1, c0, c1):
        o = F_TMP + (s * 2 + i) * 256
        return FA[r0:r1, o + c0:o + c1]

    def ktb(p, col):
        return KTB[p][min(col // 512, 4)] if col < NTOK else KTB[p][5]

    def advance(bg):
        if bg is None:
            return False
        try:
            next(bg)
            return True
        except StopIteration:
            return False

    def attention(u, qp, kp, dv, groups, finalize, bg):
        dvp = dv + 1
        step_list = []
        for gi, g in enumerate(groups):
            q = g["q"]
            merged, cur, w = [], [], 0
            for st in g["steps"]:
                (kcol, nk, vt, lo, hi, classes) = st
                wd = q[hi][0] + q[hi][1] - q[lo][0]
                if cur and w + wd > 512:
                    merged.append(cur)
                    cur, w = [], 0
                cur.append(st + (w,))
                w += wd
            if cur:
                merged.append(cur)
            for si, subs in enumerate(merged):
                step_list.append((gi, si == len(merged) - 1, subs))
        last_for = {}
        for n, (gi, lastg, subs) in enumerate(step_list):
            for sub in subs:
                for qi in range(sub[3], sub[4] + 1):
                    last_for[(gi, qi)] = (n, sub[0])
        bank_started = {}
        pending = []
        cur_n = [0]

        def acc_ap(gi, s, qi, R):
            G = len(groups[gi]["q"])
            a = s * G + qi
            bank = 4 + a // 3
            off = (a % 3) * dvp
            return bank, PS[bank][0:R, off:off + dvp], a

        def emit_qk(n):
            gi, lastg, subs = step_list[n]
            q = groups[gi]["q"]
            for s in range(2):
                bank = 2 * (n % 2) + s
                for (kcol, nk, vt, lo, hi, classes, coff) in subs:
                    c0 = q[lo][0]
                    c1 = q[hi][0] + q[hi][1]
                    k.op(PE, lambda: nc.tensor.matmul(PS[bank][0:nk, coff:coff + c1 - c0], KT_ap(kp, 64 * s, 64 * s + 64, kcol, kcol + nk),
                                                      QT_ap(qp, 64 * s, 64 * s + 64, c0, c1), start=True, stop=True, skip_group_check=True),
                         [ktb(kp, kcol), QTB[qp]], [PB[bank]])

        def emit_exp(n):
            gi, lastg, subs = step_list[n]
            q = groups[gi]["q"]
            segs = []
            for (kcol, nk, vt, lo, hi, classes, coff) in subs:
                c0 = q[lo][0]
                nsp = 0
                while nsp < len(classes) and classes[nsp] != "F":
                    nsp += 1
                wsp = sum(q[lo + i][1] for i in range(nsp))
                wtot = q[hi][0] + q[hi][1] - c0
                if nsp > 0:
                    segs.append(["S", nk, coff, wsp, 0 if classes[0] == "D" else 128])
                if wtot > wsp:
                    if segs and segs[-1][0] == "F" and segs[-1][1] == nk and segs[-1][2] + segs[-1][3] == coff + wsp:
                        segs[-1][3] += wtot - wsp
                    else:
                        segs.append(["F", nk, coff + wsp, wtot - wsp])
            i2 = n % 2
            for s in range(2):
                mp = 2 * u + s
                bank = 2 * (n % 2) + s
                for sg in segs:
                    nk, c, w = sg[1], sg[2], sg[3]
                    k.op(ACT, lambda: nc.scalar.activation(out=PT_ap(s, i2, 0, nk, c, c + w), in_=PS[bank][0:nk, c:c + w], func=AF.Exp,
                                                           bias=cfar[0:nk, mp:mp + 1], scale=1.0), [PB[bank], misc], [PTB[s][i2]])
                for sg in segs:
                    if sg[0] == "S":
                        nk, c, w, ecol = sg[1], sg[2], sg[3], sg[4]
                        k.op(DVE, lambda: nc.vector.tensor_tensor(PT_ap(s, i2, 0, nk, c, c + w), PT_ap(s, i2, 0, nk, c, c + w),
                                                                  E[0:nk, mp, ecol:ecol + w], ALU.mult),
                             [PTB[s][i2], EB], [PTB[s][i2]])

        def emit_pv(n):
            gi, lastg, subs = step_list[n]
            q = groups[gi]["q"]
            i2 = n % 2
            items = [(s, sub, qi) for s in range(2) for sub in subs for qi in range(sub[3], sub[4] + 1)]
            for idx, (s, sub, qi) in enumerate(items):
                (kcol, nk, vt, lo, hi, classes, coff) = sub
                c0 = q[lo][0]
                qc, nq = q[qi]
                bank, oap, a = acc_ap(gi, s, qi, nq)
                first = not bank_started.get((gi, bank), False)
                bank_started[(gi, bank)] = True
                k.op(PE, lambda: nc.tensor.matmul(oap, PT_ap(s, i2, 0, nk, coff + qc - c0, coff + qc - c0 + nq), V_ap(kp, 0, nk, vt, 0, dvp),
                                                  start=first, stop=(last_for[(gi, qi)] == (n, kcol)), skip_group_check=True),
                     [PTB[s][i2], VB[kp][vt]], [PB[bank]], signal=(idx == len(items) - 1))
            if lastg:
                for x in sorted(pending, key=lambda x: x[0]):
                    x[1]()
                pending.clear()
                for (dl, fn) in finalize(gi, groups[gi], acc_ap):
                    pending.append((cur_n[0] + dl, fn))

        N = len(step_list)
        emit_qk(0)
        for n in range(N + 1):
            cur_n[0] = n
            due = [x for x in pending if x[0] <= n]
            for x in due:
                pending.remove(x)
                x[1]()
            if n < N:
                emit_exp(n)
            if n >= 1:
                emit_pv(n - 1)
            if n + 1 < N:
                emit_qk(n + 1)
            advance(bg)
        for x in sorted(pending, key=lambda x: x[0]):
            x[1]()
        pending.clear()
        while advance(bg):
            pass

    oa_stride = [129]

    def evac_acc(G, R, dvp):
        nacc = 2 * G
        nb = (nacc + 2) // 3
        oa_stride[0] = dvp
        for b in range(nb):
            na = min(3, nacc - 3 * b)
            w = na * dvp
            k.op(DVE, lambda: nc.vector.tensor_copy(FA[0:R, F_OA + 3 * b * dvp:F_OA + 3 * b * dvp + w], PS[4 + b][0:R, 0:w]),
                 [PB[4 + b]], [OAB])

    def oa_ap(a, R, c0, c1):
        o = F_OA + a * oa_stride[0]
        return FA[0:R, o + c0:o + c1]

    def oa_sums(R, a0, na):
        dvp = oa_stride[0]
        return FA[0:R, F_OA + a0 * dvp:F_OA + (a0 + na) * dvp].rearrange("p (a w) -> p a w", w=dvp)[:, :, dvp - 1:dvp]

    def transposes_to_OT(kk, q, R):
        bank = 7
        pbf = PS[bank][:].bitcast(BF16)
        col = 0
        for qi, (qc, nq) in enumerate(q):
            k.op(PE, lambda: nc.tensor.transpose(pbf[:, col:col + nq], otok[0:nq, qi, :], ident[0:nq, 0:nq]),
                 [otokB[qi], identB], [PB[bank]], signal=(qi == len(q) - 1))
            col += nq
        c0 = q[0][0]
        gq = min(c0 // 512, 4)
        k.op(DVE, lambda: nc.vector.tensor_copy(OT_ap(kk, c0, c0 + col), pbf[:, 0:col]), [PB[bank]], [OTB[gq]])

    opt_n = [0]

    def out_proj_pair(j, after_tile=None):
        rid, s = ring_next()
        for t in range(NT):
            R = tile_rows(t)
            g = min(t // 4, 4)
            for dh in range(2):
                bank = next_bank([0, 1, 2, 3])
                for kk in range(2):
                    k.op(PE, lambda: nc.tensor.matmul(PS[bank][0:R, :], OT_ap(kk, t * 128, t * 128 + R),
                                                      RING[:, s, kk * 1024 + dh * 512:kk * 1024 + dh * 512 + 512],
                                                      start=(kk == 0), stop=(kk == 1)),
                         [OTB[g], ringB[s]], [PB[bank]], signal=(kk == 1))
                if True:
                    k.op(DVE, lambda: nc.vector.tensor_tensor(X[0:R, t, dh * 512:dh * 512 + 512], PS[bank][0:R, :],
                                                              X[0:R, t, dh * 512:dh * 512 + 512], ALU.add),
                         [PB[bank], XB[t]], [XB[t]])
                else:
                    ob = opt_n[0] % 2
                    opt_n[0] += 1
                    tmp = FA[0:R, F_TMP + ob * 512:F_TMP + ob * 512 + 512]
                    k.op(ACT, lambda: nc.scalar.copy(tmp, PS[bank][0:R, :]), [PB[bank]], [OPTB[ob]])
                    k.op(POOL, lambda: nc.gpsimd.tensor_tensor(X[0:R, t, dh * 512:dh * 512 + 512], tmp,
                                                               X[0:R, t, dh * 512:dh * 512 + 512], ALU.add),
                         [OPTB[ob], XB[t]], [XB[t]])
            if after_tile is not None:
                after_tile(t)
        ring_done(rid)

    def q_chunks(rq_s, qp, split=True):
        rq, sq = rq_s
        for g in range(5):
            g0 = g * 512
            g1 = min(g0 + 512, NTOK)
            halves = [(g0, g0 + 256), (g0 + 256, g1)] if (split and g1 - g0 == 512) else [(g0, g1)]
            for (c0, c1) in halves:
                bank = 7
                for c in range(8):
                    k.op(PE, lambda: nc.tensor.matmul(PS[bank][:, 0:c1 - c0], RING[:, sq, c * 128:(c + 1) * 128], hT_ap(c, c0, c1),
                                                      start=(c == 0), stop=(c == 7)),
                         [hTB[g], ringB[sq]], [PB[bank]], signal=(c == 7))
                k.op(DVE, lambda: nc.vector.tensor_scalar(QT_ap(qp, 0, 128, c0, c1), PS[bank][:, 0:c1 - c0], 0.125, None, ALU.mult),
                     [PB[bank]], [QTB[qp]])
                yield
        ring_done(rq)

    def ktrans_chunk(kp, t0, ntl):
        bank2 = 7
        pbf = PS[bank2][:].bitcast(BF16)
        col = 0
        for i in range(ntl):
            Ri = tile_rows(t0 + i)
            k.op(PE, lambda: nc.tensor.transpose(pbf[:, col:col + Ri], ARENA[0:Ri, A_KTOK + i * 128:A_KTOK + i * 128 + 128], ident[0:Ri, 0:Ri]),
                 [KTOKB, identB], [PB[bank2]], signal=(i == ntl - 1))
            col += Ri
        k.op(DVE, lambda: nc.vector.tensor_copy(KT_ap(kp, 0, 128, t0 * 128, t0 * 128 + col), pbf[:, 0:col]),
             [PB[bank2]], [KTB[kp][min(t0 // 4, 4)]])

    def proj_A(li, j, h):
        par = h % 2
        rq_s = ring_next()
        rkv, skv = ring_next()
        k.dma(POOL, ARENA[:, A_CK:A_CK + 1024].rearrange("p (t f) -> p t f", t=8),
              cache_a_k[j, :, h, :].rearrange("(t p) f -> p t f", p=128), [], [CKB], ck_sem)
        k.dma(POOL, Vall_ap(par, 17, 25)[:, :, 0:128], cache_a_v[j, :, h, :].rearrange("(t p) f -> p t f", p=128),
              [], VB[par][17:25], cv_sem[par])
        yield from q_chunks(rq_s, par)
        for t in range(NT):
            R = tile_rows(t)
            g = min(t // 4, 4)
            bank = 7
            for c in range(8):
                k.op(PE, lambda: nc.tensor.matmul(PS[bank][0:R, 0:256], hT_ap(c, t * 128, t * 128 + R), RING[:, skv, c * 256:(c + 1) * 256],
                                                  start=(c == 0), stop=(c == 7)), [hTB[g], ringB[skv]], [PB[bank]], signal=(c == 7))
            sg = stg_n[0] % NSTG
            stg_n[0] += 1
            stg = FA[0:R, F_STG + sg * 256:F_STG + sg * 256 + 256]
            k.op(DVE, lambda: nc.vector.tensor_copy(stg, PS[bank][0:R, 0:256]), [PB[bank]], [STGB[sg]])
            i4 = t % 4
            k.op(POOL, lambda: nc.gpsimd.tensor_copy(ARENA[0:R, A_KTOK + i4 * 128:A_KTOK + i4 * 128 + 128], stg[:, 0:128]),
                 [STGB[sg]], [KTOKB])
            k.op(POOL, lambda: nc.gpsimd.tensor_copy(V_ap(par, 0, R, t, 0, 128), stg[:, 128:256]), [STGB[sg]], [VB[par][t]])
            if t < 16:
                ko = a_k_prompt[j, t * 128:(t + 1) * 128, h, :]
                vo = a_v_prompt[j, t * 128:(t + 1) * 128, h, :]
            else:
                ko = a_k_sample[j, :, h, :]
                vo = a_v_sample[j, :, h, :]
            k.dma(SP, ko, stg[:, 0:128], [STGB[sg]], [], stg_sem[sg])
            k.dma(SP, vo, stg[:, 128:256], [STGB[sg]], [], stg_sem[sg])
            yield
            if i4 == 3 or t == NT - 1:
                ktrans_chunk(par, t - i4, i4 + 1)
                yield
        ring_done(rkv)
        for half in range(2):
            bank2 = 7
            pbf = PS[bank2][:].bitcast(BF16)
            for i in range(4):
                tt = half * 4 + i
                k.op(PE, lambda: nc.tensor.transpose(pbf[:, i * 128:(i + 1) * 128], ARENA[:, A_CK + tt * 128:A_CK + tt * 128 + 128], ident[:, :]),
                     [CKB, identB], [PB[bank2]], signal=(i == 3))
            k.op(DVE, lambda: nc.vector.tensor_copy(KT_ap(par, 0, 128, NTOK + half * 512, NTOK + half * 512 + 512), pbf[:, 0:512]),
                 [PB[bank2]], [KTB[par][5]])
            yield

    def proj_B(li, j, u):
        qp = u % 2
        n = u // 4
        kp = n % 2
        rq_s = ring_next()
        yield from q_chunks(rq_s, qp, split=False)
        if u % 4 != 0:
            return
        rkv, skv = ring_next()
        for dd in range(2):
            k.dma(POOL, ARENA[:, A_CK + dd * 64:A_CK + dd * 64 + 64], cache_b_k[j, :, n, :], [], [CKB], ck_sem)
        k.dma(POOL, V_ap(kp, 0, 128, 17, 0, 64), cache_b_v[j, :, n, :], [], [VB[kp][17]], cv_sem[kp])
        for t in range(NT):
            R = tile_rows(t)
            g = min(t // 4, 4)
            bank = 7
            for c in range(8):
                k.op(PE, lambda: nc.tensor.matmul(PS[bank][0:R, 0:128], hT_ap(c, t * 128, t * 128 + R), RING[:, skv, c * 128:(c + 1) * 128],
                                                  start=(c == 0), stop=(c == 7)), [hTB[g], ringB[skv]], [PB[bank]], signal=(c == 7))
            sg = stg_n[0] % NSTG
            stg_n[0] += 1
            stg = FA[0:R, F_STG + sg * 256:F_STG + sg * 256 + 128]
            k.op(DVE, lambda: nc.vector.tensor_copy(stg, PS[bank][0:R, 0:128]), [PB[bank]], [STGB[sg]])
            i4 = t % 4
            for dd in range(2):
                k.op(POOL, lambda: nc.gpsimd.tensor_copy(ARENA[0:R, A_KTOK + i4 * 128 + dd * 64:A_KTOK + i4 * 128 + dd * 64 + 64], stg[:, 0:64]),
                     [STGB[sg]], [KTOKB])
            k.op(POOL, lambda: nc.gpsimd.tensor_copy(V_ap(kp, 0, R, t, 0, 64), stg[:, 64:128]), [STGB[sg]], [VB[kp][t]])
            if t >= 15:
                if t == 15:
                    ko, vo = b_k_prompt[j, :, n, :], b_v_prompt[j, :, n, :]
                else:
                    ko, vo = b_k_sample[j, 64:128, n, :], b_v_sample[j, 64:128, n, :]
                k.dma(SP, ko, stg[:, 0:64], [STGB[sg]], [], stg_sem[sg])
                k.dma(SP, vo, stg[:, 64:128], [STGB[sg]], [], stg_sem[sg])
            yield
            if i4 == 3 or t == NT - 1:
                ktrans_chunk(kp, t - i4, i4 + 1)
                yield
        ring_done(rkv)
        bank2 = 7
        pbf = PS[bank2][:].bitcast(BF16)
        k.op(PE, lambda: nc.tensor.transpose(pbf[:, 0:128], ARENA[:, A_CK:A_CK + 128], ident[:, :]), [CKB, identB], [PB[bank2]])
        k.op(DVE, lambda: nc.vector.tensor_copy(KT_ap(kp, 0, 128, NTOK, NTOK + 128), pbf[:, 0:128]), [PB[bank2]], [KTB[kp][5]])
        yield

    def drain(gen):
        for _ in gen:
            pass

    def layer_A(li, j):
        lam_init = 0.8 - 0.6 * math.exp(-0.3 * li)
        lamB = Buf("lam")
        k.op(DVE, lambda: nc.vector.tensor_tensor(lamb[:, j, 0:64], lamb[:, j, 0:64], lamb[:, j, 64:128], ALU.mult), [misc], [lamB])
        k.op(DVE, lambda: nc.vector.tensor_tensor(lamb[:, j, 128:192], lamb[:, j, 128:192], lamb[:, j, 192:256], ALU.mult), [lamB], [lamB])
        k.op(DVE, lambda: nc.vector.reduce_sum(lamb[:, j, 256:257], lamb[:, j, 0:64], mybir.AxisListType.X), [lamB], [lamB])
        k.op(DVE, lambda: nc.vector.reduce_sum(lamb[:, j, 257:258], lamb[:, j, 128:192], mybir.AxisListType.X), [lamB], [lamB])
        k.op(ACT, lambda: nc.scalar.activation(out=lamb[:, j, 256:258], in_=lamb[:, j, 256:258], func=AF.Exp), [lamB], [lamB])
        k.op(DVE, lambda: nc.vector.tensor_tensor(lamb[:, j, 258:259], lamb[:, j, 256:257], lamb[:, j, 257:258], ALU.subtract), [lamB], [lamB])
        k.op(DVE, lambda: nc.vector.tensor_scalar(lamb[:, j, 258:259], lamb[:, j, 258:259], lam_init, None, ALU.add), [lamB], [lamB])
        k.op(DVE, lambda: nc.vector.tensor_scalar(gsub[:, j, :], gsub[:, j, :], 1.0 - lam_init, None, ALU.mult), [misc], [lamB])
        lam_ap = lamb[:, j, 258:259]
        if li > 0:
            k.op(DVE, lambda: nc.vector.tensor_copy(E[0:64, :, 192:256], QS[:]), [QSB], [EB])

        k.mark(f"A{li} norm")
        arena_barrier(unit_bufs, hnB + [junkB])
        evac_mode[0] = "alt"
        norm_to_hT(li, list(range(NT)), hT_ap, lambda t0: hTB[min(t0 // 4, 4)], [0, 1, 2, 3])
        arena_barrier(hnB + [junkB], unit_bufs)
        evac_mode[0] = "dve"
        k.op(DVE, lambda: nc.vector.memset(Vall_ap(0, 0, NVT)[:, :, 128:129], 1.0), [], VB[0])

        groups = []
        for g in range(4):
            q = [((4 * g + i) * 128, 128) for i in range(4)]
            steps = []
            for kt in range(4 * g + 4):
                lo = max(kt, 4 * g) - 4 * g
                cls = []
                for qt in range(4 * g + lo, 4 * g + 4):
                    cls.append("D" if qt == kt else ("P" if qt == kt + 1 else "F"))
                steps.append((kt * 128, 128, kt, lo, 3, cls))
            groups.append(dict(q=q, steps=steps))
        steps = []
        for i in range(8):
            steps.append((NTOK + 128 * i, 128, 17 + i, 0, 0, ["P" if i == 7 else "F"]))
        steps.append((2048, 64, 16, 0, 0, ["D"]))
        groups.append(dict(q=[(2048, 64)], steps=steps))

        k.mark(f"A{li} proj0")
        drain(proj_A(li, j, 0))
        finish_E()
        k.op(DVE, lambda: nc.vector.memset(Vall_ap(1, 0, NVT)[:, :, 128:129], 1.0), [], VB[1])
        for h in range(8):
            def finalize(gi, g, acc_ap, h=h):
                q = g["q"]
                G = len(q)
                R = q[0][1]
                evac_acc(G, R, 129)
                nacc = 2 * G

                def s2():
                    oa_stride[0] = 129
                    k.op(DVE, lambda: nc.vector.reciprocal(stats[0:R, 48:48 + nacc].rearrange("p (a o) -> p a o", o=1), oa_sums(R, 0, nacc)),
                         [OAB], [finB])
                    k.op(DVE, lambda: nc.vector.tensor_scalar(stats[0:R, 48 + G:48 + 2 * G], stats[0:R, 48 + G:48 + 2 * G], lam_ap[0:R, :], None, ALU.mult),
                         [finB, lamB], [finB])
                    k.op(DVE, lambda: nc.vector.memset(stats[0:R, 64:64 + G], 0.0), [], [finB])

                def s2q(qi):
                    def f():
                        oa_stride[0] = 129
                        k.op(DVE, lambda: nc.vector.tensor_scalar(oa_ap(G + qi, R, 0, 128), oa_ap(G + qi, R, 0, 128),
                                                                  stats[0:R, 48 + G + qi:49 + G + qi], None, ALU.mult),
                             [OAB, finB], [OAB])
                        k.op(DVE, lambda: nc.vector.scalar_tensor_tensor(
                            FA[0:R, F_O + qi * 128:F_O + qi * 128 + 128], oa_ap(qi, R, 0, 128), stats[0:R, 48 + qi:49 + qi],
                            oa_ap(G + qi, R, 0, 128), ALU.mult, ALU.subtract), [OAB, finB], [OB[qi]])
                    return f

                def s3():
                    for qi in range(G):
                        k.op(ACT, lambda: nc.scalar.activation(out=junk2[0:R, :], in_=FA[0:R, F_O + qi * 128:F_O + qi * 128 + 128],
                                                               func=AF.Square, accum_out=stats[0:R, 64 + qi:65 + qi]),
                             [OB[qi], finB], [junk2B, finB])
                    rstd_from_ss(stats[0:R, 64:64 + G], stats[0:R, 72:72 + G], 128, [finB])

                def s4q(qi):
                    def f():
                        k.op(DVE, lambda: nc.vector.scalar_tensor_tensor(
                            otok[0:R, qi, :], FA[0:R, F_O + qi * 128:F_O + qi * 128 + 128], stats[0:R, 72 + qi:73 + qi],
                            gsub[0:R, j, :], ALU.mult, ALU.mult), [OB[qi], finB, lamB], [otokB[qi]])
                    return f

                def s5():
                    transposes_to_OT(h % 2, q, R)
                if G == 1:
                    return [(1, s2), (1, s2q(0)), (2, s3), (3, s4q(0)), (4, s5)]
                return [(1, s2), (1, s2q(0)), (2, s2q(1)), (2, s2q(2)), (3, s2q(3)), (4, s3),
                        (5, s4q(0)), (5, s4q(1)), (6, s4q(2)), (6, s4q(3)), (7, s5)]

            k.mark(f"A{li} u{h} attn")
            bg = proj_A(li, j, h + 1) if h + 1 < 8 else None
            attention(h, h % 2, h % 2, 128, groups, finalize, bg)
            if h % 2 == 1:
                k.mark(f"A{li} u{h} oproj")
                out_proj_pair(j, emit_sq if h == 7 else None)

    def layer_B(li, j):
        k.op(DVE, lambda: nc.vector.memset(E[0:64, :, 192:256], 0.0), [], [EB])
        k.mark(f"B{li} norm")
        arena_barrier(unit_bufs, hnB + [junkB])
        evac_mode[0] = "alt"
        norm_to_hT(li, list(range(NT)), hT_ap, lambda t0: hTB[min(t0 // 4, 4)], [0, 1, 2, 3])
        arena_barrier(hnB + [junkB], unit_bufs)
        evac_mode[0] = "dve"
        for p in range(2):
            k.op(DVE, lambda: nc.vector.memset(Vall_ap(p, 0, NVT)[:, :, 64:65], 1.0), [], VB[p])
        groups = []
        for g in range(4):
            q = [((4 * g + i) * 128, 128) for i in range(4)]
            steps = []
            for kt in range(max(4 * g - 1, 0), 4 * g + 4):
                lo = max(kt, 4 * g)
                hi = min(kt + 1, 4 * g + 3)
                cls = ["D" if qt == kt else "P" for qt in range(lo, hi + 1)]
                steps.append((kt * 128, 128, kt, lo - 4 * g, hi - 4 * g, cls))
            groups.append(dict(q=q, steps=steps))
        groups.append(dict(q=[(2048, 64)], steps=[(NTOK, 128, 17, 0, 0, ["P"]), (2048, 64, 16, 0, 0, ["D"])]))

        k.mark(f"B{li} proj0")
        drain(proj_B(li, j, 0))
        for u in range(8):
            def finalize(gi, g, acc_ap, u=u):
                q = g["q"]
                G = len(q)
                R = q[0][1]
                evac_acc(G, R, 65)

                def s2():
                    oa_stride[0] = 65
                    for s in range(2):
                        hd = 2 * u + s
                        k.op(DVE, lambda: nc.vector.tensor_scalar(stats[0:R, 48 + s * G:48 + (s + 1) * G].rearrange("p (a o) -> p a o", o=1),
                                                                  oa_sums(R, s * G, G), esink[0:R, j, hd:hd + 1], None, ALU.add),
                             [OAB, misc], [finB])
                    k.op(DVE, lambda: nc.vector.reciprocal(stats[0:R, 48:48 + 2 * G], stats[0:R, 48:48 + 2 * G]), [finB], [finB])
                    for qi in range(G):
                        for s in range(2):
                            a = s * G + qi
                            k.op(DVE, lambda: nc.vector.tensor_scalar(otok[0:R, qi, s * 64:s * 64 + 64], oa_ap(a, R, 0, 64),
                                                                      stats[0:R, 48 + a:49 + a], None, ALU.mult),
                                 [OAB, finB], [otokB[qi]])

                def s5():
                    transposes_to_OT(u % 2, q, R)
                return [(1, s2), (3, s5)]

            k.mark(f"B{li} u{u} attn")
            bg = proj_B(li, j, u + 1) if u + 1 < 8 else None
            attention(u, u % 2, (u // 4) % 2, 64, groups, finalize, bg)
            if u % 2 == 1:
                k.mark(f"B{li} u{u} oproj")
                out_proj_pair(j, emit_sq if u == 7 else None)

    mhTB = Buf("mhT")
    ATB = Buf("AT")
    RELB = [Buf("rel0"), Buf("rel1")]

    def mlp(li):
        evac_mode[0] = "alt"
        arena_barrier(hTB + OTB, [mhTB, ATB])
        arena_barrier(unit_bufs, hnB + [junkB])
        arena_barrier(TMPB[0] + TMPB[1] + OPTB + fa_users, RELB)
        for tg in range(2):
            tiles = list(range(8)) if tg == 0 else list(range(8, NT))
            tok0 = tiles[0] * 128
            ntok = sum(tile_rows(t) for t in tiles)
            mov = [(0, 512), (512, 1024)] + ([(1024, 1088)] if tg == 1 else [])

            def dst_fn(c, c0, c1):
                return ARENA[:, M_hT + c * 1088 + (c0 - tok0):M_hT + c * 1088 + (c1 - tok0)]
            k.mark(f"M{li} tg{tg} norm")
            norm_to_hT(4 + li, tiles, dst_fn, lambda t0: mhTB, [0, 1, 2, 3])
            for fq in range(4):
                k.mark(f"M{li} tg{tg} fq{fq} up")
                ups = [ring_next() for cc in range(4)]
                for fc in range(8):
                    s = ups[fc // 2][1]
                    for (m0, m1) in mov:
                        bank = next_bank([0, 1, 2, 3])
                        for c in range(8):
                            k.op(PE, lambda c=c, bank=bank, s=s, fc=fc, m0=m0, m1=m1: nc.tensor.matmul(
                                PS[bank][:, 0:m1 - m0], RING[:, s, c * 256 + (fc % 2) * 128:c * 256 + (fc % 2) * 128 + 128],
                                ARENA[:, M_hT + c * 1088 + m0:M_hT + c * 1088 + m1], start=(c == 0), stop=(c == 7)),
                                [mhTB, ringB[s]], [PB[bank]], signal=(c == 7))
                        rb = evac_rr[0] % 2
                        evac_rr[0] += 1
                        rel = FA[:, rb * 512:rb * 512 + (m1 - m0)]
                        k.op(ACT, lambda bank=bank, rel=rel, m0=m0, m1=m1: nc.scalar.activation(out=rel, in_=PS[bank][:, 0:m1 - m0], func=AF.Relu),
                             [PB[bank]], [RELB[rb]])
                        k.op(DVE, lambda rel=rel, fc=fc, m0=m0, m1=m1: nc.vector.tensor_tensor(
                            ARENA[:, M_AT + fc * 1088 + m0:M_AT + fc * 1088 + m1], rel, rel, ALU.mult), [RELB[rb]], [ATB])
                    if fc % 2 == 1:
                        ring_done(ups[fc // 2][0])
                k.mark(f"M{li} tg{tg} fq{fq} down")
                downs = [ring_next() for cc in range(4)]
                for t in tiles:
                    R = tile_rows(t)
                    lc = t * 128 - tok0
                    for dh in range(2):
                        bank = next_bank([4, 5, 6, 7])
                        for fc in range(8):
                            s = downs[fc // 2][1]
                            k.op(PE, lambda fc=fc, s=s, bank=bank, R=R, lc=lc, dh=dh: nc.tensor.matmul(
                                PS[bank][0:R, :], ARENA[:, M_AT + fc * 1088 + lc:M_AT + fc * 1088 + lc + R],
                                RING[:, s, (fc % 2) * 1024 + dh * 512:(fc % 2) * 1024 + dh * 512 + 512],
                                start=(fc == 0), stop=(fc == 7)), [ATB, ringB[s]], [PB[bank]], signal=(fc == 7))
                        k.op(DVE, lambda bank=bank, t=t, R=R, dh=dh: nc.vector.tensor_tensor(
                            X[0:R, t, dh * 512:dh * 512 + 512], PS[bank][0:R, :], X[0:R, t, dh * 512:dh * 512 + 512], ALU.add),
                            [PB[bank], XB[t]], [XB[t]])
                    if fq == 3:
                        emit_sq(t)
                for cc in range(4):
                    ring_done(downs[cc][0])
        arena_barrier([mhTB, ATB], hTB + OTB)
        arena_barrier(hnB + [junkB], unit_bufs)
        arena_barrier(RELB, TMPB[0] + TMPB[1] + OPTB)

    def final_norm():
        gB = Buf("gfin")
        allfa = [OAB, T1B] + OB + STGB + TMPB[0] + TMPB[1] + RELB + OPTB
        arena_barrier(allfa, [gB])
        arena_barrier(unit_bufs, [junkB])
        k.dma(SP, FA[:, 0:1024], final_norm_g.ap().partition_broadcast(128), [], [gB], k.semslot("gfin"))
        yB = [Buf("y0"), Buf("y1")]
        arena_barrier(allfa, yB)
        ysem = [k.semslot("y0"), k.semslot("y1")]
        k.store_sems += ysem
        for t in range(NT):
            if t not in pre_sq:
                emit_sq(t)
        rstd_from_ss(stats[:, 0:NT], stats[:, 17:17 + NT], D, [ssB])
        for t in range(NT):
            R = tile_rows(t)
            yb = t % 2
            ya = FA[0:R, 1024 + yb * 1024:2048 + yb * 1024]
            k.op(DVE, lambda: nc.vector.scalar_tensor_tensor(ya, X[0:R, t, :], stats[0:R, 17 + t:18 + t], FA[0:R, 0:1024],
                                                             ALU.mult, ALU.mult), [XB[t], ssB, gB], [yB[yb]])
            dst = y_prompt[t * 128:(t + 1) * 128, :] if t < 16 else y_sample
            k.dma(SP, dst, ya, [yB[yb]], [], ysem[yb])

    layer_A(0, 0)
    mlp(0)
    layer_B(1, 0)
    mlp(1)
    layer_A(2, 1)
    mlp(2)
    layer_B(3, 1)
    mlp(3)
    k.mark("final")
    final_norm()
    k.mark("end")
    for ss in k.store_sems:
        if ss[1] > 0:
            nc.sync.wait_ge(ss[0], ss[1])


_CACHE = {}


def _consts():
    import jax
    import jax.numpy as jnp
    cpu = jax.devices("cpu")[0]
    with jax.default_device(cpu):
        rel = jnp.asarray(127 - np.arange(383), dtype=jnp.int32)
        nb = 16
        n = -rel
        ret = jnp.where(n < 0, nb, 0)
        n = jnp.abs(n)
        max_exact = nb // 2
        nf = jnp.maximum(n, 1).astype(jnp.float32)
        large = max_exact + (jnp.log(nf / max_exact) / math.log(128 / max_exact) * (nb - max_exact)).astype(jnp.int32)
        large = jnp.minimum(large, nb - 1)
        bucket = np.asarray(ret + jnp.where(n < max_exact, n, large))
    oh = np.zeros((32, 383), np.float32)
    oh[bucket, np.arange(383)] = 1.0
    ident = np.eye(128, dtype=np.float32)
    return oh, ident


def kernel(**inputs):
    inp = {k_: np.asarray(v) for k_, v in inputs.items()}
    if "nc" not in _CACHE:
        _CACHE["nc"] = build_program()
        _CACHE["consts"] = _consts()
    nc = _CACHE["nc"]
    oh, ident = _CACHE["consts"]
    f = lambda a: np.ascontiguousarray(a, dtype=np.float32)
    shared = {
        "rel_table": f(inp["rel_table"]), "norm_mix_g": f(inp["norm_mix_g"]), "norm_mlp_g": f(inp["norm_mlp_g"]),
        "final_norm_g": f(inp["final_norm_g"]), "a_w_qkv": f(inp["a_w_qkv"]),
        "a_lambda": f(inp["a_lambda"]).reshape(2, 256), "a_subln_g": f(inp["a_subln_g"]), "a_w_o": f(inp["a_w_o"]),
        "b_w_qkv": f(inp["b_w_qkv"]), "b_sinks": f(inp["b_sinks"]), "b_w_o": f(inp["b_w_o"]),
        "mlp_w_up": f(inp["mlp_w_up"]), "mlp_w_down": f(inp["mlp_w_down"]), "c_ohr": oh, "c_ident": ident,
    }
    in_maps = []
    for b in range(8):
        m = dict(shared)
        m["x_prompt"] = f(inp["x_prompt"][b])
        m["x_sample"] = f(inp["x_sample"][b])
        m["cache_a_k"] = f(inp["cache_a_k"][:, b]).reshape(2, PAST, 8, 128)
        m["cache_a_v"] = f(inp["cache_a_v"][:, b])
        m["cache_b_k"] = f(inp["cache_b_k"][:, b])
        m["cache_b_v"] = f(inp["cache_b_v"][:, b])
        in_maps.append(m)
    res = run_bass_kernel_spmd(nc, in_maps, core_ids=list(range(8)))
    R = res.results
    st = lambda name, ax, shp=None: np.stack([np.asarray(R[b][name], dtype=np.float32) for b in range(8)], axis=ax)
    y_prompt = st("y_prompt", 0)
    y_sample = st("y_sample", 0)
    a_k_prompt = st("a_k_prompt", 1).reshape(2, 8, SEQ, 8, 2, 64)
    a_v_prompt = st("a_v_prompt", 1).reshape(2, 8, SEQ, 8, 128)
    b_k_prompt = st("b_k_prompt", 1)
    b_v_prompt = st("b_v_prompt", 1)
    a_k_sample = st("a_k_sample", 1).reshape(2, 8, TS, 8, 2, 64)
    a_v_sample = st("a_v_sample", 1).reshape(2, 8, TS, 8, 128)
    b_k_sample = st("b_k_sample", 1)
    b_v_sample = st("b_v_sample", 1)
    return (y_prompt, y_sample, a_k_prompt, a_v_prompt, b_k_prompt, b_v_prompt,
            a_k_sample, a_v_sample, b_k_sample, b_v_sample)
```

```python
import math
from contextlib import ExitStack
import numpy as np
import concourse.bass as bass
import concourse.mybir as mybir
from concourse.bass_utils import run_bass_kernel_spmd

F32 = mybir.dt.float32
BF16 = mybir.dt.bfloat16
AF = mybir.ActivationFunctionType
ALU = mybir.AluOpType

D = 1024
SEQ = 2048
TS = 64
NTOK = SEQ + TS
NT = 17
PAST = 1024
WIN = 128
DFF = 4096
EPS = 1e-6
NRING = 6
RING_EL = 2048


def tile_rows(t):
    return 128 if t < 16 else 64


class Tok:
    __slots__ = ("sem", "val", "eng")

    def __init__(self, sem, val, eng):
        self.sem, self.val, self.eng = sem, val, eng


class Buf:
    __slots__ = ("name", "w", "r", "excl")

    def __init__(self, name, excl=False):
        self.name, self.w, self.r, self.excl = name, None, {}, excl


class Eng:
    def __init__(self, name, eng, sem, is_pe=False):
        self.name, self.eng, self.sem, self.count = name, eng, sem, 0
        self.seen = {}
        self.is_pe = is_pe

    def wait(self, tok):
        key = id(tok.sem)
        if self.seen.get(key, 0) >= tok.val:
            return
        self.eng.wait_ge(tok.sem, tok.val)
        self.seen[key] = tok.val


class K:
    def __init__(self, nc, es):
        self.nc, self.es = nc, es
        self.pe = Eng("pe", nc.tensor, self.sem("s_pe"), True)
        self.act = Eng("act", nc.scalar, self.sem("s_act"))
        self.dve = Eng("dve", nc.vector, self.sem("s_dve"))
        self.pool = Eng("pool", nc.gpsimd, self.sem("s_pool"))
        self.sp = Eng("sp", nc.sync, self.sem("s_sp"))
        self.nsem = 5
        self.store_sems = []
        self.marks = []
        self.pe_n = 0

    def sem(self, name):
        return self.es.enter_context(self.nc.semaphore(name))

    def sb(self, name, shape, dt):
        return self.es.enter_context(self.nc.sbuf_tensor(name, shape, dt))

    def _deps(self, E, reads, writes, own_sem=None):
        toks = []
        for b in reads:
            if b.w is not None:
                t = b.w
                if not (t.eng is E and E.is_pe):
                    toks.append(t)
            if b.excl:
                for t in b.r.values():
                    if t.eng is not E:
                        toks.append(t)
        for b in writes:
            if b.w is not None and b.w.eng is not E and b.w.sem is not own_sem:
                toks.append(b.w)
            for t in b.r.values():
                if t.eng is not E:
                    toks.append(t)
        for t in toks:
            E.wait(t)

    def _commit(self, tok, reads, writes):
        for b in reads:
            key = id(tok.sem)
            b.r[key] = tok
        for b in writes:
            b.w = tok
            b.r = {}

    def mark(self, label):
        self.marks.append((label, self.pe_n))

    def op(self, E, fn, reads=(), writes=(), signal=True):
        self._deps(E, reads, writes)
        inst = fn()
        if E.is_pe:
            self.pe_n += 1
        if signal:
            E.count += 1
            inst.then_inc(E.sem, 1)
            tok = Tok(E.sem, E.count, E)
        else:
            tok = Tok(E.sem, E.count + 1, E)
        self._commit(tok, reads, writes)
        return tok

    def dma(self, Q, out, in_, reads, writes, semslot, n=1, **kw):
        self._deps(Q, reads, writes, own_sem=semslot[0])
        Q.eng.dma_start(out=out, in_=in_, **kw).then_inc(semslot[0], 16)
        semslot[1] += 16
        tok = Tok(semslot[0], semslot[1], None)
        self._commit(tok, reads, writes)
        return tok

    def semslot(self, name):
        return [self.sem(name), 0]


def build_program():
    nc = bass.Bass("TRN2", target_bir_lowering=False)
    es = ExitStack()
    with es:
        _build(nc, es)
    return nc


_MARKS = []


def _build(nc, es):
    k = K(nc, es)
    _MARKS.clear()
    k.marks = _MARKS
    PE, ACT, DVE, POOL, SP = k.pe, k.act, k.dve, k.pool, k.sp

    def din(name, shape, dt=F32):
        return nc.dram_tensor(name, list(shape), dt, kind="ExternalInput")

    def dout(name, shape, dt=F32):
        return nc.dram_tensor(name, list(shape), dt, kind="ExternalOutput")

    x_prompt = din("x_prompt", [SEQ, D]).ap()
    x_sample = din("x_sample", [TS, D]).ap()
    cache_a_k = din("cache_a_k", [2, PAST, 8, 128]).ap()
    cache_a_v = din("cache_a_v", [2, PAST, 8, 128]).ap()
    cache_b_k = din("cache_b_k", [2, WIN, 2, 64]).ap()
    cache_b_v = din("cache_b_v", [2, WIN, 2, 64]).ap()
    rel_table = din("rel_table", [32, 16])
    norm_mix_g = din("norm_mix_g", [4, D])
    norm_mlp_g = din("norm_mlp_g", [4, D])
    final_norm_g = din("final_norm_g", [D])
    a_w_qkv = din("a_w_qkv", [2, D, 3072]).ap()
    a_lambda = din("a_lambda", [2, 256])
    a_subln_g = din("a_subln_g", [2, 128])
    a_w_o = din("a_w_o", [2, D, D]).ap()
    b_w_qkv = din("b_w_qkv", [2, D, 1280]).ap()
    b_sinks = din("b_sinks", [2, 16])
    b_w_o = din("b_w_o", [2, D, D]).ap()
    mlp_w_up = din("mlp_w_up", [4, D, DFF]).ap()
    mlp_w_down = din("mlp_w_down", [4, DFF, D]).ap()
    ohr_d = din("c_ohr", [32, 383]).ap()
    ident_d = din("c_ident", [128, 128]).ap()

    y_prompt = dout("y_prompt", [SEQ, D]).ap()
    y_sample = dout("y_sample", [TS, D]).ap()
    a_k_prompt = dout("a_k_prompt", [2, SEQ, 8, 128]).ap()
    a_v_prompt = dout("a_v_prompt", [2, SEQ, 8, 128]).ap()
    b_k_prompt = dout("b_k_prompt", [2, WIN, 2, 64]).ap()
    b_v_prompt = dout("b_v_prompt", [2, WIN, 2, 64]).ap()
    a_k_sample = dout("a_k_sample", [2, TS, 8, 128]).ap()
    a_v_sample = dout("a_v_sample", [2, TS, 8, 128]).ap()
    b_k_sample = dout("b_k_sample", [2, WIN, 2, 64]).ap()
    b_v_sample = dout("b_v_sample", [2, WIN, 2, 64]).ap()
    efd = nc.dram_tensor("efd_scratch", [16, 384], F32, kind="Internal")

    X = k.sb("X", [128, NT, D], F32)
    XB = [Buf(f"X{t}") for t in range(NT)]
    E = k.sb("E", [128, 16, 256], BF16)
    EB = Buf("E")
    QS = k.sb("QS", [64, 16, 64], BF16)
    RING = k.sb("RING", [128, NRING, RING_EL], BF16)
    ARENA = k.sb("ARENA", [128, 41856], BF16)
    FA = k.sb("FA", [128, 4224], F32)
    ident = k.sb("ident", [128, 128], BF16)
    gT = k.sb("gT", [128, 9, 8], F32)
    cfar = k.sb("cfar", [128, 16], F32)
    esink = k.sb("esink", [128, 2, 16], F32)
    lamb = k.sb("lamb", [128, 2, 260], F32)
    gsub = k.sb("gsub", [128, 2, 128], F32)
    stats = k.sb("stats", [128, 96], F32)
    otok = k.sb("otok", [128, 4, 128], BF16)
    junk2 = k.sb("junk2", [128, 128], BF16)
    junk1 = k.sb("junk1", [128, D], BF16)
    tbs = FA[0:32, 800:816]
    ohs = FA[0:32, 0:384]
    efs = FA[0:16, 384:768]

    PS = [es.enter_context(nc.psum_tensor(f"ps{i}", [128, 512], F32)) for i in range(8)]
    PB = [Buf(f"ps{i}", excl=True) for i in range(8)]

    A_hT = 0
    A_OT = A_hT + 8 * NTOK
    A_U = A_OT + 2 * NTOK
    KTW = NTOK + PAST
    VW = 132
    NVT = 25
    U_QT = 0
    U_KT = NTOK
    U_V = U_KT + KTW
    USZ = U_V + NVT * VW
    A_PT = A_U + 2 * USZ
    A_KTOK = A_PT + 4 * 512
    A_CK = A_KTOK + 512
    A_END = A_CK + 1024
    assert A_END <= 41856
    A_HN = A_U
    A_JUNK = A_HN + 4096
    M_hT = 0
    M_AT = 8 * 1088
    M_END = M_AT + 8 * 1088
    assert M_END <= A_U

    def hT_ap(c, c0, c1):
        return ARENA[:, A_hT + c * NTOK + c0:A_hT + c * NTOK + c1]

    def OT_ap(kk, c0, c1):
        return ARENA[:, A_OT + kk * NTOK + c0:A_OT + kk * NTOK + c1]

    def hn_ap(i, R):
        return ARENA[0:R, A_HN + i * 1024:A_HN + (i + 1) * 1024]

    def junk_ap(R, w=1024):
        return ARENA[0:R, A_JUNK:A_JUNK + w]

    F_TMP = 0
    F_OA = 1024
    F_STG = 2560
    F_O = 3584
    F_T1 = 4096
    F_END = 4224

    hTB = [Buf(f"hT{g}") for g in range(5)]
    OTB = [Buf(f"OT{g}") for g in range(5)]
    ringB = [Buf(f"ring{i}") for i in range(NRING)]
    ring_sem = [k.semslot(f"ringsem{i}") for i in range(NRING)]

    def _maxtok(a, b):
        if a is None:
            return b
        return a if a.val >= b.val else b

    def arena_barrier(bufs_old, bufs_new):
        for nb in bufs_new:
            for ob in bufs_old:
                if ob.w is not None:
                    nb.r[("w", id(ob.w.sem))] = _maxtok(nb.r.get(("w", id(ob.w.sem))), ob.w)
                for kk, t in ob.r.items():
                    nb.r[("r", id(t.sem))] = _maxtok(nb.r.get(("r", id(t.sem))), t)

    reqs = []
    ring_state = dict(issued=0, consumed=0, released=set())

    def gen_requests():
        def attn_layer(kind, j, w_qkv, w_o):
            def unit_reqs(u):
                if kind == "A":
                    reqs.append([(lambda ra: ra[:, 0:1024].rearrange("p (c f) -> p c f", c=8),
                                  w_qkv[j, :, u * 128:(u + 1) * 128].rearrange("(c p) f -> p c f", p=128))])
                    reqs.append([(lambda ra: ra[:, 0:2048].rearrange("p (c f) -> p c f", c=8)[:, :, 0:128],
                                  w_qkv[j, :, 1024 + u * 128:1024 + (u + 1) * 128].rearrange("(c p) f -> p c f", p=128)),
                                 (lambda ra: ra[:, 0:2048].rearrange("p (c f) -> p c f", c=8)[:, :, 128:256],
                                  w_qkv[j, :, 2048 + u * 128:2048 + (u + 1) * 128].rearrange("(c p) f -> p c f", p=128))])
                else:
                    n = u // 4
                    reqs.append([(lambda ra: ra[:, 0:1024].rearrange("p (c f) -> p c f", c=8),
                                  w_qkv[j, :, u * 128:(u + 1) * 128].rearrange("(c p) f -> p c f", p=128))])
                    if u % 4 == 0:
                        reqs.append([(lambda ra: ra[:, 0:1024].rearrange("p (c f) -> p c f", c=8)[:, :, 0:64],
                                      w_qkv[j, :, 1024 + n * 64:1024 + (n + 1) * 64].rearrange("(c p) f -> p c f", p=128)),
                                     (lambda ra: ra[:, 0:1024].rearrange("p (c f) -> p c f", c=8)[:, :, 64:128],
                                      w_qkv[j, :, 1152 + n * 64:1152 + (n + 1) * 64].rearrange("(c p) f -> p c f", p=128))])
            unit_reqs(0)
            for u in range(8):
                if u + 1 < 8:
                    unit_reqs(u + 1)
                if u % 2 == 1:
                    r0 = (u - 1) * 128
                    reqs.append([(lambda ra: ra[:, 0:2048].rearrange("p (k d) -> p k d", k=2),
                                  w_o[j, r0:r0 + 256, :].rearrange("(k p) d -> p k d", p=128))])

        def mlp_layer(li):
            for tg in range(2):
                for fq in range(4):
                    for cc in range(4):
                        f0 = fq * 1024 + cc * 256
                        reqs.append([(lambda ra: ra[:, 0:2048].rearrange("p (c f) -> p c f", c=8),
                                      mlp_w_up[li, :, f0:f0 + 256].rearrange("(c p) f -> p c f", p=128))])
                    for cc in range(4):
                        r0 = fq * 1024 + cc * 256
                        reqs.append([(lambda ra: ra[:, 0:2048].rearrange("p (k d) -> p k d", k=2),
                                      mlp_w_down[li, r0:r0 + 256, :].rearrange("(k p) d -> p k d", p=128))])
        attn_layer("A", 0, a_w_qkv, a_w_o)
        mlp_layer(0)
        attn_layer("B", 0, b_w_qkv, b_w_o)
        mlp_layer(1)
        attn_layer("A", 1, a_w_qkv, a_w_o)
        mlp_layer(2)
        attn_layer("B", 1, b_w_qkv, b_w_o)
        mlp_layer(3)

    def ring_try_issue():
        while ring_state["issued"] < len(reqs):
            n = ring_state["issued"]
            if n >= NRING and (n - NRING) not in ring_state["released"]:
                break
            s = n % NRING
            for (dfn, src) in reqs[n]:
                k.dma(POOL, dfn(RING[:, s, :]), src, [], [ringB[s]], ring_sem[s])
            ring_state["issued"] += 1

    def ring_next():
        n = ring_state["consumed"]
        ring_state["consumed"] += 1
        ring_try_issue()
        assert n < ring_state["issued"], f"ring chunk {n} not issued (deadlock in request order)"
        return n, n % NRING

    def ring_done(n):
        ring_state["released"].add(n)
        ring_try_issue()

    ld = k.semslot("ld")
    tabB = Buf("tab")
    misc = Buf("misc")
    ssB = Buf("ss")
    tab_sem = k.semslot("tabld")
    k.dma(SP, tbs, rel_table.ap(), [], [tabB], tab_sem)
    k.dma(SP, ohs[:, 0:383], ohr_d, [], [tabB], tab_sem)
    for g4 in range(4):
        k.dma(SP, X[:, 4 * g4:4 * g4 + 4, :],
              x_prompt[512 * g4:512 * g4 + 512, :].rearrange("(t p) d -> p t d", p=128),
              [], [XB[4 * g4 + i] for i in range(4)], k.semslot(f"xl{g4}"))
    k.op(DVE, lambda: nc.vector.memset(X[:, 16, :], 0.0), [], [XB[16]])
    k.dma(SP, X[0:64, 16, :], x_sample, [], [XB[16]], k.semslot("xls"))
    identB = Buf("ident")
    k.dma(POOL, ident[:], ident_d, [], [identB], k.semslot("identld"))
    gen_requests()
    ring_try_issue()
    with nc.allow_non_contiguous_dma(reason="small strided parameter loads"):
        k.dma(SP, gT[:, 0:4, :], norm_mix_g.ap().rearrange("l (c p) -> p l c", p=128), [], [misc], ld)
        k.dma(SP, cfar[:], bass.AP(rel_table, 15 * 16, [[0, 128], [1, 16]]), [], [misc], ld)
        k.dma(SP, esink[:].rearrange("p l h -> p (l h)"), bass.AP(b_sinks, 0, [[0, 128], [1, 32]]), [], [misc], ld)
        k.dma(SP, lamb[:, :, 0:256], bass.AP(a_lambda, 0, [[0, 128], [256, 2], [1, 256]]), [], [misc], ld)
        k.dma(SP, gsub[:], bass.AP(a_subln_g, 0, [[0, 128], [128, 2], [1, 128]]), [], [misc], ld)
        k.dma(SP, gT[:, 4:8, :], norm_mlp_g.ap().rearrange("l (c p) -> p l c", p=128), [], [misc], ld)
        k.dma(SP, gT[:, 8, :], final_norm_g.ap().rearrange("(c p) -> p c", p=128), [], [misc], ld)
    st_misc = k.semslot("st_misc")
    k.store_sems.append(st_misc)
    for j in range(2):
        k.dma(SP, b_k_sample[j, 0:64], cache_b_k[j, 64:128], [], [], st_misc)
        k.dma(SP, b_v_sample[j, 0:64], cache_b_v[j, 64:128], [], [], st_misc)

    k.op(PE, lambda: nc.tensor.matmul(PS[0][0:16, 0:383], tbs, ohs[:, 0:383], start=True, stop=True),
         [tabB], [PB[0]])
    ncf = stats[0:16, 90:91]
    stB = Buf("stB")
    k.op(DVE, lambda: nc.vector.tensor_scalar(ncf, PS[0][0:16, 382:383], -1.0, None, ALU.mult), [PB[0]], [stB])
    efB = Buf("efs")
    k.op(ACT, lambda: nc.scalar.activation(out=efs[:, 0:383], in_=PS[0][0:16, 0:383], func=AF.Exp, bias=ncf, scale=1.0),
         [PB[0], stB], [efB])
    efdB = Buf("efd")
    k.dma(SP, efd.ap()[:, 0:383], efs[:, 0:383], [efB], [efdB], k.semslot("efd_st"))
    esem = k.semslot("eld")
    e32B = Buf("e32")
    E32 = ARENA[:, A_U + USZ:A_U + USZ + 8192].bitcast(F32).rearrange("p (m c) -> p m c", m=16)
    for p in range(128):
        src = bass.AP(efd, 127 - p, [[384, 16], [1, 256]])
        k.dma(SP, E32[p:p + 1, :, :], src, [efdB], [e32B], esem)
    QSB = Buf("QS")
    e_done = [False]

    def finish_E():
        if e_done[0]:
            return
        e_done[0] = True
        k.op(DVE, lambda: nc.vector.tensor_copy(E[:, 0:8, :], E32[:, 0:8, :]), [e32B], [EB])
        k.op(DVE, lambda: nc.vector.tensor_copy(E[:, 8:16, :], E32[:, 8:16, :]), [e32B], [EB])
        arena_barrier([e32B], unit_bufs)
        k.op(DVE, lambda: nc.vector.memset(E[64:128, :, 0:64], 0.0), [EB], [EB])
        k.op(DVE, lambda: nc.vector.tensor_copy(QS[:], E[0:64, :, 192:256]), [EB], [QSB])

    k.op(DVE, lambda: nc.vector.memset(stats[:, 0:34], 0.0), [], [ssB])
    k.op(ACT, lambda: nc.scalar.activation(out=esink[:].rearrange("p l h -> p (l h)"),
                                           in_=esink[:].rearrange("p l h -> p (l h)"), func=AF.Exp), [misc], [misc])
    fa_users = []

    ps_rr = [0]

    def next_bank(cands):
        b = cands[ps_rr[0] % len(cands)]
        ps_rr[0] += 1
        return b

    evac_rr = [0]
    evac_mode = ["alt"]

    def evac_engine():
        if evac_mode[0] == "dve":
            return DVE
        evac_rr[0] += 1
        return ACT if evac_rr[0] % 8 in (0, 2, 3, 5, 6) else DVE

    def rstd_from_ss(ss_ap, out_ap, n, bufs):
        k.op(ACT, lambda: nc.scalar.activation(out=out_ap, in_=ss_ap, func=AF.Ln, scale=1.0 / n, bias=EPS), bufs, bufs)
        k.op(ACT, lambda: nc.scalar.activation(out=out_ap, in_=out_ap, func=AF.Exp, scale=-0.5), bufs, bufs)

    hnB = [Buf(f"hn{i}") for i in range(8)]
    junkB = Buf("junk")
    junk2B = Buf("junk2")

    pre_sq = set()
    junk1B = Buf("junk1")

    def emit_sq(t):
        R = tile_rows(t)
        k.op(ACT, lambda: nc.scalar.activation(out=junk1[0:R, :], in_=X[0:R, t, :], func=AF.Square, accum_out=stats[0:R, t:t + 1]),
             [XB[t], ssB], [junk1B, ssB])
        pre_sq.add(t)

    def norm_to_hT(gl, tiles, dst_fn, dstB_fn, banks):
        nt = len(tiles)
        t_0 = tiles[0]
        for t in tiles:
            if t not in pre_sq:
                emit_sq(t)
        rstd_from_ss(stats[:, t_0:t_0 + nt], stats[:, 17 + t_0:17 + t_0 + nt], D, [ssB])
        ngrp = (nt + 3) // 4
        grp_cols = {}

        def emit_hn(gidx):
            grp = tiles[4 * gidx:4 * gidx + 4]
            col = 0
            cols = []
            for i, t in enumerate(grp):
                R = tile_rows(t)
                hb = (gidx % 2) * 4 + i
                k.op(DVE, lambda: nc.vector.tensor_scalar(hn_ap(hb, R), X[0:R, t, :], stats[0:R, 17 + t:18 + t], None, ALU.mult),
                     [XB[t], ssB], [hnB[hb]])
                cols.append((t, R, hb, col))
                col += R
            grp_cols[gidx] = (cols, col)

        def emit_TE(gidx, after_T):
            cols, W = grp_cols[gidx]
            t0 = cols[0][0]
            c0 = t0 * 128
            for c in range(8):
                bank = next_bank(banks)
                pbf = PS[bank][:].bitcast(BF16)
                for jj, (t, R, hb, cc) in enumerate(cols):
                    k.op(PE, lambda: nc.tensor.transpose(pbf[:, cc:cc + R], hn_ap(hb, R)[:, c * 128:(c + 1) * 128], ident[0:R, 0:R]),
                         [hnB[hb], identB], [PB[bank]], signal=(jj == len(cols) - 1))
                if c == 0 and after_T is not None:
                    after_T()
                Eg = evac_engine()
                if Eg is ACT:
                    k.op(ACT, lambda: nc.scalar.activation(out=dst_fn(c, c0, c0 + W), in_=pbf[:, 0:W], func=AF.Copy,
                                                           scale=gT[:, gl, c:c + 1]), [PB[bank], misc], [dstB_fn(t0)])
                else:
                    k.op(DVE, lambda: nc.vector.tensor_scalar(dst_fn(c, c0, c0 + W), pbf[:, 0:W], gT[:, gl, c:c + 1], None, ALU.mult),
                         [PB[bank], misc], [dstB_fn(t0)])

        emit_hn(0)
        for gidx in range(ngrp):
            emit_TE(gidx, (lambda g=gidx: emit_hn(g + 1)) if gidx + 1 < ngrp else None)
        k.op(DVE, lambda: nc.vector.memset(stats[:, t_0:t_0 + nt], 0.0), [], [ssB])
        for t in tiles:
            pre_sq.discard(t)

    QTB = [Buf("QT0"), Buf("QT1")]
    KTB = [[Buf(f"KT{p}_{g}") for g in range(6)] for p in range(2)]
    VB = [[Buf(f"V{p}_{t}") for t in range(NVT)] for p in range(2)]
    PTB = [[Buf(f"PT{s}{i}") for i in range(2)] for s in range(2)]
    TMPB = [[Buf(f"TMP{s}{i}") for i in range(2)] for s in range(2)]
    OAB = Buf("OA")
    KTOKB = Buf("KTOK")
    CKB = Buf("CK")
    NSTG = 4
    STGB = [Buf(f"stg{i}") for i in range(NSTG)]
    stg_sem = [k.semslot(f"stg{i}") for i in range(NSTG)]
    k.store_sems += stg_sem
    stg_n = [0]
    OB = [Buf(f"O{i}") for i in range(4)]
    T1B = Buf("T1")
    otokB = [Buf(f"otok{i}") for i in range(4)]
    finB = Buf("fin")
    OPTB = [Buf("opt0"), Buf("opt1")]
    arena_barrier(fa_users, [OAB, T1B] + STGB + OB + OPTB)
    ck_sem = k.semslot("ck")
    cv_sem = [k.semslot("cv0"), k.semslot("cv1")]
    unit_bufs = QTB + KTB[0] + KTB[1] + VB[0] + VB[1]

    def QT_ap(p, r0, r1, c0, c1):
        o = A_U + p * USZ + U_QT
        return ARENA[r0:r1, o + c0:o + c1]

    def KT_ap(p, r0, r1, c0, c1):
        o = A_U + p * USZ + U_KT
        return ARENA[r0:r1, o + c0:o + c1]

    def V_ap(p, r0, r1, t, c0, c1):
        o = A_U + p * USZ + U_V + t * VW
        return ARENA[r0:r1, o + c0:o + c1]

    def Vall_ap(p, t0, t1):
        o = A_U + p * USZ + U_V
        return ARENA[:, o + t0 * VW:o + t1 * VW].rearrange("p (t w) -> p t w", w=VW)

    def PT_ap(s, i, r0, r1, c0, c1):
        o = A_PT + (s * 2 + i) * 512
        return ARENA[r0:r1, o + c0:o + c1]

    def TMP_ap(s, i, r0, r1, c0, c1):
        o = F_TMP + (s * 2 + i) * 256
        return FA[r0:r1, o + c0:o + c1]

    def ktb(p, col):
        return KTB[p][min(col // 512, 4)] if col < NTOK else KTB[p][5]

    def advance(bg):
        if bg is None:
            return False
        try:
            next(bg)
            return True
        except StopIteration:
            return False

    def attention(u, qp, kp, dv, groups, finalize, bg):
        dvp = dv + 1
        step_list = []
        for gi, g in enumerate(groups):
            q = g["q"]
            merged, cur, w = [], [], 0
            for st in g["steps"]:
                (kcol, nk, vt, lo, hi, classes) = st
                wd = q[hi][0] + q[hi][1] - q[lo][0]
                if cur and w + wd > 512:
                    merged.append(cur)
                    cur, w = [], 0
                cur.append(st + (w,))
                w += wd
            if cur:
                merged.append(cur)
            for si, subs in enumerate(merged):
                step_list.append((gi, si == len(merged) - 1, subs))
        last_for = {}
        for n, (gi, lastg, subs) in enumerate(step_list):
            for sub in subs:
                for qi in range(sub[3], sub[4] + 1):
                    last_for[(gi, qi)] = (n, sub[0])
        bank_started = {}
        pending = []
        cur_n = [0]

        def acc_ap(gi, s, qi, R):
            G = len(groups[gi]["q"])
            a = s * G + qi
            bank = 4 + a // 3
            off = (a % 3) * dvp
            return bank, PS[bank][0:R, off:off + dvp], a

        def emit_qk(n):
            gi, lastg, subs = step_list[n]
            q = groups[gi]["q"]
            for s in range(2):
                bank = 2 * (n % 2) + s
                for (kcol, nk, vt, lo, hi, classes, coff) in subs:
                    c0 = q[lo][0]
                    c1 = q[hi][0] + q[hi][1]
                    k.op(PE, lambda: nc.tensor.matmul(PS[bank][0:nk, coff:coff + c1 - c0], KT_ap(kp, 64 * s, 64 * s + 64, kcol, kcol + nk),
                                                      QT_ap(qp, 64 * s, 64 * s + 64, c0, c1), start=True, stop=True, skip_group_check=True),
                         [ktb(kp, kcol), QTB[qp]], [PB[bank]])

        def emit_exp(n):
            gi, lastg, subs = step_list[n]
            q = groups[gi]["q"]
            segs = []
            for (kcol, nk, vt, lo, hi, classes, coff) in subs:
                c0 = q[lo][0]
                nsp = 0
                while nsp < len(classes) and classes[nsp] != "F":
                    nsp += 1
                wsp = sum(q[lo + i][1] for i in range(nsp))
                wtot = q[hi][0] + q[hi][1] - c0
                if nsp > 0:
                    segs.append(["S", nk, coff, wsp, 0 if classes[0] == "D" else 128])
                if wtot > wsp:
                    if segs and segs[-1][0] == "F" and segs[-1][1] == nk and segs[-1][2] + segs[-1][3] == coff + wsp:
                        segs[-1][3] += wtot - wsp
                    else:
                        segs.append(["F", nk, coff + wsp, wtot - wsp])
            i2 = n % 2
            for s in range(2):
                mp = 2 * u + s
                bank = 2 * (n % 2) + s
                for sg in segs:
                    nk, c, w = sg[1], sg[2], sg[3]
                    k.op(ACT, lambda: nc.scalar.activation(out=PT_ap(s, i2, 0, nk, c, c + w), in_=PS[bank][0:nk, c:c + w], func=AF.Exp,
                                                           bias=cfar[0:nk, mp:mp + 1], scale=1.0), [PB[bank], misc], [PTB[s][i2]])
                for sg in segs:
                    if sg[0] == "S":
                        nk, c, w, ecol = sg[1], sg[2], sg[3], sg[4]
                        k.op(DVE, lambda: nc.vector.tensor_tensor(PT_ap(s, i2, 0, nk, c, c + w), PT_ap(s, i2, 0, nk, c, c + w),
                                                                  E[0:nk, mp, ecol:ecol + w], ALU.mult),
                             [PTB[s][i2], EB], [PTB[s][i2]])

        def emit_pv(n):
            gi, lastg, subs = step_list[n]
            q = groups[gi]["q"]
            i2 = n % 2
            items = [(s, sub, qi) for s in range(2) for sub in subs for qi in range(sub[3], sub[4] + 1)]
            for idx, (s, sub, qi) in enumerate(items):
                (kcol, nk, vt, lo, hi, classes, coff) = sub
                c0 = q[lo][0]
                qc, nq = q[qi]
                bank, oap, a = acc_ap(gi, s, qi, nq)
                first = not bank_started.get((gi, bank), False)
                bank_started[(gi, bank)] = True
                k.op(PE, lambda: nc.tensor.matmul(oap, PT_ap(s, i2, 0, nk, coff + qc - c0, coff + qc - c0 + nq), V_ap(kp, 0, nk, vt, 0, dvp),
                                                  start=first, stop=(last_for[(gi, qi)] == (n, kcol)), skip_group_check=True),
                     [PTB[s][i2], VB[kp][vt]], [PB[bank]], signal=(idx == len(items) - 1))
            if lastg:
                for x in sorted(pending, key=lambda x: x[0]):
                    x[1]()
                pending.clear()
                for (dl, fn) in finalize(gi, groups[gi], acc_ap):
                    pending.append((cur_n[0] + dl, fn))

        N = len(step_list)
        emit_qk(0)
        for n in range(N + 1):
            cur_n[0] = n
            due = [x for x in pending if x[0] <= n]
            for x in due:
                pending.remove(x)
                x[1]()
            if n < N:
                emit_exp(n)
            if n >= 1:
                emit_pv(n - 1)
            if n + 1 < N:
                emit_qk(n + 1)
            advance(bg)
        for x in sorted(pending, key=lambda x: x[0]):
            x[1]()
        pending.clear()
        while advance(bg):
            pass

    oa_stride = [129]
    b_mode = [False]

    def evac_acc(G, R, dvp):
        nacc = 2 * G
        nb = (nacc + 2) // 3
        oa_stride[0] = dvp
        for b in range(nb):
            na = min(3, nacc - 3 * b)
            w = na * dvp
            if b_mode[0]:
                k.op(ACT, lambda: nc.scalar.copy(FA[0:R, F_OA + 3 * b * dvp:F_OA + 3 * b * dvp + w], PS[4 + b][0:R, 0:w]),
                     [PB[4 + b]], [OAB])
            else:
                k.op(DVE, lambda: nc.vector.tensor_copy(FA[0:R, F_OA + 3 * b * dvp:F_OA + 3 * b * dvp + w], PS[4 + b][0:R, 0:w]),
                     [PB[4 + b]], [OAB])

    def oa_ap(a, R, c0, c1):
        o = F_OA + a * oa_stride[0]
        return FA[0:R, o + c0:o + c1]

    def oa_sums(R, a0, na):
        dvp = oa_stride[0]
        return FA[0:R, F_OA + a0 * dvp:F_OA + (a0 + na) * dvp].rearrange("p (a w) -> p a w", w=dvp)[:, :, dvp - 1:dvp]

    def transposes_to_OT(kk, q, R):
        bank = 7
        pbf = PS[bank][:].bitcast(BF16)
        col = 0
        for qi, (qc, nq) in enumerate(q):
            k.op(PE, lambda: nc.tensor.transpose(pbf[:, col:col + nq], otok[0:nq, qi, :], ident[0:nq, 0:nq]),
                 [otokB[qi], identB], [PB[bank]], signal=(qi == len(q) - 1))
            col += nq
        c0 = q[0][0]
        gq = min(c0 // 512, 4)
        k.op(DVE, lambda: nc.vector.tensor_copy(OT_ap(kk, c0, c0 + col), pbf[:, 0:col]), [PB[bank]], [OTB[gq]])

    opt_n = [0]

    def out_proj_pair(j, after_tile=None):
        rid, s = ring_next()
        for t in range(NT):
            R = tile_rows(t)
            g = min(t // 4, 4)
            for dh in range(2):
                bank = next_bank([0, 1, 2, 3])
                for kk in range(2):
                    k.op(PE, lambda: nc.tensor.matmul(PS[bank][0:R, :], OT_ap(kk, t * 128, t * 128 + R),
                                                      RING[:, s, kk * 1024 + dh * 512:kk * 1024 + dh * 512 + 512],
                                                      start=(kk == 0), stop=(kk == 1)),
                         [OTB[g], ringB[s]], [PB[bank]], signal=(kk == 1))
                if True:
                    k.op(DVE, lambda: nc.vector.tensor_tensor(X[0:R, t, dh * 512:dh * 512 + 512], PS[bank][0:R, :],
                                                              X[0:R, t, dh * 512:dh * 512 + 512], ALU.add),
                         [PB[bank], XB[t]], [XB[t]])
                else:
                    ob = opt_n[0] % 2
                    opt_n[0] += 1
                    tmp = FA[0:R, F_TMP + ob * 512:F_TMP + ob * 512 + 512]
                    k.op(ACT, lambda: nc.scalar.copy(tmp, PS[bank][0:R, :]), [PB[bank]], [OPTB[ob]])
                    k.op(POOL, lambda: nc.gpsimd.tensor_tensor(X[0:R, t, dh * 512:dh * 512 + 512], tmp,
                                                               X[0:R, t, dh * 512:dh * 512 + 512], ALU.add),
                         [OPTB[ob], XB[t]], [XB[t]])
            if after_tile is not None:
                after_tile(t)
        ring_done(rid)

    def q_chunks(rq_s, qp, split=True):
        rq, sq = rq_s
        for g in range(5):
            g0 = g * 512
            g1 = min(g0 + 512, NTOK)
            halves = [(g0, g0 + 256), (g0 + 256, g1)] if (split and g1 - g0 == 512) else [(g0, g1)]
            for (c0, c1) in halves:
                bank = 7
                for c in range(8):
                    k.op(PE, lambda: nc.tensor.matmul(PS[bank][:, 0:c1 - c0], RING[:, sq, c * 128:(c + 1) * 128], hT_ap(c, c0, c1),
                                                      start=(c == 0), stop=(c == 7)),
                         [hTB[g], ringB[sq]], [PB[bank]], signal=(c == 7))
                if b_mode[0]:
                    k.op(ACT, lambda: nc.scalar.activation(out=QT_ap(qp, 0, 128, c0, c1), in_=PS[bank][:, 0:c1 - c0], func=AF.Copy, scale=0.125),
                         [PB[bank]], [QTB[qp]])
                else:
                    k.op(DVE, lambda: nc.vector.tensor_scalar(QT_ap(qp, 0, 128, c0, c1), PS[bank][:, 0:c1 - c0], 0.125, None, ALU.mult),
                         [PB[bank]], [QTB[qp]])
                yield
        ring_done(rq)

    def ktrans_chunk(kp, t0, ntl):
        bank2 = 7
        pbf = PS[bank2][:].bitcast(BF16)
        col = 0
        for i in range(ntl):
            Ri = tile_rows(t0 + i)
            k.op(PE, lambda: nc.tensor.transpose(pbf[:, col:col + Ri], ARENA[0:Ri, A_KTOK + i * 128:A_KTOK + i * 128 + 128], ident[0:Ri, 0:Ri]),
                 [KTOKB, identB], [PB[bank2]], signal=(i == ntl - 1))
            col += Ri
        k.op(DVE, lambda: nc.vector.tensor_copy(KT_ap(kp, 0, 128, t0 * 128, t0 * 128 + col), pbf[:, 0:col]),
             [PB[bank2]], [KTB[kp][min(t0 // 4, 4)]])

    def proj_A(li, j, h):
        par = h % 2
        rq_s = ring_next()
        rkv, skv = ring_next()
        k.dma(POOL, ARENA[:, A_CK:A_CK + 1024].rearrange("p (t f) -> p t f", t=8),
              cache_a_k[j, :, h, :].rearrange("(t p) f -> p t f", p=128), [], [CKB], ck_sem)
        k.dma(POOL, Vall_ap(par, 17, 25)[:, :, 0:128], cache_a_v[j, :, h, :].rearrange("(t p) f -> p t f", p=128),
              [], VB[par][17:25], cv_sem[par])
        yield from q_chunks(rq_s, par)
        for t in range(NT):
            R = tile_rows(t)
            g = min(t // 4, 4)
            bank = 7
            for c in range(8):
                k.op(PE, lambda: nc.tensor.matmul(PS[bank][0:R, 0:256], hT_ap(c, t * 128, t * 128 + R), RING[:, skv, c * 256:(c + 1) * 256],
                                                  start=(c == 0), stop=(c == 7)), [hTB[g], ringB[skv]], [PB[bank]], signal=(c == 7))
            sg = stg_n[0] % NSTG
            stg_n[0] += 1
            stg = FA[0:R, F_STG + sg * 256:F_STG + sg * 256 + 256]
            k.op(DVE, lambda: nc.vector.tensor_copy(stg, PS[bank][0:R, 0:256]), [PB[bank]], [STGB[sg]])
            i4 = t % 4
            k.op(POOL, lambda: nc.gpsimd.tensor_copy(ARENA[0:R, A_KTOK + i4 * 128:A_KTOK + i4 * 128 + 128], stg[:, 0:128]),
                 [STGB[sg]], [KTOKB])
            k.op(POOL, lambda: nc.gpsimd.tensor_copy(V_ap(par, 0, R, t, 0, 128), stg[:, 128:256]), [STGB[sg]], [VB[par][t]])
            if t < 16:
                ko = a_k_prompt[j, t * 128:(t + 1) * 128, h, :]
                vo = a_v_prompt[j, t * 128:(t + 1) * 128, h, :]
            else:
                ko = a_k_sample[j, :, h, :]
                vo = a_v_sample[j, :, h, :]
            k.dma(SP, ko, stg[:, 0:128], [STGB[sg]], [], stg_sem[sg])
            k.dma(SP, vo, stg[:, 128:256], [STGB[sg]], [], stg_sem[sg])
            yield
            if i4 == 3 or t == NT - 1:
                ktrans_chunk(par, t - i4, i4 + 1)
                yield
        ring_done(rkv)
        for half in range(2):
            bank2 = 7
            pbf = PS[bank2][:].bitcast(BF16)
            for i in range(4):
                tt = half * 4 + i
                k.op(PE, lambda: nc.tensor.transpose(pbf[:, i * 128:(i + 1) * 128], ARENA[:, A_CK + tt * 128:A_CK + tt * 128 + 128], ident[:, :]),
                     [CKB, identB], [PB[bank2]], signal=(i == 3))
            k.op(DVE, lambda: nc.vector.tensor_copy(KT_ap(par, 0, 128, NTOK + half * 512, NTOK + half * 512 + 512), pbf[:, 0:512]),
                 [PB[bank2]], [KTB[par][5]])
            yield

    def proj_B(li, j, u):
        qp = u % 2
        n = u // 4
        kp = n % 2
        rq_s = ring_next()
        yield from q_chunks(rq_s, qp, split=False)
        if u % 4 != 0:
            return
        rkv, skv = ring_next()
        for dd in range(2):
            k.dma(POOL, ARENA[:, A_CK + dd * 64:A_CK + dd * 64 + 64], cache_b_k[j, :, n, :], [], [CKB], ck_sem)
        k.dma(POOL, V_ap(kp, 0, 128, 17, 0, 64), cache_b_v[j, :, n, :], [], [VB[kp][17]], cv_sem[kp])
        for t in range(NT):
            R = tile_rows(t)
            g = min(t // 4, 4)
            bank = 7
            for c in range(8):
                k.op(PE, lambda: nc.tensor.matmul(PS[bank][0:R, 0:128], hT_ap(c, t * 128, t * 128 + R), RING[:, skv, c * 128:(c + 1) * 128],
                                                  start=(c == 0), stop=(c == 7)), [hTB[g], ringB[skv]], [PB[bank]], signal=(c == 7))
            sg = stg_n[0] % NSTG
            stg_n[0] += 1
            stg = FA[0:R, F_STG + sg * 256:F_STG + sg * 256 + 128]
            k.op(ACT, lambda: nc.scalar.copy(stg, PS[bank][0:R, 0:128]), [PB[bank]], [STGB[sg]])
            i4 = t % 4
            for dd in range(2):
                k.op(POOL, lambda: nc.gpsimd.tensor_copy(ARENA[0:R, A_KTOK + i4 * 128 + dd * 64:A_KTOK + i4 * 128 + dd * 64 + 64], stg[:, 0:64]),
                     [STGB[sg]], [KTOKB])
            k.op(POOL, lambda: nc.gpsimd.tensor_copy(V_ap(kp, 0, R, t, 0, 64), stg[:, 64:128]), [STGB[sg]], [VB[kp][t]])
            if t >= 15:
                if t == 15:
                    ko, vo = b_k_prompt[j, :, n, :], b_v_prompt[j, :, n, :]
                else:
                    ko, vo = b_k_sample[j, 64:128, n, :], b_v_sample[j, 64:128, n, :]
                k.dma(SP, ko, stg[:, 0:64], [STGB[sg]], [], stg_sem[sg])
                k.dma(SP, vo, stg[:, 64:128], [STGB[sg]], [], stg_sem[sg])
            yield
            if i4 == 3 or t == NT - 1:
                ktrans_chunk(kp, t - i4, i4 + 1)
                yield
        ring_done(rkv)
        bank2 = 7
        pbf = PS[bank2][:].bitcast(BF16)
        k.op(PE, lambda: nc.tensor.transpose(pbf[:, 0:128], ARENA[:, A_CK:A_CK + 128], ident[:, :]), [CKB, identB], [PB[bank2]])
        k.op(DVE, lambda: nc.vector.tensor_copy(KT_ap(kp, 0, 128, NTOK, NTOK + 128), pbf[:, 0:128]), [PB[bank2]], [KTB[kp][5]])
        yield

    def drain(gen):
        for _ in gen:
            pass

    def layer_A(li, j):
        b_mode[0] = False
        lam_init = 0.8 - 0.6 * math.exp(-0.3 * li)
        lamB = Buf("lam")
        k.op(DVE, lambda: nc.vector.tensor_tensor(lamb[:, j, 0:64], lamb[:, j, 0:64], lamb[:, j, 64:128], ALU.mult), [misc], [lamB])
        k.op(DVE, lambda: nc.vector.tensor_tensor(lamb[:, j, 128:192], lamb[:, j, 128:192], lamb[:, j, 192:256], ALU.mult), [lamB], [lamB])
        k.op(DVE, lambda: nc.vector.reduce_sum(lamb[:, j, 256:257], lamb[:, j, 0:64], mybir.AxisListType.X), [lamB], [lamB])
        k.op(DVE, lambda: nc.vector.reduce_sum(lamb[:, j, 257:258], lamb[:, j, 128:192], mybir.AxisListType.X), [lamB], [lamB])
        k.op(ACT, lambda: nc.scalar.activation(out=lamb[:, j, 256:258], in_=lamb[:, j, 256:258], func=AF.Exp), [lamB], [lamB])
        k.op(DVE, lambda: nc.vector.tensor_tensor(lamb[:, j, 258:259], lamb[:, j, 256:257], lamb[:, j, 257:258], ALU.subtract), [lamB], [lamB])
        k.op(DVE, lambda: nc.vector.tensor_scalar(lamb[:, j, 258:259], lamb[:, j, 258:259], lam_init, None, ALU.add), [lamB], [lamB])
        k.op(DVE, lambda: nc.vector.tensor_scalar(gsub[:, j, :], gsub[:, j, :], 1.0 - lam_init, None, ALU.mult), [misc], [lamB])
        lam_ap = lamb[:, j, 258:259]
        if li > 0:
            k.op(DVE, lambda: nc.vector.tensor_copy(E[0:64, :, 192:256], QS[:]), [QSB], [EB])

        k.mark(f"A{li} norm")
        arena_barrier(unit_bufs, hnB + [junkB])
        evac_mode[0] = "alt"
        norm_to_hT(li, list(range(NT)), hT_ap, lambda t0: hTB[min(t0 // 4, 4)], [0, 1, 2, 3])
        arena_barrier(hnB + [junkB], unit_bufs)
        evac_mode[0] = "dve"
        k.op(DVE, lambda: nc.vector.memset(Vall_ap(0, 0, NVT)[:, :, 128:129], 1.0), [], VB[0])

        groups = []
        for g in range(4):
            q = [((4 * g + i) * 128, 128) for i in range(4)]
            steps = []
            for kt in range(4 * g + 4):
                lo = max(kt, 4 * g) - 4 * g
                cls = []
                for qt in range(4 * g + lo, 4 * g + 4):
                    cls.append("D" if qt == kt else ("P" if qt == kt + 1 else "F"))
                steps.append((kt * 128, 128, kt, lo, 3, cls))
            groups.append(dict(q=q, steps=steps))
        steps = []
        for i in range(8):
            steps.append((NTOK + 128 * i, 128, 17 + i, 0, 0, ["P" if i == 7 else "F"]))
        steps.append((2048, 64, 16, 0, 0, ["D"]))
        groups.append(dict(q=[(2048, 64)], steps=steps))

        k.mark(f"A{li} proj0")
        drain(proj_A(li, j, 0))
        finish_E()
        k.op(DVE, lambda: nc.vector.memset(Vall_ap(1, 0, NVT)[:, :, 128:129], 1.0), [], VB[1])
        for h in range(8):
            def finalize(gi, g, acc_ap, h=h):
                q = g["q"]
                G = len(q)
                R = q[0][1]
                evac_acc(G, R, 129)
                nacc = 2 * G

                def s2():
                    oa_stride[0] = 129
                    k.op(DVE, lambda: nc.vector.reciprocal(stats[0:R, 48:48 + nacc].rearrange("p (a o) -> p a o", o=1), oa_sums(R, 0, nacc)),
                         [OAB], [finB])
                    k.op(DVE, lambda: nc.vector.tensor_scalar(stats[0:R, 48 + G:48 + 2 * G], stats[0:R, 48 + G:48 + 2 * G], lam_ap[0:R, :], None, ALU.mult),
                         [finB, lamB], [finB])
                    k.op(DVE, lambda: nc.vector.memset(stats[0:R, 64:64 + G], 0.0), [], [finB])

                def s2q(qi):
                    def f():
                        oa_stride[0] = 129
                        k.op(DVE, lambda: nc.vector.tensor_scalar(oa_ap(G + qi, R, 0, 128), oa_ap(G + qi, R, 0, 128),
                                                                  stats[0:R, 48 + G + qi:49 + G + qi], None, ALU.mult),
                             [OAB, finB], [OAB])
                        k.op(DVE, lambda: nc.vector.scalar_tensor_tensor(
                            FA[0:R, F_O + qi * 128:F_O + qi * 128 + 128], oa_ap(qi, R, 0, 128), stats[0:R, 48 + qi:49 + qi],
                            oa_ap(G + qi, R, 0, 128), ALU.mult, ALU.subtract), [OAB, finB], [OB[qi]])
                    return f

                def s3():
                    for qi in range(G):
                        k.op(ACT, lambda: nc.scalar.activation(out=junk2[0:R, :], in_=FA[0:R, F_O + qi * 128:F_O + qi * 128 + 128],
                                                               func=AF.Square, accum_out=stats[0:R, 64 + qi:65 + qi]),
                             [OB[qi], finB], [junk2B, finB])
                    rstd_from_ss(stats[0:R, 64:64 + G], stats[0:R, 72:72 + G], 128, [finB])

                def s4q(qi):
                    def f():
                        k.op(DVE, lambda: nc.vector.scalar_tensor_tensor(
                            otok[0:R, qi, :], FA[0:R, F_O + qi * 128:F_O + qi * 128 + 128], stats[0:R, 72 + qi:73 + qi],
                            gsub[0:R, j, :], ALU.mult, ALU.mult), [OB[qi], finB, lamB], [otokB[qi]])
                    return f

                def s5():
                    transposes_to_OT(h % 2, q, R)
                if G == 1:
                    return [(1, s2), (1, s2q(0)), (2, s3), (3, s4q(0)), (4, s5)]
                return [(1, s2), (1, s2q(0)), (2, s2q(1)), (2, s2q(2)), (3, s2q(3)), (4, s3),
                        (5, s4q(0)), (5, s4q(1)), (6, s4q(2)), (6, s4q(3)), (7, s5)]

            k.mark(f"A{li} u{h} attn")
            bg = proj_A(li, j, h + 1) if h + 1 < 8 else None
            attention(h, h % 2, h % 2, 128, groups, finalize, bg)
            if h % 2 == 1:
                k.mark(f"A{li} u{h} oproj")
                out_proj_pair(j, emit_sq if h == 7 else None)

    def layer_B(li, j):
        b_mode[0] = True
        k.op(DVE, lambda: nc.vector.memset(E[0:64, :, 192:256], 0.0), [], [EB])
        k.mark(f"B{li} norm")
        arena_barrier(unit_bufs, hnB + [junkB])
        evac_mode[0] = "alt"
        norm_to_hT(li, list(range(NT)), hT_ap, lambda t0: hTB[min(t0 // 4, 4)], [0, 1, 2, 3])
        arena_barrier(hnB + [junkB], unit_bufs)
        evac_mode[0] = "dve"
        for p in range(2):
            k.op(DVE, lambda: nc.vector.memset(Vall_ap(p, 0, NVT)[:, :, 64:65], 1.0), [], VB[p])
        groups = []
        for g in range(4):
            q = [((4 * g + i) * 128, 128) for i in range(4)]
            steps = []
            for kt in range(max(4 * g - 1, 0), 4 * g + 4):
                lo = max(kt, 4 * g)
                hi = min(kt + 1, 4 * g + 3)
                cls = ["D" if qt == kt else "P" for qt in range(lo, hi + 1)]
                steps.append((kt * 128, 128, kt, lo - 4 * g, hi - 4 * g, cls))
            groups.append(dict(q=q, steps=steps))
        groups.append(dict(q=[(2048, 64)], steps=[(NTOK, 128, 17, 0, 0, ["P"]), (2048, 64, 16, 0, 0, ["D"])]))

        k.mark(f"B{li} proj0")
        drain(proj_B(li, j, 0))
        for u in range(8):
            def finalize(gi, g, acc_ap, u=u):
                q = g["q"]
                G = len(q)
                R = q[0][1]
                evac_acc(G, R, 65)

                def s2():
                    oa_stride[0] = 65
                    for s in range(2):
                        hd = 2 * u + s
                        k.op(DVE, lambda: nc.vector.tensor_scalar(stats[0:R, 48 + s * G:48 + (s + 1) * G].rearrange("p (a o) -> p a o", o=1),
                                                                  oa_sums(R, s * G, G), esink[0:R, j, hd:hd + 1], None, ALU.add),
                             [OAB, misc], [finB])
                    k.op(DVE, lambda: nc.vector.reciprocal(stats[0:R, 48:48 + 2 * G], stats[0:R, 48:48 + 2 * G]), [finB], [finB])
                    for qi in range(G):
                        for s in range(2):
                            a = s * G + qi
                            k.op(DVE, lambda: nc.vector.tensor_scalar(otok[0:R, qi, s * 64:s * 64 + 64], oa_ap(a, R, 0, 64),
                                                                      stats[0:R, 48 + a:49 + a], None, ALU.mult),
                                 [OAB, finB], [otokB[qi]])

                def s5():
                    transposes_to_OT(u % 2, q, R)
                return [(1, s2), (3, s5)]

            k.mark(f"B{li} u{u} attn")
            bg = proj_B(li, j, u + 1) if u + 1 < 8 else None
            attention(u, u % 2, (u // 4) % 2, 64, groups, finalize, bg)
            if u % 2 == 1:
                k.mark(f"B{li} u{u} oproj")
                out_proj_pair(j, emit_sq if u == 7 else None)

    mhTB = Buf("mhT")
    ATB = Buf("AT")
    RELB = [Buf("rel0"), Buf("rel1")]

    def mlp(li):
        evac_mode[0] = "alt"
        arena_barrier(hTB + OTB, [mhTB, ATB])
        arena_barrier(unit_bufs, hnB + [junkB])
        arena_barrier(TMPB[0] + TMPB[1] + OPTB + fa_users, RELB)
        for tg in range(2):
            tiles = list(range(8)) if tg == 0 else list(range(8, NT))
            tok0 = tiles[0] * 128
            ntok = sum(tile_rows(t) for t in tiles)
            mov = [(0, 512), (512, 1024)] + ([(1024, 1088)] if tg == 1 else [])

            def dst_fn(c, c0, c1):
                return ARENA[:, M_hT + c * 1088 + (c0 - tok0):M_hT + c * 1088 + (c1 - tok0)]
            k.mark(f"M{li} tg{tg} norm")
            norm_to_hT(4 + li, tiles, dst_fn, lambda t0: mhTB, [0, 1, 2, 3])
            for fq in range(4):
                k.mark(f"M{li} tg{tg} fq{fq} up")
                ups = [ring_next() for cc in range(4)]
                for fc in range(8):
                    s = ups[fc // 2][1]
                    for (m0, m1) in mov:
                        bank = next_bank([0, 1, 2, 3])
                        for c in range(8):
                            k.op(PE, lambda c=c, bank=bank, s=s, fc=fc, m0=m0, m1=m1: nc.tensor.matmul(
                                PS[bank][:, 0:m1 - m0], RING[:, s, c * 256 + (fc % 2) * 128:c * 256 + (fc % 2) * 128 + 128],
                                ARENA[:, M_hT + c * 1088 + m0:M_hT + c * 1088 + m1], start=(c == 0), stop=(c == 7)),
                                [mhTB, ringB[s]], [PB[bank]], signal=(c == 7))
                        rb = evac_rr[0] % 2
                        evac_rr[0] += 1
                        rel = FA[:, rb * 512:rb * 512 + (m1 - m0)]
                        k.op(ACT, lambda bank=bank, rel=rel, m0=m0, m1=m1: nc.scalar.activation(out=rel, in_=PS[bank][:, 0:m1 - m0], func=AF.Relu),
                             [PB[bank]], [RELB[rb]])
                        k.op(DVE, lambda rel=rel, fc=fc, m0=m0, m1=m1: nc.vector.tensor_tensor(
                            ARENA[:, M_AT + fc * 1088 + m0:M_AT + fc * 1088 + m1], rel, rel, ALU.mult), [RELB[rb]], [ATB])
                    if fc % 2 == 1:
                        ring_done(ups[fc // 2][0])
                k.mark(f"M{li} tg{tg} fq{fq} down")
                downs = [ring_next() for cc in range(4)]
                for t in tiles:
                    R = tile_rows(t)
                    lc = t * 128 - tok0
                    for dh in range(2):
                        bank = next_bank([4, 5, 6, 7])
                        for fc in range(8):
                            s = downs[fc // 2][1]
                            k.op(PE, lambda fc=fc, s=s, bank=bank, R=R, lc=lc, dh=dh: nc.tensor.matmul(
                                PS[bank][0:R, :], ARENA[:, M_AT + fc * 1088 + lc:M_AT + fc * 1088 + lc + R],
                                RING[:, s, (fc % 2) * 1024 + dh * 512:(fc % 2) * 1024 + dh * 512 + 512],
                                start=(fc == 0), stop=(fc == 7)), [ATB, ringB[s]], [PB[bank]], signal=(fc == 7))
                        k.op(DVE, lambda bank=bank, t=t, R=R, dh=dh: nc.vector.tensor_tensor(
                            X[0:R, t, dh * 512:dh * 512 + 512], PS[bank][0:R, :], X[0:R, t, dh * 512:dh * 512 + 512], ALU.add),
                            [PB[bank], XB[t]], [XB[t]])
                    if fq == 3:
                        emit_sq(t)
                for cc in range(4):
                    ring_done(downs[cc][0])
        arena_barrier([mhTB, ATB], hTB + OTB)
        arena_barrier(hnB + [junkB], unit_bufs)
        arena_barrier(RELB, TMPB[0] + TMPB[1] + OPTB)

    def final_norm():
        gB = Buf("gfin")
        allfa = [OAB, T1B] + OB + STGB + TMPB[0] + TMPB[1] + RELB + OPTB
        arena_barrier(allfa, [gB])
        arena_barrier(unit_bufs, [junkB])
        k.dma(SP, FA[:, 0:1024], final_norm_g.ap().partition_broadcast(128), [], [gB], k.semslot("gfin"))
        yB = [Buf("y0"), Buf("y1")]
        arena_barrier(allfa, yB)
        ysem = [k.semslot("y0"), k.semslot("y1")]
        k.store_sems += ysem
        for t in range(NT):
            if t not in pre_sq:
                emit_sq(t)
        rstd_from_ss(stats[:, 0:NT], stats[:, 17:17 + NT], D, [ssB])
        for t in range(NT):
            R = tile_rows(t)
            yb = t % 2
            ya = FA[0:R, 1024 + yb * 1024:2048 + yb * 1024]
            k.op(DVE, lambda: nc.vector.scalar_tensor_tensor(ya, X[0:R, t, :], stats[0:R, 17 + t:18 + t], FA[0:R, 0:1024],
                                                             ALU.mult, ALU.mult), [XB[t], ssB, gB], [yB[yb]])
            dst = y_prompt[t * 128:(t + 1) * 128, :] if t < 16 else y_sample
            k.dma(SP, dst, ya, [yB[yb]], [], ysem[yb])

    layer_A(0, 0)
    mlp(0)
    layer_B(1, 0)
    mlp(1)
    layer_A(2, 1)
    mlp(2)
    layer_B(3, 1)
    mlp(3)
    k.mark("final")
    final_norm()
    k.mark("end")
    for ss in k.store_sems:
        if ss[1] > 0:
            nc.sync.wait_ge(ss[0], ss[1])


_CACHE = {}


def _consts():
    import jax
    import jax.numpy as jnp
    cpu = jax.devices("cpu")[0]
    with jax.default_device(cpu):
        rel = jnp.asarray(127 - np.arange(383), dtype=jnp.int32)
        nb = 16
        n = -rel
        ret = jnp.where(n < 0, nb, 0)
        n = jnp.abs(n)
        max_exact = nb // 2
        nf = jnp.maximum(n, 1).astype(jnp.float32)
        large = max_exact + (jnp.log(nf / max_exact) / math.log(128 / max_exact) * (nb - max_exact)).astype(jnp.int32)
        large = jnp.minimum(large, nb - 1)
        bucket = np.asarray(ret + jnp.where(n < max_exact, n, large))
    oh = np.zeros((32, 383), np.float32)
    oh[bucket, np.arange(383)] = 1.0
    ident = np.eye(128, dtype=np.float32)
    return oh, ident


def kernel(**inputs):
    inp = {k_: np.asarray(v) for k_, v in inputs.items()}
    if "nc" not in _CACHE:
        _CACHE["nc"] = build_program()
        _CACHE["consts"] = _consts()
    nc = _CACHE["nc"]
    oh, ident = _CACHE["consts"]
    f = lambda a: np.ascontiguousarray(a, dtype=np.float32)
    shared = {
        "rel_table": f(inp["rel_table"]), "norm_mix_g": f(inp["norm_mix_g"]), "norm_mlp_g": f(inp["norm_mlp_g"]),
        "final_norm_g": f(inp["final_norm_g"]), "a_w_qkv": f(inp["a_w_qkv"]),
        "a_lambda": f(inp["a_lambda"]).reshape(2, 256), "a_subln_g": f(inp["a_subln_g"]), "a_w_o": f(inp["a_w_o"]),
        "b_w_qkv": f(inp["b_w_qkv"]), "b_sinks": f(inp["b_sinks"]), "b_w_o": f(inp["b_w_o"]),
        "mlp_w_up": f(inp["mlp_w_up"]), "mlp_w_down": f(inp["mlp_w_down"]), "c_ohr": oh, "c_ident": ident,
    }
    in_maps = []
    for b in range(8):
        m = dict(shared)
        m["x_prompt"] = f(inp["x_prompt"][b])
        m["x_sample"] = f(inp["x_sample"][b])
        m["cache_a_k"] = f(inp["cache_a_k"][:, b]).reshape(2, PAST, 8, 128)
        m["cache_a_v"] = f(inp["cache_a_v"][:, b])
        m["cache_b_k"] = f(inp["cache_b_k"][:, b])
        m["cache_b_v"] = f(inp["cache_b_v"][:, b])
        in_maps.append(m)
    res = run_bass_kernel_spmd(nc, in_maps, core_ids=list(range(8)))
    R = res.results
    st = lambda name, ax, shp=None: np.stack([np.asarray(R[b][name], dtype=np.float32) for b in range(8)], axis=ax)
    y_prompt = st("y_prompt", 0)
    y_sample = st("y_sample", 0)
    a_k_prompt = st("a_k_prompt", 1).reshape(2, 8, SEQ, 8, 2, 64)
    a_v_prompt = st("a_v_prompt", 1).reshape(2, 8, SEQ, 8, 128)
    b_k_prompt = st("b_k_prompt", 1)
    b_v_prompt = st("b_v_prompt", 1)
    a_k_sample = st("a_k_sample", 1).reshape(2, 8, TS, 8, 2, 64)
    a_v_sample = st("a_v_sample", 1).reshape(2, 8, TS, 8, 128)
    b_k_sample = st("b_k_sample", 1)
    b_v_sample = st("b_v_sample", 1)
    return (y_prompt, y_sample, a_k_prompt, a_v_prompt, b_k_prompt, b_v_prompt,
            a_k_sample, a_v_sample, b_k_sample, b_v_sample)
```

```python
import math
from contextlib import ExitStack
import numpy as np
import concourse.bass as bass
import concourse.mybir as mybir
from concourse.bass_utils import run_bass_kernel_spmd

F32 = mybir.dt.float32
BF16 = mybir.dt.bfloat16
AF = mybir.ActivationFunctionType
ALU = mybir.AluOpType

D = 1024
SEQ = 2048
TS = 64
NTOK = SEQ + TS
NT = 17
PAST = 1024
WIN = 128
DFF = 4096
EPS = 1e-6
NRING = 6
RING_EL = 2048


def tile_rows(t):
    return 128 if t < 16 else 64


class Tok:
    __slots__ = ("sem", "val", "eng")

    def __init__(self, sem, val, eng):
        self.sem, self.val, self.eng = sem, val, eng


class Buf:
    __slots__ = ("name", "w", "r", "excl")

    def __init__(self, name, excl=False):
        self.name, self.w, self.r, self.excl = name, None, {}, excl


class Eng:
    def __init__(self, name, eng, sem, is_pe=False):
        self.name, self.eng, self.sem, self.count = name, eng, sem, 0
        self.seen = {}
        self.is_pe = is_pe

    def wait(self, tok):
        key = id(tok.sem)
        if self.seen.get(key, 0) >= tok.val:
            return
        self.eng.wait_ge(tok.sem, tok.val)
        self.seen[key] = tok.val


class K:
    def __init__(self, nc, es):
        self.nc, self.es = nc, es
        self.pe = Eng("pe", nc.tensor, self.sem("s_pe"), True)
        self.act = Eng("act", nc.scalar, self.sem("s_act"))
        self.dve = Eng("dve", nc.vector, self.sem("s_dve"))
        self.pool = Eng("pool", nc.gpsimd, self.sem("s_pool"))
        self.sp = Eng("sp", nc.sync, self.sem("s_sp"))
        self.nsem = 5
        self.store_sems = []
        self.marks = []
        self.pe_n = 0

    def sem(self, name):
        return self.es.enter_context(self.nc.semaphore(name))

    def sb(self, name, shape, dt):
        return self.es.enter_context(self.nc.sbuf_tensor(name, shape, dt))

    def _deps(self, E, reads, writes, own_sem=None):
        toks = []
        for b in reads:
            if b.w is not None:
                t = b.w
                if not (t.eng is E and E.is_pe):
                    toks.append(t)
            if b.excl:
                for t in b.r.values():
                    if t.eng is not E:
                        toks.append(t)
        for b in writes:
            if b.w is not None and b.w.eng is not E and b.w.sem is not own_sem:
                toks.append(b.w)
            for t in b.r.values():
                if t.eng is not E:
                    toks.append(t)
        for t in toks:
            E.wait(t)

    def _commit(self, tok, reads, writes):
        for b in reads:
            key = id(tok.sem)
            b.r[key] = tok
        for b in writes:
            b.w = tok
            b.r = {}

    def mark(self, label):
        self.marks.append((label, self.pe_n))

    def op(self, E, fn, reads=(), writes=(), signal=True):
        self._deps(E, reads, writes)
        inst = fn()
        if E.is_pe:
            self.pe_n += 1
        if signal:
            E.count += 1
            inst.then_inc(E.sem, 1)
            tok = Tok(E.sem, E.count, E)
        else:
            tok = Tok(E.sem, E.count + 1, E)
        self._commit(tok, reads, writes)
        return tok

    def dma(self, Q, out, in_, reads, writes, semslot, n=1, **kw):
        self._deps(Q, reads, writes, own_sem=semslot[0])
        Q.eng.dma_start(out=out, in_=in_, **kw).then_inc(semslot[0], 16)
        semslot[1] += 16
        tok = Tok(semslot[0], semslot[1], None)
        self._commit(tok, reads, writes)
        return tok

    def semslot(self, name):
        return [self.sem(name), 0]


def build_program():
    nc = bass.Bass("TRN2", target_bir_lowering=False)
    es = ExitStack()
    with es:
        _build(nc, es)
    return nc


_MARKS = []


def _build(nc, es):
    k = K(nc, es)
    _MARKS.clear()
    k.marks = _MARKS
    PE, ACT, DVE, POOL, SP = k.pe, k.act, k.dve, k.pool, k.sp

    def din(name, shape, dt=F32):
        return nc.dram_tensor(name, list(shape), dt, kind="ExternalInput")

    def dout(name, shape, dt=F32):
        return nc.dram_tensor(name, list(shape), dt, kind="ExternalOutput")

    x_prompt = din("x_prompt", [SEQ, D]).ap()
    x_sample = din("x_sample", [TS, D]).ap()
    cache_a_k = din("cache_a_k", [2, PAST, 8, 128]).ap()
    cache_a_v = din("cache_a_v", [2, PAST, 8, 128]).ap()
    cache_b_k = din("cache_b_k", [2, WIN, 2, 64]).ap()
    cache_b_v = din("cache_b_v", [2, WIN, 2, 64]).ap()
    rel_table = din("rel_table", [32, 16])
    norm_mix_g = din("norm_mix_g", [4, D])
    norm_mlp_g = din("norm_mlp_g", [4, D])
    final_norm_g = din("final_norm_g", [D])
    a_w_qkv = din("a_w_qkv", [2, D, 3072]).ap()
    a_lambda = din("a_lambda", [2, 256])
    a_subln_g = din("a_subln_g", [2, 128])
    a_w_o = din("a_w_o", [2, D, D]).ap()
    b_w_qkv = din("b_w_qkv", [2, D, 1280]).ap()
    b_sinks = din("b_sinks", [2, 16])
    b_w_o = din("b_w_o", [2, D, D]).ap()
    mlp_w_up = din("mlp_w_up", [4, D, DFF]).ap()
    mlp_w_down = din("mlp_w_down", [4, DFF, D]).ap()
    ohr_d = din("c_ohr", [32, 383]).ap()
    ident_d = din("c_ident", [128, 128]).ap()

    y_prompt = dout("y_prompt", [SEQ, D]).ap()
    y_sample = dout("y_sample", [TS, D]).ap()
    a_k_prompt = dout("a_k_prompt", [2, SEQ, 8, 128]).ap()
    a_v_prompt = dout("a_v_prompt", [2, SEQ, 8, 128]).ap()
    b_k_prompt = dout("b_k_prompt", [2, WIN, 2, 64]).ap()
    b_v_prompt = dout("b_v_prompt", [2, WIN, 2, 64]).ap()
    a_k_sample = dout("a_k_sample", [2, TS, 8, 128]).ap()
    a_v_sample = dout("a_v_sample", [2, TS, 8, 128]).ap()
    b_k_sample = dout("b_k_sample", [2, WIN, 2, 64]).ap()
    b_v_sample = dout("b_v_sample", [2, WIN, 2, 64]).ap()
    efd = nc.dram_tensor("efd_scratch", [16, 384], F32, kind="Internal")

    X = k.sb("X", [128, NT, D], F32)
    XB = [Buf(f"X{t}") for t in range(NT)]
    E = k.sb("E", [128, 16, 256], BF16)
    EB = Buf("E")
    QS = k.sb("QS", [64, 16, 64], BF16)
    RING = k.sb("RING", [128, NRING, RING_EL], BF16)
    ARENA = k.sb("ARENA", [128, 41856], BF16)
    FA = k.sb("FA", [128, 4224], F32)
    ident = k.sb("ident", [128, 128], BF16)
    gT = k.sb("gT", [128, 9, 8], F32)
    cfar = k.sb("cfar", [128, 16], F32)
    esink = k.sb("esink", [128, 2, 16], F32)
    lamb = k.sb("lamb", [128, 2, 260], F32)
    gsub = k.sb("gsub", [128, 2, 128], F32)
    stats = k.sb("stats", [128, 96], F32)
    otok = k.sb("otok", [128, 4, 128], BF16)
    junk2 = k.sb("junk2", [128, 128], BF16)
    junk1 = k.sb("junk1", [128, D], BF16)
    tbs = FA[0:32, 800:816]
    ohs = FA[0:32, 0:384]
    efs = FA[0:16, 384:768]

    PS = [es.enter_context(nc.psum_tensor(f"ps{i}", [128, 512], F32)) for i in range(8)]
    PB = [Buf(f"ps{i}", excl=True) for i in range(8)]

    A_hT = 0
    A_OT = A_hT + 8 * NTOK
    A_U = A_OT + 2 * NTOK
    KTW = NTOK + PAST
    VW = 132
    NVT = 25
    U_QT = 0
    U_KT = NTOK
    U_V = U_KT + KTW
    USZ = U_V + NVT * VW
    A_PT = A_U + 2 * USZ
    A_KTOK = A_PT + 4 * 512
    A_CK = A_KTOK + 512
    A_END = A_CK + 1024
    assert A_END <= 41856
    A_HN = A_U
    A_JUNK = A_HN + 4096
    M_hT = 0
    M_AT = 8 * 1088
    M_END = M_AT + 8 * 1088
    assert M_END <= A_U

    def hT_ap(c, c0, c1):
        return ARENA[:, A_hT + c * NTOK + c0:A_hT + c * NTOK + c1]

    def OT_ap(kk, c0, c1):
        return ARENA[:, A_OT + kk * NTOK + c0:A_OT + kk * NTOK + c1]

    def hn_ap(i, R):
        return ARENA[0:R, A_HN + i * 1024:A_HN + (i + 1) * 1024]

    def junk_ap(R, w=1024):
        return ARENA[0:R, A_JUNK:A_JUNK + w]

    F_TMP = 0
    F_OA = 1024
    F_STG = 2560
    F_O = 3584
    F_T1 = 4096
    F_END = 4224

    hTB = [Buf(f"hT{g}") for g in range(5)]
    OTB = [Buf(f"OT{g}") for g in range(5)]
    ringB = [Buf(f"ring{i}") for i in range(NRING)]
    ring_sem = [k.semslot(f"ringsem{i}") for i in range(NRING)]

    def _maxtok(a, b):
        if a is None:
            return b
        return a if a.val >= b.val else b

    def arena_barrier(bufs_old, bufs_new):
        for nb in bufs_new:
            for ob in bufs_old:
                if ob.w is not None:
                    nb.r[("w", id(ob.w.sem))] = _maxtok(nb.r.get(("w", id(ob.w.sem))), ob.w)
                for kk, t in ob.r.items():
                    nb.r[("r", id(t.sem))] = _maxtok(nb.r.get(("r", id(t.sem))), t)

    reqs = []
    ring_state = dict(issued=0, consumed=0, released=set())

    def gen_requests():
        def attn_layer(kind, j, w_qkv, w_o):
            def unit_reqs(u):
                if kind == "A":
                    reqs.append([(lambda ra: ra[:, 0:1024].rearrange("p (c f) -> p c f", c=8),
                                  w_qkv[j, :, u * 128:(u + 1) * 128].rearrange("(c p) f -> p c f", p=128))])
                    reqs.append([(lambda ra: ra[:, 0:2048].rearrange("p (c f) -> p c f", c=8)[:, :, 0:128],
                                  w_qkv[j, :, 1024 + u * 128:1024 + (u + 1) * 128].rearrange("(c p) f -> p c f", p=128)),
                                 (lambda ra: ra[:, 0:2048].rearrange("p (c f) -> p c f", c=8)[:, :, 128:256],
                                  w_qkv[j, :, 2048 + u * 128:2048 + (u + 1) * 128].rearrange("(c p) f -> p c f", p=128))])
                else:
                    n = u // 4
                    reqs.append([(lambda ra: ra[:, 0:1024].rearrange("p (c f) -> p c f", c=8),
                                  w_qkv[j, :, u * 128:(u + 1) * 128].rearrange("(c p) f -> p c f", p=128))])
                    if u % 4 == 0:
                        reqs.append([(lambda ra: ra[:, 0:1024].rearrange("p (c f) -> p c f", c=8)[:, :, 0:64],
                                      w_qkv[j, :, 1024 + n * 64:1024 + (n + 1) * 64].rearrange("(c p) f -> p c f", p=128)),
                                     (lambda ra: ra[:, 0:1024].rearrange("p (c f) -> p c f", c=8)[:, :, 64:128],
                                      w_qkv[j, :, 1152 + n * 64:1152 + (n + 1) * 64].rearrange("(c p) f -> p c f", p=128))])
            unit_reqs(0)
            for u in range(8):
                if u + 1 < 8:
                    unit_reqs(u + 1)
                if u % 2 == 1:
                    r0 = (u - 1) * 128
                    reqs.append([(lambda ra: ra[:, 0:2048].rearrange("p (k d) -> p k d", k=2),
                                  w_o[j, r0:r0 + 256, :].rearrange("(k p) d -> p k d", p=128))])

        def mlp_layer(li):
            for tg in range(2):
                for fq in range(4):
                    for cc in range(4):
                        f0 = fq * 1024 + cc * 256
                        reqs.append([(lambda ra: ra[:, 0:2048].rearrange("p (c f) -> p c f", c=8),
                                      mlp_w_up[li, :, f0:f0 + 256].rearrange("(c p) f -> p c f", p=128))])
                    for cc in range(4):
                        r0 = fq * 1024 + cc * 256
                        reqs.append([(lambda ra: ra[:, 0:2048].rearrange("p (k d) -> p k d", k=2),
                                      mlp_w_down[li, r0:r0 + 256, :].rearrange("(k p) d -> p k d", p=128))])
        attn_layer("A", 0, a_w_qkv, a_w_o)
        mlp_layer(0)
        attn_layer("B", 0, b_w_qkv, b_w_o)
        mlp_layer(1)
        attn_layer("A", 1, a_w_qkv, a_w_o)
        mlp_layer(2)
        attn_layer("B", 1, b_w_qkv, b_w_o)
        mlp_layer(3)

    def ring_try_issue():
        while ring_state["issued"] < len(reqs):
            n = ring_state["issued"]
            if n >= NRING and (n - NRING) not in ring_state["released"]:
                break
            s = n % NRING
            for (dfn, src) in reqs[n]:
                k.dma(POOL, dfn(RING[:, s, :]), src, [], [ringB[s]], ring_sem[s])
            ring_state["issued"] += 1

    def ring_next():
        n = ring_state["consumed"]
        ring_state["consumed"] += 1
        ring_try_issue()
        assert n < ring_state["issued"], f"ring chunk {n} not issued (deadlock in request order)"
        return n, n % NRING

    def ring_done(n):
        ring_state["released"].add(n)
        ring_try_issue()

    ld = k.semslot("ld")
    tabB = Buf("tab")
    misc = Buf("misc")
    ssB = Buf("ss")
    tab_sem = k.semslot("tabld")
    k.dma(SP, tbs, rel_table.ap(), [], [tabB], tab_sem)
    k.dma(SP, ohs[:, 0:383], ohr_d, [], [tabB], tab_sem)
    for g4 in range(4):
        k.dma(SP, X[:, 4 * g4:4 * g4 + 4, :],
              x_prompt[512 * g4:512 * g4 + 512, :].rearrange("(t p) d -> p t d", p=128),
              [], [XB[4 * g4 + i] for i in range(4)], k.semslot(f"xl{g4}"))
    k.op(DVE, lambda: nc.vector.memset(X[:, 16, :], 0.0), [], [XB[16]])
    k.dma(SP, X[0:64, 16, :], x_sample, [], [XB[16]], k.semslot("xls"))
    identB = Buf("ident")
    k.dma(POOL, ident[:], ident_d, [], [identB], k.semslot("identld"))
    gen_requests()
    ring_try_issue()
    with nc.allow_non_contiguous_dma(reason="small strided parameter loads"):
        k.dma(SP, gT[:, 0:4, :], norm_mix_g.ap().rearrange("l (c p) -> p l c", p=128), [], [misc], ld)
        k.dma(SP, cfar[:], bass.AP(rel_table, 15 * 16, [[0, 128], [1, 16]]), [], [misc], ld)
        k.dma(SP, esink[:].rearrange("p l h -> p (l h)"), bass.AP(b_sinks, 0, [[0, 128], [1, 32]]), [], [misc], ld)
        k.dma(SP, lamb[:, :, 0:256], bass.AP(a_lambda, 0, [[0, 128], [256, 2], [1, 256]]), [], [misc], ld)
        k.dma(SP, gsub[:], bass.AP(a_subln_g, 0, [[0, 128], [128, 2], [1, 128]]), [], [misc], ld)
        k.dma(SP, gT[:, 4:8, :], norm_mlp_g.ap().rearrange("l (c p) -> p l c", p=128), [], [misc], ld)
        k.dma(SP, gT[:, 8, :], final_norm_g.ap().rearrange("(c p) -> p c", p=128), [], [misc], ld)
    st_misc = k.semslot("st_misc")
    k.store_sems.append(st_misc)
    for j in range(2):
        k.dma(SP, b_k_sample[j, 0:64], cache_b_k[j, 64:128], [], [], st_misc)
        k.dma(SP, b_v_sample[j, 0:64], cache_b_v[j, 64:128], [], [], st_misc)

    k.op(PE, lambda: nc.tensor.matmul(PS[0][0:16, 0:383], tbs, ohs[:, 0:383], start=True, stop=True),
         [tabB], [PB[0]])
    ncf = stats[0:16, 90:91]
    stB = Buf("stB")
    k.op(DVE, lambda: nc.vector.tensor_scalar(ncf, PS[0][0:16, 382:383], -1.0, None, ALU.mult), [PB[0]], [stB])
    efB = Buf("efs")
    k.op(ACT, lambda: nc.scalar.activation(out=efs[:, 0:383], in_=PS[0][0:16, 0:383], func=AF.Exp, bias=ncf, scale=1.0),
         [PB[0], stB], [efB])
    efdB = Buf("efd")
    k.dma(SP, efd.ap()[:, 0:383], efs[:, 0:383], [efB], [efdB], k.semslot("efd_st"))
    esem = k.semslot("eld")
    e32B = Buf("e32")
    E32 = ARENA[:, A_U + USZ:A_U + USZ + 8192].bitcast(F32).rearrange("p (m c) -> p m c", m=16)
    for p in range(128):
        src = bass.AP(efd, 127 - p, [[384, 16], [1, 256]])
        k.dma(SP, E32[p:p + 1, :, :], src, [efdB], [e32B], esem)
    QSB = Buf("QS")
    e_done = [False]

    def finish_E():
        if e_done[0]:
            return
        e_done[0] = True
        k.op(DVE, lambda: nc.vector.tensor_copy(E[:, 0:8, :], E32[:, 0:8, :]), [e32B], [EB])
        k.op(DVE, lambda: nc.vector.tensor_copy(E[:, 8:16, :], E32[:, 8:16, :]), [e32B], [EB])
        arena_barrier([e32B], unit_bufs)
        k.op(DVE, lambda: nc.vector.memset(E[64:128, :, 0:64], 0.0), [EB], [EB])
        k.op(DVE, lambda: nc.vector.tensor_copy(QS[:], E[0:64, :, 192:256]), [EB], [QSB])

    k.op(DVE, lambda: nc.vector.memset(stats[:, 0:34], 0.0), [], [ssB])
    k.op(ACT, lambda: nc.scalar.activation(out=esink[:].rearrange("p l h -> p (l h)"),
                                           in_=esink[:].rearrange("p l h -> p (l h)"), func=AF.Exp), [misc], [misc])
    fa_users = []

    ps_rr = [0]

    def next_bank(cands):
        b = cands[ps_rr[0] % len(cands)]
        ps_rr[0] += 1
        return b

    evac_rr = [0]
    evac_mode = ["alt"]

    def evac_engine():
        if evac_mode[0] == "dve":
            return DVE
        evac_rr[0] += 1
        return ACT if evac_rr[0] % 8 in (0, 2, 3, 5, 6) else DVE

    def rstd_from_ss(ss_ap, out_ap, n, bufs):
        k.op(ACT, lambda: nc.scalar.activation(out=out_ap, in_=ss_ap, func=AF.Ln, scale=1.0 / n, bias=EPS), bufs, bufs)
        k.op(ACT, lambda: nc.scalar.activation(out=out_ap, in_=out_ap, func=AF.Exp, scale=-0.5), bufs, bufs)

    def drain(gen):
        for _ in gen:
            pass

    hnB = [Buf(f"hn{i}") for i in range(8)]
    junkB = Buf("junk")
    junk2B = Buf("junk2")

    pre_sq = set()
    junk1B = Buf("junk1")

    def emit_sq(t):
        R = tile_rows(t)
        k.op(ACT, lambda: nc.scalar.activation(out=junk1[0:R, :], in_=X[0:R, t, :], func=AF.Square, accum_out=stats[0:R, t:t + 1]),
             [XB[t], ssB], [junk1B, ssB])
        pre_sq.add(t)

    def norm_to_hT(gl, tiles, dst_fn, dstB_fn, banks):
        drain(norm_gen(gl, tiles, dst_fn, dstB_fn, banks))

    def norm_gen(gl, tiles, dst_fn, dstB_fn, banks):
        nt = len(tiles)
        t_0 = tiles[0]
        for t in tiles:
            if t not in pre_sq:
                emit_sq(t)
        rstd_from_ss(stats[:, t_0:t_0 + nt], stats[:, 17 + t_0:17 + t_0 + nt], D, [ssB])
        ngrp = (nt + 3) // 4
        grp_cols = {}

        def emit_hn(gidx):
            grp = tiles[4 * gidx:4 * gidx + 4]
            col = 0
            cols = []
            for i, t in enumerate(grp):
                R = tile_rows(t)
                hb = (gidx % 2) * 4 + i
                k.op(DVE, lambda: nc.vector.tensor_scalar(hn_ap(hb, R), X[0:R, t, :], stats[0:R, 17 + t:18 + t], None, ALU.mult),
                     [XB[t], ssB], [hnB[hb]])
                cols.append((t, R, hb, col))
                col += R
            grp_cols[gidx] = (cols, col)

        def emit_TE(gidx, after_T):
            cols, W = grp_cols[gidx]
            t0 = cols[0][0]
            c0 = t0 * 128
            for c in range(8):
                bank = next_bank(banks)
                pbf = PS[bank][:].bitcast(BF16)
                for jj, (t, R, hb, cc) in enumerate(cols):
                    k.op(PE, lambda: nc.tensor.transpose(pbf[:, cc:cc + R], hn_ap(hb, R)[:, c * 128:(c + 1) * 128], ident[0:R, 0:R]),
                         [hnB[hb], identB], [PB[bank]], signal=(jj == len(cols) - 1))
                if c == 0 and after_T is not None:
                    after_T()
                Eg = evac_engine()
                if Eg is ACT:
                    k.op(ACT, lambda: nc.scalar.activation(out=dst_fn(c, c0, c0 + W), in_=pbf[:, 0:W], func=AF.Copy,
                                                           scale=gT[:, gl, c:c + 1]), [PB[bank], misc], [dstB_fn(t0)])
                else:
                    k.op(DVE, lambda: nc.vector.tensor_scalar(dst_fn(c, c0, c0 + W), pbf[:, 0:W], gT[:, gl, c:c + 1], None, ALU.mult),
                         [PB[bank], misc], [dstB_fn(t0)])
                yield

        emit_hn(0)
        yield
        for gidx in range(ngrp):
            yield from emit_TE(gidx, (lambda g=gidx: emit_hn(g + 1)) if gidx + 1 < ngrp else None)
        k.op(DVE, lambda: nc.vector.memset(stats[:, t_0:t_0 + nt], 0.0), [], [ssB])
        for t in tiles:
            pre_sq.discard(t)

    QTB = [Buf("QT0"), Buf("QT1")]
    KTB = [[Buf(f"KT{p}_{g}") for g in range(6)] for p in range(2)]
    VB = [[Buf(f"V{p}_{t}") for t in range(NVT)] for p in range(2)]
    PTB = [[Buf(f"PT{s}{i}") for i in range(2)] for s in range(2)]
    TMPB = [[Buf(f"TMP{s}{i}") for i in range(2)] for s in range(2)]
    OAB = Buf("OA")
    KTOKB = Buf("KTOK")
    CKB = Buf("CK")
    NSTG = 4
    STGB = [Buf(f"stg{i}") for i in range(NSTG)]
    stg_sem = [k.semslot(f"stg{i}") for i in range(NSTG)]
    k.store_sems += stg_sem
    stg_n = [0]
    OB = [Buf(f"O{i}") for i in range(4)]
    T1B = Buf("T1")
    otokB = [Buf(f"otok{i}") for i in range(4)]
    finB = Buf("fin")
    OPTB = [Buf("opt0"), Buf("opt1")]
    arena_barrier(fa_users, [OAB, T1B] + STGB + OB + OPTB)
    ck_sem = k.semslot("ck")
    cv_sem = [k.semslot("cv0"), k.semslot("cv1")]
    unit_bufs = QTB + KTB[0] + KTB[1] + VB[0] + VB[1]

    def QT_ap(p, r0, r1, c0, c1):
        o = A_U + p * USZ + U_QT
        return ARENA[r0:r1, o + c0:o + c1]

    def KT_ap(p, r0, r1, c0, c1):
        o = A_U + p * USZ + U_KT
        return ARENA[r0:r1, o + c0:o + c1]

    def V_ap(p, r0, r1, t, c0, c1):
        o = A_U + p * USZ + U_V + t * VW
        return ARENA[r0:r1, o + c0:o + c1]

    def Vall_ap(p, t0, t1):
        o = A_U + p * USZ + U_V
        return ARENA[:, o + t0 * VW:o + t1 * VW].rearrange("p (t w) -> p t w", w=VW)

    def PT_ap(s, i, r0, r1, c0, c1):
        o = A_PT + (s * 2 + i) * 512
        return ARENA[r0:r1, o + c0:o + c1]

    def TMP_ap(s, i, r0, r1, c0, c1):
        o = F_TMP + (s * 2 + i) * 256
        return FA[r0:r1, o + c0:o + c1]

    def ktb(p, col):
        return KTB[p][min(col // 512, 4)] if col < NTOK else KTB[p][5]

    def advance(bg):
        if bg is None:
            return False
        try:
            next(bg)
            return True
        except StopIteration:
            return False

    def attention(u, qp, kp, dv, groups, finalize, bg):
        dvp = dv + 1
        step_list = []
        for gi, g in enumerate(groups):
            q = g["q"]
            merged, cur, w = [], [], 0
            for st in g["steps"]:
                (kcol, nk, vt, lo, hi, classes) = st
                wd = q[hi][0] + q[hi][1] - q[lo][0]
                if cur and w + wd > 512:
                    merged.append(cur)
                    cur, w = [], 0
                cur.append(st + (w,))
                w += wd
            if cur:
                merged.append(cur)
            for si, subs in enumerate(merged):
                step_list.append((gi, si == len(merged) - 1, subs))
        last_for = {}
        for n, (gi, lastg, subs) in enumerate(step_list):
            for sub in subs:
                for qi in range(sub[3], sub[4] + 1):
                    last_for[(gi, qi)] = (n, sub[0])
        bank_started = {}
        pending = []
        cur_n = [0]

        def acc_ap(gi, s, qi, R):
            G = len(groups[gi]["q"])
            a = s * G + qi
            bank = 4 + a // 3
            off = (a % 3) * dvp
            return bank, PS[bank][0:R, off:off + dvp], a

        def emit_qk(n):
            gi, lastg, subs = step_list[n]
            q = groups[gi]["q"]
            for s in range(2):
                bank = 2 * (n % 2) + s
                for (kcol, nk, vt, lo, hi, classes, coff) in subs:
                    c0 = q[lo][0]
                    c1 = q[hi][0] + q[hi][1]
                    k.op(PE, lambda: nc.tensor.matmul(PS[bank][0:nk, coff:coff + c1 - c0], KT_ap(kp, 64 * s, 64 * s + 64, kcol, kcol + nk),
                                                      QT_ap(qp, 64 * s, 64 * s + 64, c0, c1), start=True, stop=True, skip_group_check=True),
                         [ktb(kp, kcol), QTB[qp]], [PB[bank]])

        def emit_exp(n):
            gi, lastg, subs = step_list[n]
            q = groups[gi]["q"]
            segs = []
            for (kcol, nk, vt, lo, hi, classes, coff) in subs:
                c0 = q[lo][0]
                nsp = 0
                while nsp < len(classes) and classes[nsp] != "F":
                    nsp += 1
                wsp = sum(q[lo + i][1] for i in range(nsp))
                wtot = q[hi][0] + q[hi][1] - c0
                if nsp > 0:
                    segs.append(["S", nk, coff, wsp, 0 if classes[0] == "D" else 128])
                if wtot > wsp:
                    if segs and segs[-1][0] == "F" and segs[-1][1] == nk and segs[-1][2] + segs[-1][3] == coff + wsp:
                        segs[-1][3] += wtot - wsp
                    else:
                        segs.append(["F", nk, coff + wsp, wtot - wsp])
            i2 = n % 2
            for s in range(2):
                mp = 2 * u + s
                bank = 2 * (n % 2) + s
                for sg in segs:
                    nk, c, w = sg[1], sg[2], sg[3]
                    k.op(ACT, lambda: nc.scalar.activation(out=PT_ap(s, i2, 0, nk, c, c + w), in_=PS[bank][0:nk, c:c + w], func=AF.Exp,
                                                           bias=cfar[0:nk, mp:mp + 1], scale=1.0), [PB[bank], misc], [PTB[s][i2]])
                for sg in segs:
                    if sg[0] == "S":
                        nk, c, w, ecol = sg[1], sg[2], sg[3], sg[4]
                        k.op(DVE, lambda: nc.vector.tensor_tensor(PT_ap(s, i2, 0, nk, c, c + w), PT_ap(s, i2, 0, nk, c, c + w),
                                                                  E[0:nk, mp, ecol:ecol + w], ALU.mult),
                             [PTB[s][i2], EB], [PTB[s][i2]])

        def emit_pv(n):
            gi, lastg, subs = step_list[n]
            q = groups[gi]["q"]
            i2 = n % 2
            items = [(s, sub, qi) for s in range(2) for sub in subs for qi in range(sub[3], sub[4] + 1)]
            for idx, (s, sub, qi) in enumerate(items):
                (kcol, nk, vt, lo, hi, classes, coff) = sub
                c0 = q[lo][0]
                qc, nq = q[qi]
                bank, oap, a = acc_ap(gi, s, qi, nq)
                first = not bank_started.get((gi, bank), False)
                bank_started[(gi, bank)] = True
                k.op(PE, lambda: nc.tensor.matmul(oap, PT_ap(s, i2, 0, nk, coff + qc - c0, coff + qc - c0 + nq), V_ap(kp, 0, nk, vt, 0, dvp),
                                                  start=first, stop=(last_for[(gi, qi)] == (n, kcol)), skip_group_check=True),
                     [PTB[s][i2], VB[kp][vt]], [PB[bank]], signal=(idx == len(items) - 1))
            if lastg:
                for x in sorted(pending, key=lambda x: x[0]):
                    x[1]()
                pending.clear()
                for (dl, fn) in finalize(gi, groups[gi], acc_ap):
                    pending.append((cur_n[0] + dl, fn))

        N = len(step_list)
        emit_qk(0)
        for n in range(N + 1):
            cur_n[0] = n
            due = [x for x in pending if x[0] <= n]
            for x in due:
                pending.remove(x)
                x[1]()
            if n < N:
                emit_exp(n)
            if n >= 1:
                emit_pv(n - 1)
            if n + 1 < N:
                emit_qk(n + 1)
            advance(bg)
        for x in sorted(pending, key=lambda x: x[0]):
            x[1]()
        pending.clear()
        while advance(bg):
            pass

    oa_stride = [129]
    b_mode = [False]

    def evac_acc(G, R, dvp):
        nacc = 2 * G
        nb = (nacc + 2) // 3
        oa_stride[0] = dvp
        for b in range(nb):
            na = min(3, nacc - 3 * b)
            w = na * dvp
            if b_mode[0]:
                k.op(ACT, lambda: nc.scalar.copy(FA[0:R, F_OA + 3 * b * dvp:F_OA + 3 * b * dvp + w], PS[4 + b][0:R, 0:w]),
                     [PB[4 + b]], [OAB])
            else:
                k.op(DVE, lambda: nc.vector.tensor_copy(FA[0:R, F_OA + 3 * b * dvp:F_OA + 3 * b * dvp + w], PS[4 + b][0:R, 0:w]),
                     [PB[4 + b]], [OAB])

    def oa_ap(a, R, c0, c1):
        o = F_OA + a * oa_stride[0]
        return FA[0:R, o + c0:o + c1]

    def oa_sums(R, a0, na):
        dvp = oa_stride[0]
        return FA[0:R, F_OA + a0 * dvp:F_OA + (a0 + na) * dvp].rearrange("p (a w) -> p a w", w=dvp)[:, :, dvp - 1:dvp]

    def transposes_to_OT(kk, q, R):
        bank = 7
        pbf = PS[bank][:].bitcast(BF16)
        col = 0
        for qi, (qc, nq) in enumerate(q):
            k.op(PE, lambda: nc.tensor.transpose(pbf[:, col:col + nq], otok[0:nq, qi, :], ident[0:nq, 0:nq]),
                 [otokB[qi], identB], [PB[bank]], signal=(qi == len(q) - 1))
            col += nq
        c0 = q[0][0]
        gq = min(c0 // 512, 4)
        k.op(DVE, lambda: nc.vector.tensor_copy(OT_ap(kk, c0, c0 + col), pbf[:, 0:col]), [PB[bank]], [OTB[gq]])

    opt_n = [0]

    def out_proj_pair(j, after_tile=None):
        rid, s = ring_next()
        for t in range(NT):
            R = tile_rows(t)
            g = min(t // 4, 4)
            for dh in range(2):
                bank = next_bank([0, 1, 2, 3])
                for kk in range(2):
                    k.op(PE, lambda: nc.tensor.matmul(PS[bank][0:R, :], OT_ap(kk, t * 128, t * 128 + R),
                                                      RING[:, s, kk * 1024 + dh * 512:kk * 1024 + dh * 512 + 512],
                                                      start=(kk == 0), stop=(kk == 1)),
                         [OTB[g], ringB[s]], [PB[bank]], signal=(kk == 1))
                if True:
                    k.op(DVE, lambda: nc.vector.tensor_tensor(X[0:R, t, dh * 512:dh * 512 + 512], PS[bank][0:R, :],
                                                              X[0:R, t, dh * 512:dh * 512 + 512], ALU.add),
                         [PB[bank], XB[t]], [XB[t]])
                else:
                    ob = opt_n[0] % 2
                    opt_n[0] += 1
                    tmp = FA[0:R, F_TMP + ob * 512:F_TMP + ob * 512 + 512]
                    k.op(ACT, lambda: nc.scalar.copy(tmp, PS[bank][0:R, :]), [PB[bank]], [OPTB[ob]])
                    k.op(POOL, lambda: nc.gpsimd.tensor_tensor(X[0:R, t, dh * 512:dh * 512 + 512], tmp,
                                                               X[0:R, t, dh * 512:dh * 512 + 512], ALU.add),
                         [OPTB[ob], XB[t]], [XB[t]])
            if after_tile is not None:
                after_tile(t)
        ring_done(rid)

    def q_chunks(rq_s, qp, split=True):
        rq, sq = rq_s
        for g in range(5):
            g0 = g * 512
            g1 = min(g0 + 512, NTOK)
            halves = [(g0, g0 + 256), (g0 + 256, g1)] if (split and g1 - g0 == 512) else [(g0, g1)]
            for (c0, c1) in halves:
                bank = 7
                for c in range(8):
                    k.op(PE, lambda: nc.tensor.matmul(PS[bank][:, 0:c1 - c0], RING[:, sq, c * 128:(c + 1) * 128], hT_ap(c, c0, c1),
                                                      start=(c == 0), stop=(c == 7)),
                         [hTB[g], ringB[sq]], [PB[bank]], signal=(c == 7))
                if b_mode[0]:
                    k.op(ACT, lambda: nc.scalar.activation(out=QT_ap(qp, 0, 128, c0, c1), in_=PS[bank][:, 0:c1 - c0], func=AF.Copy, scale=0.125),
                         [PB[bank]], [QTB[qp]])
                else:
                    k.op(DVE, lambda: nc.vector.tensor_scalar(QT_ap(qp, 0, 128, c0, c1), PS[bank][:, 0:c1 - c0], 0.125, None, ALU.mult),
                         [PB[bank]], [QTB[qp]])
                yield
        ring_done(rq)

    def ktrans_chunk(kp, t0, ntl):
        bank2 = 7
        pbf = PS[bank2][:].bitcast(BF16)
        col = 0
        for i in range(ntl):
            Ri = tile_rows(t0 + i)
            k.op(PE, lambda: nc.tensor.transpose(pbf[:, col:col + Ri], ARENA[0:Ri, A_KTOK + i * 128:A_KTOK + i * 128 + 128], ident[0:Ri, 0:Ri]),
                 [KTOKB, identB], [PB[bank2]], signal=(i == ntl - 1))
            col += Ri
        k.op(DVE, lambda: nc.vector.tensor_copy(KT_ap(kp, 0, 128, t0 * 128, t0 * 128 + col), pbf[:, 0:col]),
             [PB[bank2]], [KTB[kp][min(t0 // 4, 4)]])

    def proj_A(li, j, h):
        par = h % 2
        rq_s = ring_next()
        rkv, skv = ring_next()
        k.dma(POOL, ARENA[:, A_CK:A_CK + 1024].rearrange("p (t f) -> p t f", t=8),
              cache_a_k[j, :, h, :].rearrange("(t p) f -> p t f", p=128), [], [CKB], ck_sem)
        k.dma(POOL, Vall_ap(par, 17, 25)[:, :, 0:128], cache_a_v[j, :, h, :].rearrange("(t p) f -> p t f", p=128),
              [], VB[par][17:25], cv_sem[par])
        yield from q_chunks(rq_s, par)
        for t in range(NT):
            R = tile_rows(t)
            g = min(t // 4, 4)
            bank = 7
            for c in range(8):
                k.op(PE, lambda: nc.tensor.matmul(PS[bank][0:R, 0:256], hT_ap(c, t * 128, t * 128 + R), RING[:, skv, c * 256:(c + 1) * 256],
                                                  start=(c == 0), stop=(c == 7)), [hTB[g], ringB[skv]], [PB[bank]], signal=(c == 7))
            sg = stg_n[0] % NSTG
            stg_n[0] += 1
            stg = FA[0:R, F_STG + sg * 256:F_STG + sg * 256 + 256]
            k.op(DVE, lambda: nc.vector.tensor_copy(stg, PS[bank][0:R, 0:256]), [PB[bank]], [STGB[sg]])
            i4 = t % 4
            k.op(POOL, lambda: nc.gpsimd.tensor_copy(ARENA[0:R, A_KTOK + i4 * 128:A_KTOK + i4 * 128 + 128], stg[:, 0:128]),
                 [STGB[sg]], [KTOKB])
            k.op(POOL, lambda: nc.gpsimd.tensor_copy(V_ap(par, 0, R, t, 0, 128), stg[:, 128:256]), [STGB[sg]], [VB[par][t]])
            if t < 16:
                ko = a_k_prompt[j, t * 128:(t + 1) * 128, h, :]
                vo = a_v_prompt[j, t * 128:(t + 1) * 128, h, :]
            else:
                ko = a_k_sample[j, :, h, :]
                vo = a_v_sample[j, :, h, :]
            k.dma(SP, ko, stg[:, 0:128], [STGB[sg]], [], stg_sem[sg])
            k.dma(SP, vo, stg[:, 128:256], [STGB[sg]], [], stg_sem[sg])
            yield
            if i4 == 3 or t == NT - 1:
                ktrans_chunk(par, t - i4, i4 + 1)
                yield
        ring_done(rkv)
        for half in range(2):
            bank2 = 7
            pbf = PS[bank2][:].bitcast(BF16)
            for i in range(4):
                tt = half * 4 + i
                k.op(PE, lambda: nc.tensor.transpose(pbf[:, i * 128:(i + 1) * 128], ARENA[:, A_CK + tt * 128:A_CK + tt * 128 + 128], ident[:, :]),
                     [CKB, identB], [PB[bank2]], signal=(i == 3))
            k.op(DVE, lambda: nc.vector.tensor_copy(KT_ap(par, 0, 128, NTOK + half * 512, NTOK + half * 512 + 512), pbf[:, 0:512]),
                 [PB[bank2]], [KTB[par][5]])
            yield

    def proj_B(li, j, u):
        qp = u % 2
        n = u // 4
        kp = n % 2
        rq_s = ring_next()
        yield from q_chunks(rq_s, qp, split=False)
        if u % 4 != 0:
            return
        rkv, skv = ring_next()
        for dd in range(2):
            k.dma(POOL, ARENA[:, A_CK + dd * 64:A_CK + dd * 64 + 64], cache_b_k[j, :, n, :], [], [CKB], ck_sem)
        k.dma(POOL, V_ap(kp, 0, 128, 17, 0, 64), cache_b_v[j, :, n, :], [], [VB[kp][17]], cv_sem[kp])
        for t in range(NT):
            R = tile_rows(t)
            g = min(t // 4, 4)
            bank = 7
            for c in range(8):
                k.op(PE, lambda: nc.tensor.matmul(PS[bank][0:R, 0:128], hT_ap(c, t * 128, t * 128 + R), RING[:, skv, c * 128:(c + 1) * 128],
                                                  start=(c == 0), stop=(c == 7)), [hTB[g], ringB[skv]], [PB[bank]], signal=(c == 7))
            sg = stg_n[0] % NSTG
            stg_n[0] += 1
            stg = FA[0:R, F_STG + sg * 256:F_STG + sg * 256 + 128]
            k.op(ACT, lambda: nc.scalar.copy(stg, PS[bank][0:R, 0:128]), [PB[bank]], [STGB[sg]])
            i4 = t % 4
            for dd in range(2):
                k.op(POOL, lambda: nc.gpsimd.tensor_copy(ARENA[0:R, A_KTOK + i4 * 128 + dd * 64:A_KTOK + i4 * 128 + dd * 64 + 64], stg[:, 0:64]),
                     [STGB[sg]], [KTOKB])
            k.op(POOL, lambda: nc.gpsimd.tensor_copy(V_ap(kp, 0, R, t, 0, 64), stg[:, 64:128]), [STGB[sg]], [VB[kp][t]])
            if t >= 15:
                if t == 15:
                    ko, vo = b_k_prompt[j, :, n, :], b_v_prompt[j, :, n, :]
                else:
                    ko, vo = b_k_sample[j, 64:128, n, :], b_v_sample[j, 64:128, n, :]
                k.dma(SP, ko, stg[:, 0:64], [STGB[sg]], [], stg_sem[sg])
                k.dma(SP, vo, stg[:, 64:128], [STGB[sg]], [], stg_sem[sg])
            yield
            if i4 == 3 or t == NT - 1:
                ktrans_chunk(kp, t - i4, i4 + 1)
                yield
        ring_done(rkv)
        bank2 = 7
        pbf = PS[bank2][:].bitcast(BF16)
        k.op(PE, lambda: nc.tensor.transpose(pbf[:, 0:128], ARENA[:, A_CK:A_CK + 128], ident[:, :]), [CKB, identB], [PB[bank2]])
        k.op(DVE, lambda: nc.vector.tensor_copy(KT_ap(kp, 0, 128, NTOK, NTOK + 128), pbf[:, 0:128]), [PB[bank2]], [KTB[kp][5]])
        yield

    def layer_A(li, j):
        b_mode[0] = False
        lam_init = 0.8 - 0.6 * math.exp(-0.3 * li)
        lamB = Buf("lam")
        k.op(DVE, lambda: nc.vector.tensor_tensor(lamb[:, j, 0:64], lamb[:, j, 0:64], lamb[:, j, 64:128], ALU.mult), [misc], [lamB])
        k.op(DVE, lambda: nc.vector.tensor_tensor(lamb[:, j, 128:192], lamb[:, j, 128:192], lamb[:, j, 192:256], ALU.mult), [lamB], [lamB])
        k.op(DVE, lambda: nc.vector.reduce_sum(lamb[:, j, 256:257], lamb[:, j, 0:64], mybir.AxisListType.X), [lamB], [lamB])
        k.op(DVE, lambda: nc.vector.reduce_sum(lamb[:, j, 257:258], lamb[:, j, 128:192], mybir.AxisListType.X), [lamB], [lamB])
        k.op(ACT, lambda: nc.scalar.activation(out=lamb[:, j, 256:258], in_=lamb[:, j, 256:258], func=AF.Exp), [lamB], [lamB])
        k.op(DVE, lambda: nc.vector.tensor_tensor(lamb[:, j, 258:259], lamb[:, j, 256:257], lamb[:, j, 257:258], ALU.subtract), [lamB], [lamB])
        k.op(DVE, lambda: nc.vector.tensor_scalar(lamb[:, j, 258:259], lamb[:, j, 258:259], lam_init, None, ALU.add), [lamB], [lamB])
        k.op(DVE, lambda: nc.vector.tensor_scalar(gsub[:, j, :], gsub[:, j, :], 1.0 - lam_init, None, ALU.mult), [misc], [lamB])
        lam_ap = lamb[:, j, 258:259]
        if li > 0:
            k.op(DVE, lambda: nc.vector.tensor_copy(E[0:64, :, 192:256], QS[:]), [QSB], [EB])

        k.mark(f"A{li} norm")
        arena_barrier(unit_bufs, hnB + [junkB])
        evac_mode[0] = "alt"
        norm_to_hT(li, list(range(NT)), hT_ap, lambda t0: hTB[min(t0 // 4, 4)], [0, 1, 2, 3])
        arena_barrier(hnB + [junkB], unit_bufs)
        evac_mode[0] = "dve"
        k.op(DVE, lambda: nc.vector.memset(Vall_ap(0, 0, NVT)[:, :, 128:129], 1.0), [], VB[0])

        groups = []
        for g in range(4):
            q = [((4 * g + i) * 128, 128) for i in range(4)]
            steps = []
            for kt in range(4 * g + 4):
                lo = max(kt, 4 * g) - 4 * g
                cls = []
                for qt in range(4 * g + lo, 4 * g + 4):
                    cls.append("D" if qt == kt else ("P" if qt == kt + 1 else "F"))
                steps.append((kt * 128, 128, kt, lo, 3, cls))
            groups.append(dict(q=q, steps=steps))
        steps = []
        for i in range(8):
            steps.append((NTOK + 128 * i, 128, 17 + i, 0, 0, ["P" if i == 7 else "F"]))
        steps.append((2048, 64, 16, 0, 0, ["D"]))
        groups.append(dict(q=[(2048, 64)], steps=steps))

        k.mark(f"A{li} proj0")
        drain(proj_A(li, j, 0))
        finish_E()
        k.op(DVE, lambda: nc.vector.memset(Vall_ap(1, 0, NVT)[:, :, 128:129], 1.0), [], VB[1])
        for h in range(8):
            def finalize(gi, g, acc_ap, h=h):
                q = g["q"]
                G = len(q)
                R = q[0][1]
                evac_acc(G, R, 129)
                nacc = 2 * G

                def s2():
                    oa_stride[0] = 129
                    k.op(DVE, lambda: nc.vector.reciprocal(stats[0:R, 48:48 + nacc].rearrange("p (a o) -> p a o", o=1), oa_sums(R, 0, nacc)),
                         [OAB], [finB])
                    k.op(DVE, lambda: nc.vector.tensor_scalar(stats[0:R, 48 + G:48 + 2 * G], stats[0:R, 48 + G:48 + 2 * G], lam_ap[0:R, :], None, ALU.mult),
                         [finB, lamB], [finB])
                    k.op(DVE, lambda: nc.vector.memset(stats[0:R, 64:64 + G], 0.0), [], [finB])

                def s2q(qi):
                    def f():
                        oa_stride[0] = 129
                        k.op(DVE, lambda: nc.vector.tensor_scalar(oa_ap(G + qi, R, 0, 128), oa_ap(G + qi, R, 0, 128),
                                                                  stats[0:R, 48 + G + qi:49 + G + qi], None, ALU.mult),
                             [OAB, finB], [OAB])
                        k.op(DVE, lambda: nc.vector.scalar_tensor_tensor(
                            FA[0:R, F_O + qi * 128:F_O + qi * 128 + 128], oa_ap(qi, R, 0, 128), stats[0:R, 48 + qi:49 + qi],
                            oa_ap(G + qi, R, 0, 128), ALU.mult, ALU.subtract), [OAB, finB], [OB[qi]])
                    return f

                def s3():
                    for qi in range(G):
                        k.op(ACT, lambda: nc.scalar.activation(out=junk2[0:R, :], in_=FA[0:R, F_O + qi * 128:F_O + qi * 128 + 128],
                                                               func=AF.Square, accum_out=stats[0:R, 64 + qi:65 + qi]),
                             [OB[qi], finB], [junk2B, finB])
                    rstd_from_ss(stats[0:R, 64:64 + G], stats[0:R, 72:72 + G], 128, [finB])

                def s4q(qi):
                    def f():
                        k.op(DVE, lambda: nc.vector.scalar_tensor_tensor(
                            otok[0:R, qi, :], FA[0:R, F_O + qi * 128:F_O + qi * 128 + 128], stats[0:R, 72 + qi:73 + qi],
                            gsub[0:R, j, :], ALU.mult, ALU.mult), [OB[qi], finB, lamB], [otokB[qi]])
                    return f

                def s5():
                    transposes_to_OT(h % 2, q, R)
                if G == 1:
                    return [(1, s2), (1, s2q(0)), (2, s3), (3, s4q(0)), (4, s5)]
                return [(1, s2), (1, s2q(0)), (2, s2q(1)), (2, s2q(2)), (3, s2q(3)), (4, s3),
                        (5, s4q(0)), (5, s4q(1)), (6, s4q(2)), (6, s4q(3)), (7, s5)]

            k.mark(f"A{li} u{h} attn")
            bg = proj_A(li, j, h + 1) if h + 1 < 8 else None
            attention(h, h % 2, h % 2, 128, groups, finalize, bg)
            if h % 2 == 1:
                k.mark(f"A{li} u{h} oproj")
                out_proj_pair(j, emit_sq if h == 7 else None)

    def layer_B(li, j):
        b_mode[0] = True
        k.op(DVE, lambda: nc.vector.memset(E[0:64, :, 192:256], 0.0), [], [EB])
        k.mark(f"B{li} norm")
        arena_barrier(unit_bufs, hnB + [junkB])
        evac_mode[0] = "alt"
        norm_to_hT(li, list(range(NT)), hT_ap, lambda t0: hTB[min(t0 // 4, 4)], [0, 1, 2, 3])
        arena_barrier(hnB + [junkB], unit_bufs)
        evac_mode[0] = "dve"
        for p in range(2):
            k.op(DVE, lambda: nc.vector.memset(Vall_ap(p, 0, NVT)[:, :, 64:65], 1.0), [], VB[p])
        groups = []
        for g in range(4):
            q = [((4 * g + i) * 128, 128) for i in range(4)]
            steps = []
            for kt in range(max(4 * g - 1, 0), 4 * g + 4):
                lo = max(kt, 4 * g)
                hi = min(kt + 1, 4 * g + 3)
                cls = ["D" if qt == kt else "P" for qt in range(lo, hi + 1)]
                steps.append((kt * 128, 128, kt, lo - 4 * g, hi - 4 * g, cls))
            groups.append(dict(q=q, steps=steps))
        groups.append(dict(q=[(2048, 64)], steps=[(NTOK, 128, 17, 0, 0, ["P"]), (2048, 64, 16, 0, 0, ["D"])]))

        k.mark(f"B{li} proj0")
        drain(proj_B(li, j, 0))
        for u in range(8):
            def finalize(gi, g, acc_ap, u=u):
                q = g["q"]
                G = len(q)
                R = q[0][1]
                evac_acc(G, R, 65)

                def s2():
                    oa_stride[0] = 65
                    for s in range(2):
                        hd = 2 * u + s
                        k.op(DVE, lambda: nc.vector.tensor_scalar(stats[0:R, 48 + s * G:48 + (s + 1) * G].rearrange("p (a o) -> p a o", o=1),
                                                                  oa_sums(R, s * G, G), esink[0:R, j, hd:hd + 1], None, ALU.add),
                             [OAB, misc], [finB])
                    k.op(DVE, lambda: nc.vector.reciprocal(stats[0:R, 48:48 + 2 * G], stats[0:R, 48:48 + 2 * G]), [finB], [finB])
                    for qi in range(G):
                        for s in range(2):
                            a = s * G + qi
                            k.op(DVE, lambda: nc.vector.tensor_scalar(otok[0:R, qi, s * 64:s * 64 + 64], oa_ap(a, R, 0, 64),
                                                                      stats[0:R, 48 + a:49 + a], None, ALU.mult),
                                 [OAB, finB], [otokB[qi]])

                def s5():
                    transposes_to_OT(u % 2, q, R)
                return [(1, s2), (3, s5)]

            k.mark(f"B{li} u{u} attn")
            bg = proj_B(li, j, u + 1) if u + 1 < 8 else None
            attention(u, u % 2, (u // 4) % 2, 64, groups, finalize, bg)
            if u % 2 == 1:
                k.mark(f"B{li} u{u} oproj")
                out_proj_pair(j, emit_sq if u == 7 else None)

    mhTB = Buf("mhT")
    ATB = Buf("AT")
    RELB = [Buf("rel0"), Buf("rel1")]

    M_hT2 = 29312
    assert M_hT2 >= A_HN + 8192 and M_hT2 + 8 * 1088 <= A_PT
    mhTB2 = Buf("mhT2")

    def mlp(li):
        evac_mode[0] = "alt"
        arena_barrier(hTB + OTB, [mhTB, ATB])
        arena_barrier(unit_bufs, hnB + [junkB, mhTB2])
        arena_barrier(TMPB[0] + TMPB[1] + OPTB + fa_users, RELB)
        tile_sets = [list(range(8)), list(range(8, NT))]
        hbases = [M_hT, M_hT2]
        hbufs = [mhTB, mhTB2]

        def mk_dst(tg):
            tok0 = tile_sets[tg][0] * 128
            hb = hbases[tg]
            return lambda c, c0, c1: ARENA[:, hb + c * 1088 + (c0 - tok0):hb + c * 1088 + (c1 - tok0)]
        k.mark(f"M{li} tg0 norm")
        norm_to_hT(4 + li, tile_sets[0], mk_dst(0), lambda t0: mhTB, [0, 1, 2, 3])
        gen1 = norm_gen(4 + li, tile_sets[1], mk_dst(1), lambda t0: mhTB2, [4, 5, 6, 7])
        for tg in range(2):
            tiles = tile_sets[tg]
            tok0 = tiles[0] * 128
            hb = hbases[tg]
            hB = hbufs[tg]
            mov = [(0, 512), (512, 1024)] + ([(1024, 1088)] if tg == 1 else [])
            if tg == 1:
                k.mark(f"M{li} tg1 norm")
                drain(gen1)
            for fq in range(4):
                k.mark(f"M{li} tg{tg} fq{fq} up")
                ups = [ring_next() for cc in range(4)]
                for fc in range(8):
                    s = ups[fc // 2][1]
                    for (m0, m1) in mov:
                        bank = next_bank([0, 1, 2, 3])
                        for c in range(8):
                            k.op(PE, lambda: nc.tensor.matmul(
                                PS[bank][:, 0:m1 - m0], RING[:, s, c * 256 + (fc % 2) * 128:c * 256 + (fc % 2) * 128 + 128],
                                ARENA[:, hb + c * 1088 + m0:hb + c * 1088 + m1], start=(c == 0), stop=(c == 7)),
                                [hB, ringB[s]], [PB[bank]], signal=(c == 7))
                        rb = evac_rr[0] % 2
                        evac_rr[0] += 1
                        rel = FA[:, rb * 512:rb * 512 + (m1 - m0)]
                        k.op(ACT, lambda: nc.scalar.activation(out=rel, in_=PS[bank][:, 0:m1 - m0], func=AF.Relu),
                             [PB[bank]], [RELB[rb]])
                        k.op(DVE, lambda: nc.vector.tensor_tensor(
                            ARENA[:, M_AT + fc * 1088 + m0:M_AT + fc * 1088 + m1], rel, rel, ALU.mult), [RELB[rb]], [ATB])
                        if tg == 0 and fq >= 1:
                            advance(gen1)
                    if fc % 2 == 1:
                        ring_done(ups[fc // 2][0])
                k.mark(f"M{li} tg{tg} fq{fq} down")
                downs = [ring_next() for cc in range(4)]
                for t in tiles:
                    R = tile_rows(t)
                    lc = t * 128 - tok0
                    for dh in range(2):
                        bank = next_bank([4, 5, 6, 7])
                        for fc in range(8):
                            s = downs[fc // 2][1]
                            k.op(PE, lambda: nc.tensor.matmul(
                                PS[bank][0:R, :], ARENA[:, M_AT + fc * 1088 + lc:M_AT + fc * 1088 + lc + R],
                                RING[:, s, (fc % 2) * 1024 + dh * 512:(fc % 2) * 1024 + dh * 512 + 512],
                                start=(fc == 0), stop=(fc == 7)), [ATB, ringB[s]], [PB[bank]], signal=(fc == 7))
                        k.op(DVE, lambda: nc.vector.tensor_tensor(
                            X[0:R, t, dh * 512:dh * 512 + 512], PS[bank][0:R, :], X[0:R, t, dh * 512:dh * 512 + 512], ALU.add),
                            [PB[bank], XB[t]], [XB[t]])
                    if fq == 3:
                        emit_sq(t)
                for cc in range(4):
                    ring_done(downs[cc][0])
        arena_barrier([mhTB, ATB], hTB + OTB)
        arena_barrier(hnB + [junkB, mhTB2], unit_bufs)
        arena_barrier(RELB, TMPB[0] + TMPB[1] + OPTB)

    def final_norm():
        gB = Buf("gfin")
        allfa = [OAB, T1B] + OB + STGB + TMPB[0] + TMPB[1] + RELB + OPTB
        arena_barrier(allfa, [gB])
        arena_barrier(unit_bufs, [junkB])
        k.dma(SP, FA[:, 0:1024], final_norm_g.ap().partition_broadcast(128), [], [gB], k.semslot("gfin"))
        yB = [Buf("y0"), Buf("y1")]
        arena_barrier(allfa, yB)
        ysem = [k.semslot("y0"), k.semslot("y1")]
        k.store_sems += ysem
        for t in range(NT):
            if t not in pre_sq:
                emit_sq(t)
        rstd_from_ss(stats[:, 0:NT], stats[:, 17:17 + NT], D, [ssB])
        for t in range(NT):
            R = tile_rows(t)
            yb = t % 2
            ya = FA[0:R, 1024 + yb * 1024:2048 + yb * 1024]
            k.op(DVE, lambda: nc.vector.scalar_tensor_tensor(ya, X[0:R, t, :], stats[0:R, 17 + t:18 + t], FA[0:R, 0:1024],
                                                             ALU.mult, ALU.mult), [XB[t], ssB, gB], [yB[yb]])
            dst = y_prompt[t * 128:(t + 1) * 128, :] if t < 16 else y_sample
            k.dma(SP, dst, ya, [yB[yb]], [], ysem[yb])

    layer_A(0, 0)
    mlp(0)
    layer_B(1, 0)
    mlp(1)
    layer_A(2, 1)
    mlp(2)
    layer_B(3, 1)
    mlp(3)
    k.mark("final")
    final_norm()
    k.mark("end")
    for ss in k.store_sems:
        if ss[1] > 0:
            nc.sync.wait_ge(ss[0], ss[1])


_CACHE = {}


def _consts():
    import jax
    import jax.numpy as jnp
    cpu = jax.devices("cpu")[0]
    with jax.default_device(cpu):
        rel = jnp.asarray(127 - np.arange(383), dtype=jnp.int32)
        nb = 16
        n = -rel
        ret = jnp.where(n < 0, nb, 0)
        n = jnp.abs(n)
        max_exact = nb // 2
        nf = jnp.maximum(n, 1).astype(jnp.float32)
        large = max_exact + (jnp.log(nf / max_exact) / math.log(128 / max_exact) * (nb - max_exact)).astype(jnp.int32)
        large = jnp.minimum(large, nb - 1)
        bucket = np.asarray(ret + jnp.where(n < max_exact, n, large))
    oh = np.zeros((32, 383), np.float32)
    oh[bucket, np.arange(383)] = 1.0
    ident = np.eye(128, dtype=np.float32)
    return oh, ident


def kernel(**inputs):
    inp = {k_: np.asarray(v) for k_, v in inputs.items()}
    if "nc" not in _CACHE:
        _CACHE["nc"] = build_program()
        _CACHE["consts"] = _consts()
    nc = _CACHE["nc"]
    oh, ident = _CACHE["consts"]
    f = lambda a: np.ascontiguousarray(a, dtype=np.float32)
    shared = {
        "rel_table": f(inp["rel_table"]), "norm_mix_g": f(inp["norm_mix_g"]), "norm_mlp_g": f(inp["norm_mlp_g"]),
        "final_norm_g": f(inp["final_norm_g"]), "a_w_qkv": f(inp["a_w_qkv"]),
        "a_lambda": f(inp["a_lambda"]).reshape(2, 256), "a_subln_g": f(inp["a_subln_g"]), "a_w_o": f(inp["a_w_o"]),
        "b_w_qkv": f(inp["b_w_qkv"]), "b_sinks": f(inp["b_sinks"]), "b_w_o": f(inp["b_w_o"]),
        "mlp_w_up": f(inp["mlp_w_up"]), "mlp_w_down": f(inp["mlp_w_down"]), "c_ohr": oh, "c_ident": ident,
    }
    in_maps = []
    for b in range(8):
        m = dict(shared)
        m["x_prompt"] = f(inp["x_prompt"][b])
        m["x_sample"] = f(inp["x_sample"][b])
        m["cache_a_k"] = f(inp["cache_a_k"][:, b]).reshape(2, PAST, 8, 128)
        m["cache_a_v"] = f(inp["cache_a_v"][:, b])
        m["cache_b_k"] = f(inp["cache_b_k"][:, b])
        m["cache_b_v"] = f(inp["cache_b_v"][:, b])
        in_maps.append(m)
    res = run_bass_kernel_spmd(nc, in_maps, core_ids=list(range(8)))
    R = res.results
    st = lambda name, ax, shp=None: np.stack([np.asarray(R[b][name], dtype=np.float32) for b in range(8)], axis=ax)
    y_prompt = st("y_prompt", 0)
    y_sample = st("y_sample", 0)
    a_k_prompt = st("a_k_prompt", 1).reshape(2, 8, SEQ, 8, 2, 64)
    a_v_prompt = st("a_v_prompt", 1).reshape(2, 8, SEQ, 8, 128)
    b_k_prompt = st("b_k_prompt", 1)
    b_v_prompt = st("b_v_prompt", 1)
    a_k_sample = st("a_k_sample", 1).reshape(2, 8, TS, 8, 2, 64)
    a_v_sample = st("a_v_sample", 1).reshape(2, 8, TS, 8, 128)
    b_k_sample = st("b_k_sample", 1)
    b_v_sample = st("b_v_sample", 1)
    return (y_prompt, y_sample, a_k_prompt, a_v_prompt, b_k_prompt, b_v_prompt,
            a_k_sample, a_v_sample, b_k_sample, b_v_sample)
```
